# Optimizing a Trainium2 kernel written in Bass

```python
import math
import jax, jax.numpy as jnp
from jax import lax
import numpy as np

D_MODEL = 2048
BATCH = 2
SEQ = 4096
DEPTH = 2

GRID_W = 64
CTX_LEN = 256
HEAD_DIM = 128
N_HEADS_NA = 6
N_HEADS_MLA = 5
N_HEADS_DIFF = 5
NA_KH = 8
NA_KW = 16
MLA_Q_RANK = 768
MLA_KV_RANK = 512
MLA_NOPE_DIM = 128
MLA_ROPE_DIM = 64
MLA_V_DIM = 128
DIFF_QK_DIM = HEAD_DIM // 2
FFN_DIM = 5632
N_MOD = 9
ROPE_THETA = 10000.0
NORM_EPS = 1e-6
Q_BLOCK = 128
NEG_INF = -1e30
NA_W = N_HEADS_NA * HEAD_DIM
MLA_W = N_HEADS_MLA * MLA_V_DIM
DIFF_W = N_HEADS_DIFF * HEAD_DIM
MIX_W = NA_W + MLA_W + DIFF_W
IN_SPLITS = (NA_W, NA_W, NA_W, MLA_Q_RANK, MLA_KV_RANK, MLA_ROPE_DIM, DIFF_W, DIFF_W, DIFF_W, D_MODEL, D_MODEL, D_MODEL)
IN_W = sum(IN_SPLITS)
NA_SCALE = HEAD_DIM ** -0.5
MLA_SCALE = (MLA_NOPE_DIM + MLA_ROPE_DIM) ** -0.5
DIFF_SCALE = DIFF_QK_DIM ** -0.5

kernel_name = "hybrid_dit_na_mla_diffattn_macaron"


def rmsnorm(x, g):
    xf = x.astype(jnp.float32)
    y = xf * lax.rsqrt(jnp.mean(xf * xf, axis=-1, keepdims=True) + NORM_EPS)
    return (y * g.astype(jnp.float32)).astype(x.dtype)


def modulate(x, g, shift, scale):
    return rmsnorm(x, g) * (1 + scale) + shift


def swiglu(h, w_in, w_out):
    a, b = jnp.split(h @ w_in, 2, axis=-1)
    return (jax.nn.silu(a) * b) @ w_out


def split_cols(p, sizes):
    return jnp.split(p, [int(i) for i in np.cumsum(sizes)[:-1]], axis=-1)


def to_heads(t, n_heads):
    return t.reshape(*t.shape[:2], n_heads, -1)


def rope_2d(x, rows, cols):
    half = x.shape[-1] // 2
    quarter = half // 2
    freqs = ROPE_THETA ** (-jnp.arange(quarter, dtype=jnp.float32) / quarter)

    def rotate(xp, pos):
        ang = pos.astype(jnp.float32)[:, None] * freqs
        cos = jnp.cos(ang)[:, None, :].astype(xp.dtype)
        sin = jnp.sin(ang)[:, None, :].astype(xp.dtype)
        x1, x2 = xp[..., :quarter], xp[..., quarter:]
        return jnp.concatenate([x1 * cos - x2 * sin, x1 * sin + x2 * cos], axis=-1)

    return jnp.concatenate([rotate(x[..., :half], rows), rotate(x[..., half:], cols)], axis=-1)


def scores(q, k, scale):
    return jnp.einsum('bqhd,bkhd->bhqk', q, k).astype(jnp.float32) * scale


def attend(q, k, v, scale):
    p = jax.nn.softmax(scores(q, k, scale), axis=-1).astype(v.dtype)
    return jnp.einsum('bhqk,bkhd->bqhd', p, v)


def diff_attend(q1, q2, k1, k2, v, lam, subln, lambda_init):
    p1 = jax.nn.softmax(scores(q1, k1, DIFF_SCALE), axis=-1)
    p2 = jax.nn.softmax(scores(q2, k2, DIFF_SCALE), axis=-1)
    o = jnp.einsum('bhqk,bkhd->bqhd', (p1 - lam * p2).astype(v.dtype), v)
    return rmsnorm(o, subln) * (1.0 - lambda_init)


def sweep_query_blocks(fn, *qs):
    B, S = qs[0].shape[:2]
    nb = S // Q_BLOCK
    blocks = tuple(jnp.moveaxis(q.reshape(B, nb, Q_BLOCK, *q.shape[2:]), 1, 0) for q in qs)
    out = lax.map(lambda xs: fn(*xs), blocks)
    out = jnp.moveaxis(out, 0, 1)
    return out.reshape(B, S, *out.shape[3:])


def neighbourhood_attention(q, k, v, k_ctx, v_ctx, rpb):
    B, S, H, Dh = q.shape
    rows_n = S // GRID_W
    kh = min(NA_KH, rows_n)
    kw = NA_KW
    r = jnp.arange(rows_n)
    col = jnp.arange(GRID_W)
    r0 = jnp.clip(r - kh // 2, 0, rows_n - kh)
    row_idx = r0[:, None] + jnp.arange(kh)[None, :]
    c0 = jnp.clip(col - kw // 2, 0, GRID_W - kw)
    col_ok = (col[None, :] >= c0[:, None]) & (col[None, :] < c0[:, None] + kw)
    grid = lambda t: t.reshape(B, rows_n, GRID_W, H, Dh)
    qg = grid(q)
    kg = jnp.take(grid(k), row_idx.reshape(-1), axis=1).reshape(B, rows_n, kh, GRID_W, H, Dh)
    vg = jnp.take(grid(v), row_idx.reshape(-1), axis=1).reshape(B, rows_n, kh, GRID_W, H, Dh)
    s_loc = jnp.einsum('brqhd,brjkhd->bhrqjk', qg, kg).astype(jnp.float32) * NA_SCALE
    row_off = row_idx - r[:, None] + (NA_KH - 1)
    col_off = jnp.clip(col[None, :] - col[:, None], -(NA_KW - 1), NA_KW - 1) + (NA_KW - 1)
    bias = rpb[:, row_off[:, None, :, None], col_off[None, :, None, :]]
    s_loc = jnp.where(col_ok[:, None, :], s_loc + bias.astype(jnp.float32), NEG_INF)
    n_loc = kh * GRID_W
    s_loc = s_loc.reshape(B, H, rows_n, GRID_W, n_loc)
    s_ctx = jnp.einsum('brqhd,bkhd->bhrqk', qg, k_ctx).astype(jnp.float32) * NA_SCALE
    p = jax.nn.softmax(jnp.concatenate([s_loc, s_ctx], axis=-1), axis=-1).astype(v.dtype)
    p_loc = p[..., :n_loc].reshape(B, H, rows_n, GRID_W, kh, GRID_W)
    o = (jnp.einsum('bhrqjk,brjkhd->brqhd', p_loc, vg)
         + jnp.einsum('bhrqk,bkhd->brqhd', p[..., n_loc:], v_ctx))
    return o.reshape(B, S, H * Dh)


def token_mixer(h, hc, w_in, na_rpb, mla_q_norm, mla_kv_norm, mla_w_uq, mla_w_ukv,
                diff_lambda, diff_subln, w_branch, w_out, lambda_init, rows, cols, with_ctx_out):
    p = split_cols(h @ w_in, IN_SPLITS)
    pc = split_cols(hc @ w_in, IN_SPLITS)

    na_q, na_k, na_v = (to_heads(t, N_HEADS_NA) for t in p[0:3])
    na_qc, na_kc, na_vc = (to_heads(t, N_HEADS_NA) for t in pc[0:3])
    o_na = neighbourhood_attention(na_q, na_k, na_v, na_kc, na_vc, na_rpb)

    def mla_queries(cq):
        return to_heads(rmsnorm(cq, mla_q_norm) @ mla_w_uq, N_HEADS_MLA)

    def mla_keys_values(ckv, k_pe):
        kv = to_heads(rmsnorm(ckv, mla_kv_norm) @ mla_w_ukv, N_HEADS_MLA)
        k_nope, v = kv[..., :MLA_NOPE_DIM], kv[..., MLA_NOPE_DIM:]
        k_pe = jnp.broadcast_to(k_pe, k_nope.shape[:-1] + (MLA_ROPE_DIM,))
        return jnp.concatenate([k_nope, k_pe], axis=-1), v

    mq = mla_queries(p[3])
    mq = jnp.concatenate([mq[..., :MLA_NOPE_DIM], rope_2d(mq[..., MLA_NOPE_DIM:], rows, cols)], axis=-1)
    mk, mv = mla_keys_values(p[4], rope_2d(p[5][:, :, None, :], rows, cols))
    mkc, mvc = mla_keys_values(pc[4], pc[5][:, :, None, :])
    mk_all = jnp.concatenate([mkc, mk], axis=1)
    mv_all = jnp.concatenate([mvc, mv], axis=1)
    o_mla = sweep_query_blocks(lambda qb: attend(qb, mk_all, mv_all, MLA_SCALE), mq)

    lam = (jnp.exp(jnp.sum(diff_lambda[0] * diff_lambda[1]))
           - jnp.exp(jnp.sum(diff_lambda[2] * diff_lambda[3])) + lambda_init)
    dq, dk, dv = (to_heads(t, N_HEADS_DIFF) for t in p[6:9])
    dqc, dkc, dvc = (to_heads(t, N_HEADS_DIFF) for t in pc[6:9])
    halves = lambda t: (t[..., :DIFF_QK_DIM], t[..., DIFF_QK_DIM:])
    rot = lambda t: rope_2d(t, rows, cols)
    dq1, dq2 = (rot(t) for t in halves(dq))
    dk1, dk2 = (rot(t) for t in halves(dk))
    dk1c, dk2c = halves(dkc)
    dk1_all = jnp.concatenate([dk1c, dk1], axis=1)
    dk2_all = jnp.concatenate([dk2c, dk2], axis=1)
    dv_all = jnp.concatenate([dvc, dv], axis=1)
    o_diff = sweep_query_blocks(
        lambda q1b, q2b: diff_attend(q1b, q2b, dk1_all, dk2_all, dv_all, lam, diff_subln, lambda_init),
        dq1, dq2)

    def merge(a, b, d, g_a, g_b, g_d):
        flat = lambda t: t.reshape(*t.shape[:2], -1)
        y = (jax.nn.sigmoid(g_a) * (flat(a) @ w_branch[:NA_W])
             + jax.nn.sigmoid(g_b) * (flat(b) @ w_branch[NA_W:NA_W + MLA_W])
             + jax.nn.sigmoid(g_d) * (flat(d) @ w_branch[NA_W + MLA_W:]))
        return y @ w_out

    out = merge(o_na, o_mla, o_diff, p[9], p[10], p[11])
    if not with_ctx_out:
        return out, None
    dq1c, dq2c = halves(dqc)
    oc = merge(attend(na_qc, na_kc, na_vc, NA_SCALE),
               attend(mla_queries(pc[3]), mkc, mvc, MLA_SCALE),
               diff_attend(dq1c, dq2c, dk1c, dk2c, dvc, lam, diff_subln, lambda_init),
               pc[9], pc[10], pc[11])
    return out, oc


def setup_inputs(seed: int = 0) -> dict:
    key = jax.random.key(seed)
    ks = jax.random.split(key, 20)
    nrm = lambda k, shape: jax.random.normal(k, shape, jnp.float32)
    w = lambda k, shape, fan_in, gain=1.0: nrm(k, shape) * (gain * fan_in ** -0.5)
    gn = lambda k, shape: 1.0 + 0.1 * nrm(k, shape)
    return {
        "x": nrm(ks[0], (BATCH, SEQ, D_MODEL)),
        "c": nrm(ks[1], (BATCH, D_MODEL)),
        "ctx": nrm(ks[2], (BATCH, CTX_LEN, D_MODEL)),
        "c_ctx": nrm(ks[3], (D_MODEL,)),
        "w_ada": w(ks[4], (DEPTH, D_MODEL, N_MOD * D_MODEL), D_MODEL, 0.5),
        "b_ada": 0.02 * nrm(ks[5], (DEPTH, N_MOD * D_MODEL)),
        "norm_w": gn(ks[6], (DEPTH, 3, D_MODEL)),
        "ffn_w_in": w(ks[7], (DEPTH, 2, D_MODEL, 2 * FFN_DIM), D_MODEL),
        "ffn_w_out": w(ks[8], (DEPTH, 2, FFN_DIM, D_MODEL), FFN_DIM),
        "w_in": w(ks[9], (DEPTH, D_MODEL, IN_W), D_MODEL),
        "na_rpb": 0.1 * nrm(ks[10], (DEPTH, N_HEADS_NA, 2 * NA_KH - 1, 2 * NA_KW - 1)),
        "mla_q_norm": gn(ks[11], (DEPTH, MLA_Q_RANK)),
        "mla_kv_norm": gn(ks[12], (DEPTH, MLA_KV_RANK)),
        "mla_w_uq": w(ks[13], (DEPTH, MLA_Q_RANK, N_HEADS_MLA * (MLA_NOPE_DIM + MLA_ROPE_DIM)), MLA_Q_RANK),
        "mla_w_ukv": w(ks[14], (DEPTH, MLA_KV_RANK, N_HEADS_MLA * (MLA_NOPE_DIM + MLA_V_DIM)), MLA_KV_RANK),
        "diff_lambda": 0.1 * nrm(ks[15], (DEPTH, 4, DIFF_QK_DIM)),
        "diff_subln": gn(ks[16], (DEPTH, HEAD_DIM)),
        "w_branch": w(ks[17], (DEPTH, MIX_W, D_MODEL), MIX_W),
        "w_out": w(ks[18], (DEPTH, D_MODEL, D_MODEL), D_MODEL),
        "final_norm": gn(ks[19], (D_MODEL,)),
    }


def reference(x, c, ctx, c_ctx, w_ada, b_ada, norm_w, ffn_w_in, ffn_w_out, w_in, na_rpb,
              mla_q_norm, mla_kv_norm, mla_w_uq, mla_w_ukv, diff_lambda, diff_subln,
              w_branch, w_out, final_norm):
    S = x.shape[1]
    t = jnp.arange(S)
    rows, cols = t // GRID_W, t % GRID_W
    xc = ctx
    for l in range(DEPTH):
        last = l == DEPTH - 1
        lambda_init = 0.8 - 0.6 * math.exp(-0.3 * l)
        m = [mm[:, None, :] for mm in jnp.split(jax.nn.silu(c) @ w_ada[l] + b_ada[l], N_MOD, axis=-1)]
        mc = jnp.split(jax.nn.silu(c_ctx) @ w_ada[l] + b_ada[l], N_MOD, axis=-1)
        x = x + 0.5 * m[2] * swiglu(modulate(x, norm_w[l, 0], m[0], m[1]), ffn_w_in[l, 0], ffn_w_out[l, 0])
        xc = xc + 0.5 * mc[2] * swiglu(modulate(xc, norm_w[l, 0], mc[0], mc[1]), ffn_w_in[l, 0], ffn_w_out[l, 0])
        o, oc = token_mixer(modulate(x, norm_w[l, 1], m[3], m[4]), modulate(xc, norm_w[l, 1], mc[3], mc[4]),
                            w_in[l], na_rpb[l], mla_q_norm[l], mla_kv_norm[l], mla_w_uq[l], mla_w_ukv[l],
                            diff_lambda[l], diff_subln[l], w_branch[l], w_out[l], lambda_init, rows, cols,
                            not last)
        x = x + m[5] * o
        x = x + 0.5 * m[8] * swiglu(modulate(x, norm_w[l, 2], m[6], m[7]), ffn_w_in[l, 1], ffn_w_out[l, 1])
        if not last:
            xc = xc + mc[5] * oc
            xc = xc + 0.5 * mc[8] * swiglu(modulate(xc, norm_w[l, 2], mc[6], mc[7]), ffn_w_in[l, 1], ffn_w_out[l, 1])
    return rmsnorm(x, final_norm)
```

```python
import math
import os
from contextlib import ExitStack

import numpy as np
import concourse.bass as bass
import concourse.mybir as mybir
from concourse.bass_utils import run_bass_kernel_spmd

F32 = mybir.dt.float32
BF16 = mybir.dt.bfloat16
AF = mybir.ActivationFunctionType
ALU = mybir.AluOpType

D = 2048
KC = 16
TL = 1024
TCX = 64
T = TL + TCX
FFN = 5632
NEG = -30000.0
EPS = 1e-6
NA_SCALE = 128 ** -0.5
MLA_SCALE = 192 ** -0.5
DIFF_SCALE = 64 ** -0.5
TBS = [(0, 384), (384, 768), (768, 1088)]
QBS = [(0, 512), (512, 1024), (1024, 1088)]
O_NAQ, O_NAK, O_NAV, O_CQ, O_CKV, O_KR, O_DQ, O_DK, O_DV, O_GA, O_GB, O_GD = (
    0, 768, 1536, 2304, 3072, 3584, 3648, 4288, 4928, 5568, 7616, 9664)
KROWS = 16 * 128 + 64
DEBUG_STOP = os.environ.get("MK_STOP", "")
FAKE = bool(os.environ.get("MK_FAKE"))


VT_NA = [(0, 256), (256, 128), (384, 256), (640, 128)]
VT_D = [(0, 256), (256, 128), (384, 256)]
VT_M = [(0, 384), (384, 256)]
KCH_ROWS = [384, 384, 384, 320, 384, 256]
VCH_COLS = [384, 384, 384, 256, 384, 256]


def kvloc(br, h):
    if br == "na":
        return h // 3, (h % 3) * 128
    if br == "mla":
        return 2 + h // 3, (h % 3) * 128
    if br == "kr":
        return 3, 256
    return 4 + h // 3, (h % 3) * 128


def na_chunks(r):
    lo = min(r - 4, 8)
    hi = max(r + 4, max(r - 4, 0) + 8, min(r - 4, 8) + 8)
    return (lo + 4) // 2, (hi - 1 + 4) // 2


NA_CH = [na_chunks(r) for r in range(16)]
NA_OFF = np.cumsum([0] + [b - a + 1 for a, b in NA_CH]).tolist()


def partner64(d):
    return d + 16 if (d % 32) < 16 else d - 16


def wtile(W, cols):
    K = W.shape[0]
    kc = K // 128
    a = W[:, cols].reshape(kc, 128, len(cols)).transpose(1, 0, 2).reshape(128, kc * len(cols))
    return a, kc, len(cols)


def build_wbig(inp):
    tiles = []
    dirn = {}
    off = [0]

    def put(key, W, cols):
        a, kc, n = wtile(W, np.asarray(cols))
        dirn[key] = (off[0], kc, n)
        off[0] += a.shape[1]
        tiles.append(a)

    ar = np.arange
    for l in range(2):
        w_in = inp["w_in"][l]
        uq = inp["mla_w_uq"][l]
        ukv = inp["mla_w_ukv"][l]
        p64 = np.array([partner64(d) for d in range(64)])
        p128 = np.concatenate([p64, 64 + p64])
        for s in range(2):
            fwi = inp["ffn_w_in"][l, s]
            fwo = inp["ffn_w_out"][l, s]
            for hf in range(2):
                for jj in range(22):
                    j = hf * 22 + jj
                    put(("fi", l, s, j), fwi, np.concatenate([ar(j * 128, j * 128 + 128), FFN + ar(j * 128, j * 128 + 128)]))
                for dj in range(16):
                    put(("fo", l, s, hf, dj), fwo[hf * 2816:(hf + 1) * 2816], ar(dj * 128, dj * 128 + 128))
        for h in range(6):
            put(("nak", l, h), w_in, O_NAK + h * 128 + ar(128))
        for t in range(2):
            put(("ckv", l, t), w_in, O_CKV + t * 256 + ar(256))
        for h in range(5):
            put(("ukvk", l, h), ukv, h * 256 + ar(128))
        put(("kr", l), w_in, np.concatenate([O_KR + ar(64), O_KR + p64]))
        for h in range(5):
            put(("dk", l, h, 0), w_in, O_DK + h * 128 + ar(128))
            put(("dk", l, h, 1), w_in, O_DK + h * 128 + p128)
        for t, (a, n) in enumerate(VT_NA):
            put(("nav", l, t), w_in, O_NAV + a + ar(n))
        for t, (a, n) in enumerate(VT_D):
            put(("dv", l, t), w_in, O_DV + a + ar(n))
        vcols = np.concatenate([h * 256 + 128 + ar(128) for h in range(5)])
        for t, (a, n) in enumerate(VT_M):
            put(("mv", l, t), ukv, vcols[a:a + n])
        for h in range(6):
            put(("naq", l, h), w_in, O_NAQ + h * 128 + ar(128))
        for t in range(3):
            put(("cq", l, t), w_in, O_CQ + t * 256 + ar(256))
        for h in range(5):
            put(("uqn", l, h), uq, h * 192 + ar(128))
            put(("uqr", l, h), uq, np.concatenate([h * 192 + 128 + ar(64), h * 192 + 128 + p64]))
        for h in range(5):
            put(("dq", l, h, 0), w_in, O_DQ + h * 128 + ar(128))
            put(("dq", l, h, 1), w_in, O_DQ + h * 128 + p128)
        wb = inp["w_branch"][l]
        wo = inp["w_out"][l]
        for dj in range(16):
            for br, o in enumerate((O_GA, O_GB, O_GD)):
                put(("g", l, dj, br), w_in, o + dj * 128 + ar(128))
            put(("wb", l, dj), wb, dj * 128 + ar(128))
        for half in range(2):
            for t in range(8):
                put(("wo", l, half, t), wo[half * 1024:(half + 1) * 1024], t * 256 + ar(256))
    return np.ascontiguousarray(np.concatenate(tiles, axis=1)), dirn


def fm(v):
    return np.ascontiguousarray(v.reshape(-1, 128).T)


def build_vecs(inp):
    cols = []
    for l in range(2):
        for n in range(3):
            cols.append(fm(inp["norm_w"][l, n]))
    cols.append(fm(inp["final_norm"]))
    for l in range(2):
        cols.append(fm(inp["mla_q_norm"][l]))
    for l in range(2):
        cols.append(fm(inp["mla_kv_norm"][l]))
    for l in range(2):
        cols.append(fm(inp["diff_subln"][l]))
    return np.ascontiguousarray(np.concatenate(cols, axis=1).astype(np.float32))


V_NW, V_FN, V_QN, V_KVN, V_SUB = 0, 96, 112, 124, 132
NV = 134


def rope_table(rq):
    theta = 10000.0
    quarter = 16
    freqs = (np.float32(theta) ** (-np.arange(quarter, dtype=np.float32) / np.float32(quarter))).astype(np.float32)
    g = rq * TL + np.arange(TL)
    rows = (g // 64).astype(np.float32)
    colsp = (g % 64).astype(np.float32)
    tab = np.zeros((64, 2, T), np.float32)
    tab[:, 0, TL:] = 1.0
    for d in range(64):
        pos = rows if d < 32 else colsp
        ang = (pos * freqs[d % 16]).astype(np.float32)
        tab[d, 0, :TL] = np.cos(ang)
        sn = np.sin(ang)
        tab[d, 1, :TL] = -sn if (d % 32) < 16 else sn
    return np.ascontiguousarray(np.concatenate([tab, tab], axis=0))


def na_bias(rpb, rq):
    out = np.full((2, 6, 128, NA_OFF[-1], 64), NEG, np.float32)
    qc = np.arange(64)
    kc = np.arange(64)
    c0 = np.clip(qc - 8, 0, 48)
    col_ok = (kc[:, None] >= c0[None, :]) & (kc[:, None] < c0[None, :] + 16)
    col_off = np.clip(kc[:, None] - qc[None, :], -15, 15) + 15
    for r in range(16):
        gr = 16 * rq + r
        r0 = min(max(gr - 4, 0), 56)
        clo, chi = NA_CH[r]
        for ci, c in enumerate(range(clo, chi + 1)):
            for kk in range(2):
                kr = 16 * rq + (-4 + 2 * c + kk)
                if kr < 0 or kr >= 64 or kr < r0 or kr >= r0 + 8:
                    continue
                ro = kr - gr + 7
                vals = rpb[:, :, ro, :][:, :, col_off]
                blk = out[:, :, kk * 64:(kk + 1) * 64, NA_OFF[r] + ci, :]
                blk[...] = np.where(col_ok[None, None], vals, NEG)
    return out


class Op:
    __slots__ = ("eng", "fn", "deps", "pos", "awaited", "kind", "idx", "waits", "sem", "semval", "ccg")

    def __init__(self, eng, fn, kind):
        self.eng = eng
        self.fn = fn
        self.kind = kind
        self.deps = []
        self.awaited = False
        self.waits = []
        self.sem = None
        self.semval = 0


ENGS = ("pe", "act", "dve", "pool", "sp")
NDSEM = 16
SEM_CH = 4000


class Prog:
    def __init__(self, nc):
        self.nc = nc
        self.ops = []
        self.eng_ops = {e: [] for e in ENGS}
        self.recs = {}
        self.ndma = 0
        self.dma_ops = []
        self.skip = set()

    @staticmethod
    def _dsz(dt):
        return mybir.dt.size(dt)

    def region(self, ap):
        t = ap.tensor
        name = t.name
        if name in self.skip:
            return None
        pat = ap.ap
        rng = t.manual_sbuf_range
        if rng is not None:
            ps = 1
            for s in t.shape[1:]:
                ps *= int(s)
            lo = int(ap.offset) % ps
            ext = 1
            for (st, cnt) in pat[1:]:
                ext += (int(cnt) - 1) * abs(int(st))
            esz = self._dsz(t.dtype)
            return ("sb", rng[0] + lo * esz, rng[0] + (lo + ext) * esz)
        tn = type(t).__name__
        if tn.startswith("PSum") or tn.startswith("SB"):
            ps = 1
            for s in t.shape[1:]:
                ps *= int(s)
            lo = int(ap.offset) % ps
            ext = 1
            for (st, cnt) in pat[1:]:
                ext += (int(cnt) - 1) * abs(int(st))
            return (name, lo, lo + ext)
        lo = int(ap.offset)
        ext = 1
        for (st, cnt) in pat:
            ext += (int(cnt) - 1) * abs(int(st))
        return (name, lo, lo + ext)

    def _pages(self, sp, lo, hi):
        pg = 2048 if sp == "sb" else 65536
        return range(lo // pg, (hi - 1) // pg + 1)

    def add(self, eng, fn, reads=(), writes=(), kind="c", ccg=0):
        op = Op(eng, fn, kind)
        op.ccg = ccg
        op.idx = len(self.ops)
        deps = {}
        rkey = (eng + kind) if kind == "c" else ("d", op.idx)
        for ap in reads:
            rg = self.region(ap)
            if rg is None:
                continue
            sp, lo, hi = rg
            pages = self.recs.setdefault(sp, {})
            found = None
            for pgi in self._pages(sp, lo, hi):
                for rec in pages.get(pgi, ()):
                    if rec[0] < hi and lo < rec[1]:
                        w = rec[2]
                        if w is not None:
                            deps[w.idx] = (w, "raw")
                        if rec[0] == lo and rec[1] == hi:
                            found = rec
            if found is None:
                found = [lo, hi, None, {}]
                for pgi in self._pages(sp, lo, hi):
                    pages.setdefault(pgi, []).append(found)
            found[3][rkey] = op
        for ap in writes:
            rg = self.region(ap)
            if rg is None:
                continue
            sp, lo, hi = rg
            pages = self.recs.setdefault(sp, {})
            newrec = [lo, hi, op, {}]
            for pgi in self._pages(sp, lo, hi):
                lst = pages.get(pgi)
                if lst is None:
                    pages[pgi] = [newrec]
                    continue
                keep = []
                for rec in lst:
                    if rec[0] < hi and lo < rec[1]:
                        w = rec[2]
                        if w is not None and w.idx not in deps:
                            deps[w.idx] = (w, "waw")
                        for rd in rec[3].values():
                            if rd is not op and rd.idx not in deps:
                                deps[rd.idx] = (rd, "war")
                        if lo <= rec[0] and rec[1] <= hi:
                            continue
                    keep.append(rec)
                keep.append(newrec)
                pages[pgi] = keep
        if kind != "c":
            if kind == "d":
                op.sem = ("d", self.ndma % NDSEM)
                op.semval = 16 * (self.ndma // NDSEM + 1)
                if self.ndma >= NDSEM:
                    prev = self.dma_ops[self.ndma - NDSEM]
                    deps.setdefault(prev.idx, (prev, "raw"))
                self.dma_ops.append(op)
                self.ndma += 1
        for (d, typ) in deps.values():
            if d.kind == "c" and d.eng == eng:
                if eng == "pe":
                    continue
                if typ != "raw" and kind == "c":
                    continue
            op.deps.append(d)
        op.pos = len(self.eng_ops[eng])
        self.eng_ops[eng].append(op)
        self.ops.append(op)
        return op

    def finalize(self, stack):
        nc = self.nc
        known = {e: {f: -1 for f in ENGS} for e in ENGS}
        knownd = {e: set() for e in ENGS}
        for op in self.ops:
            e = op.eng
            for d in op.deps:
                if d.kind == "c":
                    if d.pos <= known[e][d.eng]:
                        continue
                    known[e][d.eng] = d.pos
                    d.awaited = True
                    op.waits.append(d)
                else:
                    if d.idx in knownd[e]:
                        continue
                    knownd[e].add(d.idx)
                    op.waits.append(d)
        esems = {}
        for e in ENGS:
            cnt = 0
            for op in self.eng_ops[e]:
                if op.kind == "c" and op.awaited:
                    op.sem = (e, cnt // SEM_CH)
                    op.semval = cnt % SEM_CH + 1
                    cnt += 1
            esems[e] = [stack.enter_context(nc.semaphore(f"s_{e}_{i}")) for i in range(cnt // SEM_CH + 1)]
        dsems = [stack.enter_context(nc.semaphore(f"s_dma_{i}")) for i in range(NDSEM)]
        ccsems = [stack.enter_context(nc.semaphore(f"s_cc_{i}")) for i in range(2)]
        ccops = [op for op in self.ops if op.kind == "cc"]
        for op in ccops:
            if op.ccg == 0:
                op.sem = ("cc", 0)
                op.semval = sum(1 for o in ccops if o.ccg == 0)
            else:
                op.sem = ("cc", 1)
                op.semval = sum(1 for o in ccops if 0 < o.ccg <= op.ccg)

        def semof(op):
            kind, i = op.sem
            if kind == "d":
                return dsems[i]
            if kind == "cc":
                return ccsems[i]
            return esems[kind][i]

        def emit(ename, eng):
            for op in self.eng_ops[ename]:
                for d in op.waits:
                    eng.wait_ge(semof(d), d.semval)
                if op.fn is None:
                    continue
                inst = op.fn(eng)
                if op.kind == "d":
                    inst.then_inc(semof(op), 16)
                elif op.kind == "cc":
                    inst.then_inc(semof(op))
                elif op.awaited:
                    inst.then_inc(semof(op), 1)

        block = stack.enter_context(nc.Block())

        @block.tensor
        def _(eng):
            emit("pe", eng)

        @block.scalar
        def _(eng):
            emit("act", eng)

        @block.vector
        def _(eng):
            emit("dve", eng)

        @block.gpsimd
        def _(eng):
            emit("pool", eng)

        @block.sync
        def _(eng):
            emit("sp", eng)


class StopBuild(Exception):
    pass


class Builder:
    def __init__(self, dirn, wtot):
        self.dirn = dirn
        self.wtot = wtot
        self.nc = nc = bass.Bass("TRN2", target_bir_lowering=False)
        self.P = Prog(nc)
        P = self.P
        dt = nc.dram_tensor
        self.xT = dt("xT", [D, T], F32, kind="ExternalInput").ap()
        self.cT = dt("cT", [128, KC * 2], F32, kind="ExternalInput").ap()
        if FAKE:
            wtot = self.wtot = 8192
        self.adaw = dt("adaw", [1 if FAKE else 72, 128, 2048], F32, kind="ExternalInput").ap()
        self.adab = dt("adab", [1, 72 * 128], F32, kind="ExternalInput").ap()
        self.wbig = dt("wbig", [128, wtot], F32, kind="ExternalInput").ap()
        self.vecs_d = dt("vecs", [128, NV], F32, kind="ExternalInput").ap()
        self.dlam_d = dt("dlam", [1, 512], F32, kind="ExternalInput").ap()
        self.rope_d = dt("rope", [128, 2 * T], F32, kind="ExternalInput").ap()
        self.nab_d = dt("nab", [1 if FAKE else 12, 128, NA_OFF[-1] * 64], F32, kind="ExternalInput").ap()
        self.sel_d = dt("sel", [128, 16], F32, kind="ExternalInput").ap()
        self.ident_d = dt("ident", [128, 128], F32, kind="ExternalInput").ap()
        for n in ("xT", "cT", "adaw", "adab", "wbig", "vecs", "dlam", "rope", "nab", "sel", "ident"):
            P.skip.add(n)
        self.out = dt("out", [TL, D], F32, kind="ExternalOutput").ap()
        self.dbg = None
        if DEBUG_STOP:
            self.dbg = dt("dbg", [128, KC * T], F32, kind="ExternalOutput").ap()
        self.cinK = [dt(f"cinK{i}", [n, T], BF16) for i, n in enumerate(KCH_ROWS)]
        self.coutK = [dt(f"coutK{i}", [4 * n, T], BF16) for i, n in enumerate(KCH_ROWS)]
        self.cinV = [dt(f"cinV{i}", [T, n], BF16) for i, n in enumerate(VCH_COLS)]
        self.coutV = [dt(f"coutV{i}", [4 * T, n], BF16) for i, n in enumerate(VCH_COLS)]
        self.cinA = dt("cinA", [128, 144], F32)
        self.coutA = dt("coutA", [4 * 128, 144], F32)
        arena = nc.alloc_sbuf_tensor("arena", [128, 212736], mybir.dt.uint8)
        self.abase = int(nc.lookup_mloc(arena).addr)
        self.ncnt = 0
        self.psum = [nc.alloc_psum_tensor(f"ps{i}", [128, 512], F32) for i in range(8)]
        self.OX = 0
        self.OB = 69632
        self.OC = self.OB + 34816
        self.OD = self.OC + 47872
        self.X = self.sb("X", [128, KC, T], F32, self.OX)
        self.H = self.sb("H", [128, KC, T], BF16, self.OB)
        self.OT = self.sb("OT", [128, KC, T], BF16, self.OB)
        self.U = self.sb("U", [128, 22, T], BF16, self.OC)
        self.H2 = self.sb("H2", [128, KC, T], BF16, self.OC)
        self.QNA = self.sb("QNA", [128, 6, T], BF16, self.OC)
        self.QMN = self.sb("QMN", [128, 5, T], BF16, self.OC + 13056)
        self.QMR = self.sb("QMR", [128, 5, T], BF16, self.OC + 13056 + 10880)
        self.QD = self.sb("QD", [128, 5, T], BF16, self.OC + 13056 + 21760)
        self.CQN = self.sb("CQN", [128, 6, T], BF16, self.OC + 13056 + 21760)
        o = self.OD
        self.MALL = self.sb("MALL", [128, 4, 144], F32, o); o += 3456
        self.MODS = self.sb("MODS", [128, 2, 144], F32, o); o += 1152
        self.VECS = self.sb("VECS", [128, NV], F32, o); o += 544
        self.AV = self.sb("AV", [128, 2, 16], F32, o); o += 128
        self.BV = self.sb("BV", [128, 2, 16], F32, o); o += 128
        self.GV = self.sb("GV", [128, 2, 16], F32, o); o += 128
        self.SEL = self.sb("SEL", [128, 16], F32, o); o += 64
        self.NEGLAM = self.sb("NEGLAM", [128, 4], F32, o); o += 32
        self.SUBW = self.sb("SUBW", [128, 2], F32, o); o += 32
        self.EPS_AP = self.sb("EPSC", [128, 1], F32, o); o += 32
        self.ONESB = self.sb("ONESB", [128, 128], BF16, o); o += 256
        self.ONESF = self.sb("ONESF", [128, 128], F32, o); o += 512
        self.o_scr = o
        self.RS = self.sb("RS", [128, T], F32, o); o += 4352
        self.TMPF = self.sb("TMPF", [128, T], F32, o); o += 4352
        self.SQ = self.sb("SQ", [128, 2, 512], BF16, o); o += 2048
        self.SA = self.sb("SA", [128, 2, 512], F32, o); o += 4096
        o_x = o
        o += 2048 + 4096
        self.ROPE = self.sb("ROPE", [128, 2, T], BF16, self.o_scr + 4352)
        self.KST = [self.sb(f"KST{i}", [128, T], BF16, self.o_scr + 10752 + i * 2176) for i in range(2)]
        self.VST = [self.sb(f"VST{i}", [128, 384], BF16, self.o_scr + 10752 + 4352 + i * 768) for i in range(2)]
        self.RT = [self.sb(f"RT{i}", [128, 512], F32, o_x + 2048 + i * 2048) for i in range(2)]
        self.CKVN = self.sb("CKVN", [128, 4, T], BF16, self.OC + 13056 + 21760)
        a = self.o_scr
        self.PT = [self.sb(f"PT{i}", [128, 512], BF16, a + i * 1024) for i in range(4)]; a += 4096
        self.RC = [self.sb(f"RC{i}", [128, 512], F32, a + i * 2048) for i in range(2)]; a += 4096
        self.AO = [self.sb(f"AO{i}", [128, 512], F32, a + i * 2048) for i in range(2)]; a += 4096
        self.TBI = self.sb("TBI", [128, 384], F32, a); a += 1536
        self.KCX = [self.sb(f"KCX{i}", [128, 2, 4, 64], BF16, a + i * 1024) for i in range(2)]; a += 2048
        self.VCX = [self.sb(f"VCX{i}", [64, 4, 128], BF16, a + i * 1024) for i in range(2)]; a += 2048
        assert a <= o, (a, o)
        a = self.o_scr + 4352
        self.SG = [self.sb(f"SG{i}", [128, 512], F32, a + i * 2048) for i in range(3)]; a += 6144
        self.MT = [self.sb(f"MT{i}", [128, 512], F32, a + i * 2048) for i in range(2)]; a += 4096
        assert a <= o
        self.YQ = self.sb("YQ", [128, 4, T], BF16, self.OC + 34816)
        self.o_ring = o
        self.NSLOT = 4
        self.ring = [self.sb(f"ring{i}", [128, 4096], BF16, o + i * 8192) for i in range(self.NSLOT)]
        self.ringf = [self.sb(f"ringf{i}", [128, 2048], F32, o + i * 8192) for i in range(self.NSLOT)]
        o += self.NSLOT * 8192
        self.o_end = o
        assert o <= 212736, o
        self.wi = 0
        self.psi = 0
        self.pending = []

    def sb(self, name, shape, dt, off):
        self.ncnt += 1
        return self.nc.alloc_sbuf_tensor_at(f"{name}_{self.ncnt}", shape, dt, offset=self.abase + off)

    def mm(self, out, lhsT, rhs, start=True, stop=True):
        self.P.add("pe", lambda e: e.matmul(out, lhsT, rhs, start=start, stop=stop), [lhsT, rhs], [out])

    def act(self, out, in_, func, bias=None, scale=None):
        kw = {}
        rd = [in_]
        if bias is not None:
            kw["bias"] = bias
            if not isinstance(bias, (int, float)):
                rd.append(bias)
        if scale is not None:
            kw["scale"] = scale
            if not isinstance(scale, (int, float)):
                rd.append(scale)
        self.P.add("act", lambda e: e.activation(out, in_, func, **kw), rd, [out])

    def tt(self, out, in0, in1, op):
        self.P.add("dve", lambda e: e.tensor_tensor(out, in0, in1, op), [in0, in1], [out])

    def ts(self, out, in0, s1, s2, op0, op1=None):
        rd = [in0] + [s for s in (s1, s2) if s is not None and not isinstance(s, (int, float))]
        if op1 is None:
            self.P.add("dve", lambda e: e.tensor_scalar(out, in0, s1, None, op0), rd, [out])
        else:
            self.P.add("dve", lambda e: e.tensor_scalar(out, in0, s1, s2, op0, op1), rd, [out])

    def stt(self, out, in0, scalar, in1, op0, op1):
        rd = [in0, in1] + ([] if isinstance(scalar, (int, float)) else [scalar])
        self.P.add("dve", lambda e: e.scalar_tensor_tensor(out, in0, scalar, in1, op0, op1), rd, [out])

    def recip(self, out, in_):
        self.P.add("dve", lambda e: e.reciprocal(out, in_), [in_], [out])

    def vcopy(self, out, in_):
        self.P.add("dve", lambda e: e.tensor_copy(out, in_), [in_], [out])

    def memset(self, ap, v):
        self.P.add("dve", lambda e: e.memset(ap, v), [], [ap])

    def dma(self, out, in_, eng="sp"):
        return self.P.add(eng, lambda e: e.dma_start(out=out, in_=in_), [in_], [out], kind="d")

    def bank(self):
        b = self.psum[self.psi % 4]
        self.psi += 1
        return b

    def wload(self, key):
        off, kc, n = self.dirn[key]
        if FAKE:
            off = 0
        slot = self.ring[self.wi % self.NSLOT]
        self.wi += 1
        v = slot[:, 0:kc * n]
        self.dma(v, self.wbig[:, off:off + kc * n], eng="pool")
        self.tick()
        return v.rearrange("p (k c) -> p k c", k=kc)

    def tick(self, flush=False):
        keep = []
        for it in self.pending:
            it[0] -= 1
            if it[0] <= 0 or flush:
                it[1]()
            else:
                keep.append(it)
        self.pending = keep

    def setup(self):
        self.memset(self.ONESB[:, :], 1.0)
        self.memset(self.ONESF[:, :], 1.0)
        self.dma(self.X[:, :, :], self.xT.rearrange("(j p) t -> p j t", p=128))
        self.dma(self.VECS[:, :], self.vecs_d[:, :])
        self.dma(self.SEL[:, :], self.sel_d[:, :])
        CS = self.sb("CS", [128, KC * 2], F32, self.o_scr)
        BAD = self.sb("BAD", [1, 72 * 128], F32, self.OC)
        ADAL = self.sb("ADAL", [128, 144], F32, self.o_scr + 192)
        self.dma(CS[:, :], self.cT[:, :])
        self.dma(BAD[:, :], self.adab[:, :])
        self.act(CS[:, :], CS[:, :], AF.Silu)
        CS3 = CS[:, :].rearrange("p (k v) -> p k v", v=2)
        for t in range(72):
            wt = self.ringf[self.wi % self.NSLOT]
            self.wi += 1
            self.dma(wt[:, :], self.adaw[0 if FAKE else t])
            w3 = wt[:, :].rearrange("p (k c) -> p k c", k=KC)
            ps = self.bank()
            for k in range(KC):
                self.mm(ps[:, 0:2], w3[:, k, :], CS3[:, k, :], start=(k == 0), stop=False)
            self.mm(ps[:, 0:2], BAD[0:1, t * 128:(t + 1) * 128], self.ONESF[0:1, 0:2], start=False, stop=True)
            self.vcopy(ADAL[:, t * 2:(t + 1) * 2], ps[:, 0:2])
        self.dma(self.cinA.ap(), ADAL[:, :])
        self.P.add("pool", lambda e: e.collective_compute(
            "AllGather", ALU.bypass, replica_groups=[[0, 1, 2, 3], [4, 5, 6, 7]],
            ins=[self.cinA.ap().opt()], outs=[self.coutA.ap().opt()]),
            [self.cinA.ap()], [self.coutA.ap()], kind="cc")
        self.dma(self.MALL[:, :, :], self.coutA.ap().rearrange("(i p) c -> p i c", p=128))

    def load_mods(self, l):
        for cls in range(2):
            src = self.MALL[:, :, l * 72 + cls:l * 72 + 72:2]
            dst = self.MODS[:, cls, :].rearrange("p (i j) -> p i j", i=4)
            self.vcopy(dst, src)

    def mod(self, cls, n, j=None):
        if j is None:
            return self.MODS[:, cls, n * 16:(n + 1) * 16]
        return self.MODS[:, cls, n * 16 + j:n * 16 + j + 1]

    def rms_stats(self, nfeat_scale=1.0 / D):
        for (t0, t1) in TBS:
            n = t1 - t0
            ps = self.psum[7]
            for j in range(KC):
                sq = self.SQ[:, j % 2, 0:n]
                self.act(sq, self.X[:, j, t0:t1], AF.Square)
                self.mm(ps[:, 0:n], self.ONESB[:, :], sq, start=(j == 0), stop=(j == KC - 1))
            self.act(self.RS[:, t0:t1], ps[:, 0:n], AF.Sqrt, bias=self.EPS_AP[:, 0:1], scale=nfeat_scale)
            self.recip(self.RS[:, t0:t1], self.RS[:, t0:t1])

    def modulate(self, l, nidx, n_shift, n_scale, dst):
        g = self.VECS[:, V_NW + (l * 3 + nidx) * 16:V_NW + (l * 3 + nidx + 1) * 16]
        for cls in range(2):
            self.ts(self.AV[:, cls, :], self.mod(cls, n_scale), 1.0, None, ALU.add)
            self.tt(self.AV[:, cls, :], self.AV[:, cls, :], g, ALU.mult)
        for j in range(KC):
            for cls, (t0, t1) in ((0, (0, TL)), (1, (TL, T))):
                self.tt(self.TMPF[:, t0:t1], self.X[:, j, t0:t1], self.RS[:, t0:t1], ALU.mult)
                self.act(dst[:, j, t0:t1], self.TMPF[:, t0:t1], AF.Identity,
                         bias=self.mod(cls, n_shift, j), scale=self.AV[:, cls, j:j + 1])

    def gate_vec(self, n_gate, factor):
        for cls in range(2):
            self.ts(self.GV[:, cls, :], self.mod(cls, n_gate), float(factor), None, ALU.mult)

    def resid_add(self, dj, t0, t1, ps):
        for (a, b, cls) in ((t0, min(t1, TL), 0), (max(t0, TL), t1, 1)):
            if b <= a:
                continue
            self.stt(self.X[:, dj, a:b], ps[:, a - t0:b - t0], self.GV[:, cls, dj:dj + 1], self.X[:, dj, a:b],
                     ALU.mult, ALU.add)

    def ffn(self, l, s, nidx, n0):
        self.rms_stats()
        self.modulate(l, nidx, n0, n0 + 1, self.H)
        self.gate_vec(n0 + 2, 0.5)
        for hf in range(2):
            for jj in range(22):
                wt = self.wload(("fi", l, s, hf * 22 + jj))
                for (t0, t1) in TBS:
                    n = t1 - t0
                    pa = self.bank()
                    pb = self.bank()
                    for k in range(KC):
                        self.mm(pa[:, 0:n], wt[:, k, 0:128], self.H[:, k, t0:t1], start=(k == 0), stop=(k == KC - 1))
                    for k in range(KC):
                        self.mm(pb[:, 0:n], wt[:, k, 128:256], self.H[:, k, t0:t1], start=(k == 0), stop=(k == KC - 1))
                    sa = self.SA[:, (self.psi // 2) % 2, 0:n]
                    self.act(sa, pa[:, 0:n], AF.Silu)
                    self.tt(self.U[:, jj, t0:t1], sa, pb[:, 0:n], ALU.mult)
            for dj in range(KC):
                wt = self.wload(("fo", l, s, hf, dj))
                for (t0, t1) in TBS:
                    n = t1 - t0
                    ps = self.bank()
                    for k in range(22):
                        self.mm(ps[:, 0:n], wt[:, k, :], self.U[:, k, t0:t1], start=(k == 0), stop=(k == 21))
                    self.resid_add(dj, t0, t1, ps)


    def proj_fm(self, wt, c0, ncol, kc, src, ps, t0, t1):
        n = t1 - t0
        for k in range(kc):
            self.mm(ps[0:ncol, 0:n], wt[:, k, c0:c0 + ncol], src[:, k, t0:t1], start=(k == 0), stop=(k == kc - 1))

    def rope_evac(self, dst, ps1, ps2, np_, t0, t1):
        n = t1 - t0
        self.tt(self.RT[0][0:np_, 0:n], ps1[0:np_, 0:n], self.ROPE[0:np_, 0, t0:t1], ALU.mult)
        self.tt(self.RT[1][0:np_, 0:n], ps2[0:np_, 0:n], self.ROPE[0:np_, 1, t0:t1], ALU.mult)
        self.tt(dst, self.RT[0][0:np_, 0:n], self.RT[1][0:np_, 0:n], ALU.add)

    def mla_norm(self, l, tiles, nch, wcol0, dst):
        for (t0, t1) in TBS:
            n = t1 - t0
            pss = []
            for c in range(nch):
                ps = self.psum[c]
                wt = tiles[c // 2]
                self.proj_fm(wt, (c % 2) * 128, 128, KC, self.H, ps, t0, t1)
                pss.append(ps)
            pn = self.psum[7]
            for c in range(nch):
                sq = self.SQ[:, c % 2, 0:n]
                self.act(sq, pss[c][:, 0:n], AF.Square)
                self.mm(pn[:, 0:n], self.ONESB[:, :], sq, start=(c == 0), stop=(c == nch - 1))
            self.act(self.RS[:, t0:t1], pn[:, 0:n], AF.Sqrt, bias=self.EPS_AP[:, 0:1], scale=1.0 / (nch * 128))
            self.recip(self.RS[:, t0:t1], self.RS[:, t0:t1])
            for c in range(nch):
                self.stt(dst[:, c, t0:t1], pss[c][:, 0:n], self.VECS[:, wcol0 + c:wcol0 + c + 1],
                         self.RS[:, t0:t1], ALU.mult, ALU.mult)

    def k_phase(self, l):
        def kdst(br, h, nrows=128):
            ci, r0 = kvloc(br, h)
            return self.cinK[ci].ap()[r0:r0 + nrows, :]

        self.dma(self.ROPE[:, :, :].rearrange("p a t -> p (a t)"), self.rope_d[:, :], eng="pool")
        ki = [0]
        vi = [0]
        grp = [[0, 1, 2, 3], [4, 5, 6, 7]]
        nocc = DEBUG_STOP.endswith("_nocc")

        def fire(kind, i):
            ci, co = (self.cinK[i], self.coutK[i]) if kind == "k" else (self.cinV[i], self.coutV[i])
            self.P.add("pool", lambda e, ci=ci, co=co: e.collective_compute(
                "AllGather", ALU.bypass, replica_groups=grp,
                ins=[ci.ap().opt()], outs=[co.ap().opt()]),
                [ci.ap()], [co.ap()], kind="cc", ccg=l + 1)

        def trigger(kind, i):
            if nocc:
                return
            self.pending.append([3, lambda: fire(kind, i)])

        def kst():
            ki[0] += 1
            return self.KST[ki[0] % 2]

        def vproj(key, src, kc, chunk, col0):
            wt = self.wload(key)
            ncol = self.dirn[key][2]
            dst = self.cinV[chunk].ap()
            for tc in range(9):
                t0 = tc * 128
                nt = min(128, T - t0)
                ps = self.bank()
                for k in range(kc):
                    self.mm(ps[0:nt, 0:ncol], src[:, k, t0:t0 + nt], wt[:, k, :], start=(k == 0), stop=(k == kc - 1))
                vi[0] += 1
                vs = self.VST[vi[0] % 2]
                self.act(vs[0:nt, 0:ncol], ps[0:nt, 0:ncol], AF.Copy)
                self.dma(dst[t0:t0 + nt, col0:col0 + ncol], vs[0:nt, 0:ncol])

        for h in range(6):
            wt = self.wload(("nak", l, h))
            st = kst()
            for (t0, t1) in TBS:
                ps = self.bank()
                self.proj_fm(wt, 0, 128, KC, self.H, ps, t0, t1)
                self.act(st[:, t0:t1], ps[:, 0:t1 - t0], AF.Copy)
            self.dma(kdst("na", h), st[:, :])
            if h % 3 == 2:
                trigger("k", h // 3)
        for t, (a, n) in enumerate(VT_NA):
            vproj(("nav", l, t), self.H, KC, a // 384, a % 384)
            if t % 2 == 1:
                trigger("v", t // 2)
        tiles = [self.wload(("ckv", l, t)) for t in range(2)]
        self.mla_norm(l, tiles, 4, V_KVN + l * 4, self.CKVN)
        for h in range(5):
            wt = self.wload(("ukvk", l, h))
            st = kst()
            for (t0, t1) in TBS:
                ps = self.bank()
                self.proj_fm(wt, 0, 128, 4, self.CKVN, ps, t0, t1)
                self.act(st[:, t0:t1], ps[:, 0:t1 - t0], AF.Copy)
            self.dma(kdst("mla", h), st[:, :])
            if h == 2:
                trigger("k", 2)
        wt = self.wload(("kr", l))
        st = kst()
        for (t0, t1) in TBS:
            p1 = self.bank()
            p2 = self.bank()
            self.proj_fm(wt, 0, 64, KC, self.H, p1, t0, t1)
            self.proj_fm(wt, 64, 64, KC, self.H, p2, t0, t1)
            self.rope_evac(st[0:64, t0:t1], p1, p2, 64, t0, t1)
        self.dma(kdst("kr", 0, 64), st[0:64, :])
        trigger("k", 3)
        for t, (a, n) in enumerate(VT_M):
            vproj(("mv", l, t), self.CKVN, 4, 2 + a // 384, a % 384)
            trigger("v", 2 + t)
        for h in range(5):
            w0 = self.wload(("dk", l, h, 0))
            w1 = self.wload(("dk", l, h, 1))
            st = kst()
            for (t0, t1) in TBS:
                p1 = self.bank()
                p2 = self.bank()
                self.proj_fm(w0, 0, 128, KC, self.H, p1, t0, t1)
                self.proj_fm(w1, 0, 128, KC, self.H, p2, t0, t1)
                self.rope_evac(st[:, t0:t1], p1, p2, 128, t0, t1)
            self.dma(kdst("diff", h), st[:, :])
            if h == 2 or h == 4:
                trigger("k", 4 + h // 3)
        for t, (a, n) in enumerate(VT_D):
            vproj(("dv", l, t), self.H, KC, 4 + a // 384, a % 384)
            if t >= 1:
                trigger("v", 4 + t - 1)

    def q_phase(self, l):
        for h in range(6):
            wt = self.wload(("naq", l, h))
            for (t0, t1) in TBS:
                ps = self.bank()
                self.proj_fm(wt, 0, 128, KC, self.H, ps, t0, t1)
                self.act(self.QNA[:, h, t0:t1], ps[:, 0:t1 - t0], AF.Copy)
        tiles = [self.wload(("cq", l, t)) for t in range(3)]
        self.mla_norm(l, tiles, 6, V_QN + l * 6, self.CQN)
        for h in range(5):
            wn = self.wload(("uqn", l, h))
            wr = self.wload(("uqr", l, h))
            for (t0, t1) in TBS:
                ps = self.bank()
                self.proj_fm(wn, 0, 128, 6, self.CQN, ps, t0, t1)
                self.act(self.QMN[:, h, t0:t1], ps[:, 0:t1 - t0], AF.Copy)
                p1 = self.bank()
                p2 = self.bank()
                self.proj_fm(wr, 0, 64, 6, self.CQN, p1, t0, t1)
                self.proj_fm(wr, 64, 64, 6, self.CQN, p2, t0, t1)
                self.rope_evac(self.QMR[0:64, h, t0:t1], p1, p2, 64, t0, t1)
        for h in range(5):
            w0 = self.wload(("dq", l, h, 0))
            w1 = self.wload(("dq", l, h, 1))
            for (t0, t1) in TBS:
                p1 = self.bank()
                p2 = self.bank()
                self.proj_fm(w0, 0, 128, KC, self.H, p1, t0, t1)
                self.proj_fm(w1, 0, 128, KC, self.H, p2, t0, t1)
                self.rope_evac(self.QD[:, h, t0:t1], p1, p2, 128, t0, t1)

    def attend(self, nq, chunks, maps, v_fn, scale, obanks, lbanks):
        nm = len(maps)
        sb_i = [0]

        def scores(ci):
            ch = chunks[ci]
            nk = ch[0]
            out = []
            for m in range(nm):
                ps = self.psum[sb_i[0] % 4]
                sb_i[0] += 1
                pieces = maps[m]
                for pi, (kf, q) in enumerate(pieces):
                    self.mm(ps[0:nk, 0:nq], kf(ch), q, start=(pi == 0), stop=(pi == len(pieces) - 1))
                out.append(ps)
            return out

        pend = scores(0)
        pti = 0
        for ci in range(len(chunks)):
            cur = pend
            if ci + 1 < len(chunks):
                pend = scores(ci + 1)
            nk = chunks[ci][0]
            for m in range(nm):
                pt = self.PT[pti % 4]
                pti += 1
                self.act(pt[0:nk, 0:nq], cur[m][0:nk, 0:nq], AF.Exp, scale=float(scale))
                first = ci == 0
                last = ci == len(chunks) - 1
                self.mm(obanks[m][:, 0:nq], v_fn(chunks[ci]), pt[0:nk, 0:nq], start=first, stop=last)
                self.mm(lbanks[m][:, 0:nq], self.ONESB[0:nk, :], pt[0:nk, 0:nq], start=first, stop=last)

    def kview(self, br, h, nrows=128):
        ci, r0 = kvloc(br, h)
        return self.coutK[ci].ap().rearrange("(r k) t -> r k t", r=4)[:, r0:r0 + nrows, :]

    def vview(self, br, h):
        ci, c0 = kvloc(br, h)
        return self.coutV[ci].ap().rearrange("(r t) c -> r t c", r=4)[:, :, c0:c0 + 128]

    def mla_attn(self, l):
        KR = self.ring[0][0:64, :].rearrange("p (r t) -> p r t", r=4)
        ckr = self.kview("kr", 0, 64)
        self.dma(KR, ckr[:, :, 0:TL].rearrange("r d t -> d r t"))
        KRC = self.sb("KRC", [64, 4, 64], BF16, self.o_scr + 20992 - 512)
        self.dma(KRC[:, :, :], ckr[:, :, TL:T].rearrange("r d t -> d r t"))
        for h in range(5):
            KN = self.ring[1 if h % 2 == 0 else 3][:, :].rearrange("p (r t) -> p r t", r=4)
            VV = self.ring[2][:, :].rearrange("p (r n c) -> p r n c", r=4, n=8)
            KC_ = self.KCX[h % 2]
            VC_ = self.VCX[h % 2]
            ck = self.kview("mla", h)
            cv = self.vview("mla", h)
            self.dma(KN, ck[:, :, 0:TL].rearrange("r d t -> d r t"))
            self.dma(KC_[:, 0, :, :], ck[:, :, TL:T].rearrange("r d t -> d r t"))
            for r_ in range(4):
                self.dma(VV[:, r_, :, :], cv[r_, 0:TL, :].rearrange("(n p) c -> p n c", p=128))
            self.dma(VC_[:, :, :], cv[:, TL:T, :].rearrange("r p c -> p r c"))
            lat = [(128, "l", r, n) for r in range(4) for n in range(8)]
            ctxc = [(64, "c", r, 0) for r in range(4)]

            def kn(ch):
                return KN[:, ch[2], ch[3] * 128:(ch[3] + 1) * 128] if ch[1] == "l" else KC_[:, 0, ch[2], :]

            def kr(ch):
                return KR[:, ch[2], ch[3] * 128:(ch[3] + 1) * 128] if ch[1] == "l" else KRC[:, ch[2], :]

            def vf(ch):
                return VV[:, ch[2], ch[3], :] if ch[1] == "l" else VC_[:, ch[2], :]

            for (t0, t1) in QBS:
                nq = t1 - t0
                chunks = (lat + ctxc) if t0 < TL else ctxc
                maps = [[(kn, self.QMN[:, h, t0:t1]), (kr, self.QMR[0:64, h, t0:t1])]]
                self.attend(nq, chunks, maps, vf, MLA_SCALE, [self.psum[4]], [self.psum[5]])
                self.recip(self.RC[0][:, 0:nq], self.psum[5][:, 0:nq])
                self.tt(self.OT[:, 6 + h, t0:t1], self.psum[4][:, 0:nq], self.RC[0][:, 0:nq], ALU.mult)

    def diff_attn(self, l, lam_init):
        DL = self.sb("DL", [1, 512], F32, self.o_scr + 16384)
        LS = self.sb("LS", [1, 8], F32, self.o_scr + 16384 + 2048)
        self.dma(DL[:, :], self.dlam_d[:, :])
        b0 = l * 256
        self.tt(DL[0:1, b0:b0 + 64], DL[0:1, b0:b0 + 64], DL[0:1, b0 + 64:b0 + 128], ALU.mult)
        self.tt(DL[0:1, b0 + 128:b0 + 192], DL[0:1, b0 + 128:b0 + 192], DL[0:1, b0 + 192:b0 + 256], ALU.mult)
        self.P.add("dve", lambda e: e.reduce_sum(LS[0:1, 0:1], DL[0:1, b0:b0 + 64], mybir.AxisListType.X),
                   [DL[0:1, b0:b0 + 64]], [LS[0:1, 0:1]])
        self.P.add("dve", lambda e: e.reduce_sum(LS[0:1, 1:2], DL[0:1, b0 + 128:b0 + 192], mybir.AxisListType.X),
                   [DL[0:1, b0 + 128:b0 + 192]], [LS[0:1, 1:2]])
        self.act(LS[0:1, 2:4], LS[0:1, 0:2], AF.Exp)
        self.tt(LS[0:1, 4:5], LS[0:1, 3:4], LS[0:1, 2:3], ALU.subtract)
        self.ts(LS[0:1, 4:5], LS[0:1, 4:5], float(-lam_init), None, ALU.add)
        ps = self.psum[0]
        self.mm(ps[:, 0:1], self.ONESF[0:1, :], LS[0:1, 4:5], start=True, stop=True)
        self.vcopy(self.NEGLAM[:, 0:1], ps[:, 0:1])
        self.ts(self.SUBW[:, 0:1], self.VECS[:, V_SUB + l:V_SUB + l + 1], float(1.0 - lam_init), None, ALU.mult)
        for h in range(5):
            KD = self.ring[h % 2][:, :].rearrange("p (r t) -> p r t", r=4)
            VV = self.ring[2 + h % 2][:, :].rearrange("p (r n c) -> p r n c", r=4, n=8)
            KC_ = self.KCX[h % 2]
            VC_ = self.VCX[h % 2]
            ck = self.kview("diff", h)
            cv = self.vview("diff", h)
            self.dma(KD, ck[:, :, 0:TL].rearrange("r d t -> d r t"))
            self.dma(KC_[:, 0, :, :], ck[:, :, TL:T].rearrange("r d t -> d r t"))
            for r_ in range(4):
                self.dma(VV[:, r_, :, :], cv[r_, 0:TL, :].rearrange("(n p) c -> p n c", p=128))
            self.dma(VC_[:, :, :], cv[:, TL:T, :].rearrange("r p c -> p r c"))
            lat = [(128, "l", r, n) for r in range(4) for n in range(8)]
            ctxc = [(64, "c", r, 0) for r in range(4)]

            def k1(ch):
                return KD[0:64, ch[2], ch[3] * 128:(ch[3] + 1) * 128] if ch[1] == "l" else KC_[0:64, 0, ch[2], :]

            def k2(ch):
                return KD[64:128, ch[2], ch[3] * 128:(ch[3] + 1) * 128] if ch[1] == "l" else KC_[64:128, 0, ch[2], :]

            def vf(ch):
                return VV[:, ch[2], ch[3], :] if ch[1] == "l" else VC_[:, ch[2], :]

            for (t0, t1) in QBS:
                nq = t1 - t0
                chunks = (lat + ctxc) if t0 < TL else ctxc
                maps = [[(k1, self.QD[0:64, h, t0:t1])], [(k2, self.QD[64:128, h, t0:t1])]]
                O1, L1, O2, L2 = self.psum[4], self.psum[5], self.psum[6], self.psum[7]
                self.attend(nq, chunks, maps, vf, DIFF_SCALE, [O1, O2], [L1, L2])
                self.recip(self.RC[0][:, 0:nq], L1[:, 0:nq])
                self.recip(self.RC[1][:, 0:nq], L2[:, 0:nq])
                self.tt(self.AO[0][:, 0:nq], O1[:, 0:nq], self.RC[0][:, 0:nq], ALU.mult)
                self.tt(self.AO[1][:, 0:nq], O2[:, 0:nq], self.RC[1][:, 0:nq], ALU.mult)
                self.stt(self.AO[0][:, 0:nq], self.AO[1][:, 0:nq], self.NEGLAM[:, 0:1], self.AO[0][:, 0:nq],
                         ALU.mult, ALU.add)
                sq = self.PT[0][:, 0:nq]
                self.act(sq, self.AO[0][:, 0:nq], AF.Square)
                pn = self.psum[0]
                self.mm(pn[:, 0:nq], self.ONESB[:, :], sq, start=True, stop=True)
                self.act(self.RC[0][:, 0:nq], pn[:, 0:nq], AF.Sqrt, bias=self.EPS_AP[:, 0:1], scale=1.0 / 128)
                self.recip(self.RC[0][:, 0:nq], self.RC[0][:, 0:nq])
                self.stt(self.OT[:, 11 + h, t0:t1], self.AO[0][:, 0:nq], self.SUBW[:, 0:1], self.RC[0][:, 0:nq],
                         ALU.mult, ALU.mult)

    def na_attn(self, l):
        base = self.o_ring
        KW = self.sb("KW", [128, 1536], BF16, base)
        KCN = self.sb("KCN", [128, 4, 64], BF16, base + 3072)
        VW = self.sb("VW", [128, 12, 128], BF16, base + 3584)
        VCN = self.sb("VCN", [64, 4, 128], BF16, base + 6656)
        CKP = self.sb("CKP", [128, 4, 256], BF16, base + 8192)
        CKN = self.sb("CKN", [128, 4, 256], BF16, base + 8192 + 2048)
        CVP = self.sb("CVP", [128, 4, 2, 128], BF16, base + 8192 + 4096)
        CVN = self.sb("CVN", [128, 4, 2, 128], BF16, base + 8192 + 6144)
        NB = [self.sb("NB0", [128, 40 * 64], BF16, base + 16384), self.sb("NB1", [128, 38 * 64], BF16, base + 24576)]
        assert NA_OFF[8] == 40 and NA_OFF[16] - NA_OFF[8] == 38
        for h in range(6):
            ci_, o_ = kvloc("na", h)
            ck = self.kview("na", h)
            cv = self.vview("na", h)
            self.dma(KW[:, 256:1280], self.cinK[ci_].ap()[o_:o_ + 128, 0:TL])
            self.dma(CKP[:, :, :], ck[:, :, 768:1024].rearrange("r d t -> d r t"))
            self.dma(CKN[:, :, :], ck[:, :, 0:256].rearrange("r d t -> d r t"))
            self.dma(KCN[:, :, :], ck[:, :, TL:T].rearrange("r d t -> d r t"))
            self.dma(VW[:, 2:10, :], self.cinV[ci_].ap()[0:TL, o_:o_ + 128].rearrange("(n p) c -> p n c", p=128))
            for r_ in range(4):
                self.dma(CVP[:, r_, :, :], cv[r_, 768:1024, :].rearrange("(n p) c -> p n c", p=128))
                self.dma(CVN[:, r_, :, :], cv[r_, 0:256, :].rearrange("(n p) c -> p n c", p=128))
            self.dma(VCN[:, :, :], cv[:, TL:T, :].rearrange("r p c -> p r c"))
            nbsrc = self.nab_d[0 if FAKE else l * 6 + h]
            self.dma(NB[0][:, :], nbsrc[:, 0:40 * 64], eng="pool")
            self.dma(NB[1][:, :], nbsrc[:, 40 * 64:78 * 64], eng="pool")
            for (dst, cand, so) in ((KW[:, 0:256], CKP, 0), (KW[:, 1280:1536], CKN, 4)):
                self.ts(dst, cand[:, 0, :], self.SEL[:, so:so + 1], None, ALU.mult)
                for r in range(1, 4):
                    self.stt(dst, cand[:, r, :], self.SEL[:, so + r:so + r + 1], dst, ALU.mult, ALU.add)
            for (dst, cand, so) in ((VW[:, 0:2, :], CVP, 0), (VW[:, 10:12, :], CVN, 4)):
                self.ts(dst, cand[:, 0, :, :], self.SEL[:, so:so + 1], None, ALU.mult)
                for r in range(1, 4):
                    self.stt(dst, cand[:, r, :, :], self.SEL[:, so + r:so + r + 1], dst, ALU.mult, ALU.add)
            O, L = self.psum[4], self.psum[5]
            for r8 in range(2):
                for rr in range(8):
                    r = r8 * 8 + rr
                    q = self.QNA[:, h, r * 64:(r + 1) * 64]
                    clo, chi = NA_CH[r]
                    nch = chi - clo + 1
                    ps = self.psum[(2 * r) % 4]
                    pc = self.psum[(2 * r + 1) % 4]
                    for ci in range(nch):
                        c = clo + ci
                        self.mm(ps[:, ci * 64:(ci + 1) * 64], KW[:, c * 128:(c + 1) * 128], q, start=True, stop=True)
                    for rk in range(4):
                        self.mm(pc[0:64, rk * 64:(rk + 1) * 64], KCN[:, rk, :], q, start=True, stop=True)
                    boff = (NA_OFF[r] - NA_OFF[r8 * 8]) * 64
                    tb = self.TBI[:, 0:nch * 64]
                    self.stt(tb, ps[:, 0:nch * 64], float(NA_SCALE), NB[r8][:, boff:boff + nch * 64], ALU.mult, ALU.add)
                    pt = self.PT[r % 2]
                    ptc = self.PT[2 + r % 2]
                    self.act(pt[:, 0:nch * 64], tb, AF.Exp)
                    self.act(ptc[0:64, 0:256], pc[0:64, 0:256], AF.Exp, scale=float(NA_SCALE))
                    oc = slice(rr * 64, (rr + 1) * 64)
                    for ci in range(nch):
                        c = clo + ci
                        self.mm(O[:, oc], VW[:, c, :], pt[:, ci * 64:(ci + 1) * 64], start=(ci == 0), stop=False)
                        self.mm(L[:, oc], self.ONESB[:, :], pt[:, ci * 64:(ci + 1) * 64], start=(ci == 0), stop=False)
                    for rk in range(4):
                        self.mm(O[:, oc], VCN[:, rk, :], ptc[0:64, rk * 64:(rk + 1) * 64], start=False, stop=(rk == 3))
                        self.mm(L[:, oc], self.ONESB[0:64, :], ptc[0:64, rk * 64:(rk + 1) * 64], start=False, stop=(rk == 3))
                self.recip(self.RC[0][:, :], L[:, :])
                self.tt(self.OT[:, h, r8 * 512:(r8 + 1) * 512], O[:, :], self.RC[0][:, :], ALU.mult)
            q = self.QNA[:, h, TL:T]
            pc = self.psum[0]
            for rk in range(4):
                self.mm(pc[0:64, rk * 64:(rk + 1) * 64], KCN[:, rk, :], q, start=True, stop=True)
            ptc = self.PT[2]
            self.act(ptc[0:64, 0:256], pc[0:64, 0:256], AF.Exp, scale=float(NA_SCALE))
            for rk in range(4):
                self.mm(O[:, 0:64], VCN[:, rk, :], ptc[0:64, rk * 64:(rk + 1) * 64], start=(rk == 0), stop=(rk == 3))
                self.mm(L[:, 0:64], self.ONESB[0:64, :], ptc[0:64, rk * 64:(rk + 1) * 64], start=(rk == 0), stop=(rk == 3))
            self.recip(self.RC[0][:, 0:64], L[:, 0:64])
            self.tt(self.OT[:, h, TL:T], O[:, 0:64], self.RC[0][:, 0:64], ALU.mult)

    def merge(self, l):
        self.rms_stats()
        self.modulate(l, 1, 3, 4, self.H2)
        self.gate_vec(5, 1.0)
        for q4 in range(4):
            for dq in range(4):
                dj = q4 * 4 + dq
                wg = [self.wload(("g", l, dj, br)) for br in range(3)]
                wb = self.wload(("wb", l, dj))
                for (t0, t1) in TBS:
                    n = t1 - t0
                    pg = [self.psum[i] for i in range(3)]
                    pb = [self.psum[3 + i] for i in range(3)]
                    for br in range(3):
                        self.proj_fm(wg[br], 0, 128, KC, self.H2, pg[br], t0, t1)
                    for br, (ka, kb) in enumerate(((0, 6), (6, 11), (11, 16))):
                        for k in range(ka, kb):
                            self.mm(pb[br][:, 0:n], wb[:, k, :], self.OT[:, k, t0:t1], start=(k == ka), stop=(k == kb - 1))
                    for br in range(3):
                        self.act(self.SG[br][:, 0:n], pg[br][:, 0:n], AF.Sigmoid)
                    self.tt(self.MT[0][:, 0:n], self.SG[0][:, 0:n], pb[0][:, 0:n], ALU.mult)
                    self.tt(self.MT[1][:, 0:n], self.SG[1][:, 0:n], pb[1][:, 0:n], ALU.mult)
                    self.tt(self.MT[0][:, 0:n], self.MT[0][:, 0:n], self.MT[1][:, 0:n], ALU.add)
                    self.tt(self.MT[1][:, 0:n], self.SG[2][:, 0:n], pb[2][:, 0:n], ALU.mult)
                    self.tt(self.YQ[:, dq, t0:t1], self.MT[0][:, 0:n], self.MT[1][:, 0:n], ALU.add)
            for t in range(8):
                wt = self.wload(("wo", l, q4 // 2, t))
                ko = (q4 % 2) * 4
                for dd in range(2):
                    dj2 = t * 2 + dd
                    for (t0, t1) in TBS:
                        n = t1 - t0
                        ps = self.psum[6 + (dd % 2)]
                        for k in range(4):
                            self.mm(ps[:, 0:n], wt[:, ko + k, dd * 128:(dd + 1) * 128], self.YQ[:, k, t0:t1],
                                    start=(k == 0), stop=(k == 3))
                        self.resid_add(dj2, t0, t1, ps)

    def mixer(self, l):
        lam_init = 0.8 - 0.6 * math.exp(-0.3 * l)
        self.rms_stats()
        self.modulate(l, 1, 3, 4, self.H)
        self.k_phase(l)
        if DEBUG_STOP.startswith("raw_kphase"):
            self.tick(flush=True)
        self.chk("kphase")
        self.q_phase(l)
        self.tick(flush=True)
        self.chk("qphase")
        self.na_attn(l)
        self.chk("na")
        self.mla_attn(l)
        self.chk("mla")
        self.diff_attn(l, lam_init)
        self.chk("diff")
        self.merge(l)

    def chk(self, name):
        if DEBUG_STOP in ("raw_" + name, "raw_" + name + "_nocc"):
            raise StopBuild()

    def dump(self, src_f32_ap):
        self.dma(self.dbg, src_f32_ap)

    def finish(self, out_ops):
        op = self.P.add("sp", None, [], [], kind="c")
        op.deps = list(out_ops) + [o for o in self.P.ops if o.kind == "cc"]

    def final(self):
        self.rms_stats()
        g = V_FN
        IDF = self.sb("IDF", [128, 128], F32, self.o_ring)
        OST = [self.sb(f"OST{i}", [128, D], F32, self.o_ring + 8192 * (1 + i)) for i in range(2)]
        YF = [self.sb(f"YF{i}", [128, 128], F32, self.o_ring + 512 + 512 * i) for i in range(4)]
        self.dma(IDF[:, :], self.ident_d[:, :])
        outs = []
        for tc in range(8):
            ost = OST[tc % 2]
            for j4 in range(4):
                ps = self.bank()
                for jj in range(4):
                    j = j4 * 4 + jj
                    yf = YF[j % 4]
                    self.stt(yf[:, :], self.X[:, j, tc * 128:(tc + 1) * 128], self.VECS[:, g + j:g + j + 1],
                             self.RS[:, tc * 128:(tc + 1) * 128], ALU.mult, ALU.mult)
                    self.P.add("pe", lambda e, o=ps[:, jj * 128:(jj + 1) * 128], i=yf[:, :]: e.transpose(o, i, IDF[:, :]),
                               [yf[:, :], IDF[:, :]], [ps[:, jj * 128:(jj + 1) * 128]])
                self.act(ost[:, j4 * 512:(j4 + 1) * 512], ps[:, :], AF.Copy)
            outs.append(self.dma(self.out[tc * 128:(tc + 1) * 128, :], ost[:, :]))
        return outs

    def build(self):
        nc = self.nc
        self.memset(self.EPS_AP[:, :], EPS)
        self.setup()
        outs = []
        stop = DEBUG_STOP
        done = False
        for l in range(2):
            self.load_mods(l)
            if stop == "mods" and l == 0:
                outs.append(self.dump_small(self.MODS[:, :, :].rearrange("p a b -> p (a b)"), 288))
                done = True
                break
            self.ffn(l, 0, 0, 0)
            if stop == "ffn1" and l == 0:
                outs.append(self.dma(self.dbg, self.X[:, :, :].rearrange("p j t -> p (j t)")))
                done = True
                break
            try:
                self.mixer(l)
            except StopBuild:
                src = self.OT if stop in ("raw_na", "raw_mla", "raw_diff") else self.H
                for j in range(KC):
                    self.vcopy(self.X[:, j, :], src[:, j, :])
                outs.append(self.dma(self.dbg, self.X[:, :, :].rearrange("p j t -> p (j t)")))
                done = True
                break
            if stop == "mixer" and l == 0:
                outs.append(self.dma(self.dbg, self.X[:, :, :].rearrange("p j t -> p (j t)")))
                done = True
                break
            self.ffn(l, 1, 2, 6)
            if stop == "layer0" and l == 0:
                outs.append(self.dma(self.dbg, self.X[:, :, :].rearrange("p j t -> p (j t)")))
                done = True
                break
        if not done:
            outs += self.final()
        self.finish(outs)
        with ExitStack() as stack:
            self.P.finalize(stack)
        return nc

    def dump_small(self, ap, n):
        return self.dma(self.dbg[:, 0:n], ap)


_CACHE = {}


def prep_inputs(inputs):
    inp = {k: np.asarray(v) for k, v in inputs.items()}
    wbig, dirn = build_wbig(inp)
    vecs = build_vecs(inp)
    dlam = np.ascontiguousarray(inp["diff_lambda"].reshape(1, 512).astype(np.float32))
    ident = np.eye(128, dtype=np.float32)
    maps = []
    for core in range(8):
        b, rq = core // 4, core % 4
        xT = np.concatenate([inp["x"][b, rq * TL:(rq + 1) * TL, :].T, inp["ctx"][b, rq * TCX:(rq + 1) * TCX, :].T], axis=1)
        cv = np.stack([inp["c"][b], inp["c_ctx"]], axis=-1)
        cT = cv.reshape(KC, 128, 2).transpose(1, 0, 2).reshape(128, KC * 2)
        adaw = np.empty((72, 128, 2048), np.float32)
        adab = np.empty((1, 72 * 128), np.float32)
        for l in range(2):
            for jj in range(36):
                c0 = (36 * rq + jj) * 128
                W = inp["w_ada"][l][:, c0:c0 + 128]
                adaw[l * 36 + jj] = W.reshape(KC, 128, 128).transpose(1, 0, 2).reshape(128, 2048)
                adab[0, (l * 36 + jj) * 128:(l * 36 + jj + 1) * 128] = inp["b_ada"][l, c0:c0 + 128]
        sel = np.zeros((128, 16), np.float32)
        sel[:, 8 + b] = 1.0
        if rq > 0:
            sel[:, rq - 1] = 1.0
        if rq < 3:
            sel[:, 4 + rq + 1] = 1.0
        nab = na_bias(inp["na_rpb"], rq).reshape(12, 128, NA_OFF[-1] * 64)
        maps.append({
            "xT": np.ascontiguousarray(xT.astype(np.float32)),
            "cT": np.ascontiguousarray(cT.astype(np.float32)),
            "adaw": adaw, "adab": adab, "wbig": wbig, "vecs": vecs, "dlam": dlam,
            "rope": rope_table(rq).reshape(128, 2 * T), "nab": np.ascontiguousarray(nab),
            "sel": sel, "ident": ident,
        })
    return maps, dirn, wbig.shape[1]


def kernel(**inputs):
    maps, dirn, wtot = prep_inputs(inputs)
    b = Builder(dirn, wtot)
    nc = b.build()
    res = run_bass_kernel_spmd(nc, maps, core_ids=list(range(8)))
    kernel.last = res
    out = np.empty((2, 4096, D), np.float32)
    for core in range(8):
        bb, rq = core // 4, core % 4
        out[bb, rq * TL:(rq + 1) * TL, :] = res.results[core]["out"]
    return out
```

```python
import math
import os
from contextlib import ExitStack

import numpy as np
import concourse.bass as bass
import concourse.mybir as mybir
from concourse.bass_utils import run_bass_kernel_spmd

F32 = mybir.dt.float32
BF16 = mybir.dt.bfloat16
AF = mybir.ActivationFunctionType
ALU = mybir.AluOpType

D = 2048
KC = 16
TL = 1024
TCX = 64
T = TL + TCX
FFN = 5632
NEG = -30000.0
EPS = 1e-6
NA_SCALE = 128 ** -0.5
MLA_SCALE = 192 ** -0.5
DIFF_SCALE = 64 ** -0.5
TBS = [(0, 384), (384, 768), (768, 1088)]
QBS = [(0, 512), (512, 1024), (1024, 1088)]
O_NAQ, O_NAK, O_NAV, O_CQ, O_CKV, O_KR, O_DQ, O_DK, O_DV, O_GA, O_GB, O_GD = (
    0, 768, 1536, 2304, 3072, 3584, 3648, 4288, 4928, 5568, 7616, 9664)
KROWS = 16 * 128 + 64
DEBUG_STOP = os.environ.get("MK_STOP", "")
FAKE = bool(os.environ.get("MK_FAKE"))


VT_NA = [(0, 256), (256, 128), (384, 256), (640, 128)]
VT_D = [(0, 256), (256, 128), (384, 256)]
VT_M = [(0, 384), (384, 256)]
KCH_ROWS = [384, 384, 384, 320, 384, 256]
VCH_COLS = [384, 384, 384, 256, 384, 256]


def kvloc(br, h):
    if br == "na":
        return h // 3, (h % 3) * 128
    if br == "mla":
        return 2 + h // 3, (h % 3) * 128
    if br == "kr":
        return 3, 256
    return 4 + h // 3, (h % 3) * 128


def na_chunks(r):
    lo = min(r - 4, 8)
    hi = max(r + 4, max(r - 4, 0) + 8, min(r - 4, 8) + 8)
    return (lo + 4) // 2, (hi - 1 + 4) // 2


NA_CH = [na_chunks(r) for r in range(16)]
NA_OFF = np.cumsum([0] + [b - a + 1 for a, b in NA_CH]).tolist()


def partner64(d):
    return d + 16 if (d % 32) < 16 else d - 16


def wtile(W, cols):
    K = W.shape[0]
    kc = K // 128
    a = W[:, cols].reshape(kc, 128, len(cols)).transpose(1, 0, 2).reshape(128, kc * len(cols))
    return a, kc, len(cols)


def build_wbig(inp):
    tiles = []
    dirn = {}
    off = [0]

    def put(key, W, cols):
        a, kc, n = wtile(W, np.asarray(cols))
        dirn[key] = (off[0], kc, n)
        off[0] += a.shape[1]
        tiles.append(a)

    ar = np.arange
    for l in range(2):
        w_in = inp["w_in"][l]
        uq = inp["mla_w_uq"][l]
        ukv = inp["mla_w_ukv"][l]
        p64 = np.array([partner64(d) for d in range(64)])
        p128 = np.concatenate([p64, 64 + p64])
        for s in range(2):
            fwi = inp["ffn_w_in"][l, s]
            fwo = inp["ffn_w_out"][l, s]
            for hf in range(2):
                for jj in range(22):
                    j = hf * 22 + jj
                    put(("fi", l, s, j), fwi, np.concatenate([ar(j * 128, j * 128 + 128), FFN + ar(j * 128, j * 128 + 128)]))
                for dj in range(16):
                    put(("fo", l, s, hf, dj), fwo[hf * 2816:(hf + 1) * 2816], ar(dj * 128, dj * 128 + 128))
        for h in range(6):
            put(("nak", l, h), w_in, O_NAK + h * 128 + ar(128))
        for t in range(2):
            put(("ckv", l, t), w_in, O_CKV + t * 256 + ar(256))
        for h in range(5):
            put(("ukvk", l, h), ukv, h * 256 + ar(128))
        put(("kr", l), w_in, np.concatenate([O_KR + ar(64), O_KR + p64]))
        for h in range(5):
            put(("dk", l, h, 0), w_in, O_DK + h * 128 + ar(128))
            put(("dk", l, h, 1), w_in, O_DK + h * 128 + p128)
        for t, (a, n) in enumerate(VT_NA):
            put(("nav", l, t), w_in, O_NAV + a + ar(n))
        for t, (a, n) in enumerate(VT_D):
            put(("dv", l, t), w_in, O_DV + a + ar(n))
        vcols = np.concatenate([h * 256 + 128 + ar(128) for h in range(5)])
        for t, (a, n) in enumerate(VT_M):
            put(("mv", l, t), ukv, vcols[a:a + n])
        for h in range(6):
            put(("naq", l, h), w_in, O_NAQ + h * 128 + ar(128))
        for t in range(3):
            put(("cq", l, t), w_in, O_CQ + t * 256 + ar(256))
        for h in range(5):
            put(("uqn", l, h), uq, h * 192 + ar(128))
            put(("uqr", l, h), uq, np.concatenate([h * 192 + 128 + ar(64), h * 192 + 128 + p64]))
        for h in range(5):
            put(("dq", l, h, 0), w_in, O_DQ + h * 128 + ar(128))
            put(("dq", l, h, 1), w_in, O_DQ + h * 128 + p128)
        wb = inp["w_branch"][l]
        wo = inp["w_out"][l]
        for dj in range(16):
            for br, o in enumerate((O_GA, O_GB, O_GD)):
                put(("g", l, dj, br), w_in, o + dj * 128 + ar(128))
            put(("wb", l, dj), wb, dj * 128 + ar(128))
        for half in range(2):
            for t in range(8):
                put(("wo", l, half, t), wo[half * 1024:(half + 1) * 1024], t * 256 + ar(256))
    return np.ascontiguousarray(np.concatenate(tiles, axis=1)), dirn


def fm(v):
    return np.ascontiguousarray(v.reshape(-1, 128).T)


def build_vecs(inp):
    cols = []
    for l in range(2):
        for n in range(3):
            cols.append(fm(inp["norm_w"][l, n]))
    cols.append(fm(inp["final_norm"]))
    for l in range(2):
        cols.append(fm(inp["mla_q_norm"][l]))
    for l in range(2):
        cols.append(fm(inp["mla_kv_norm"][l]))
    for l in range(2):
        cols.append(fm(inp["diff_subln"][l]))
    return np.ascontiguousarray(np.concatenate(cols, axis=1).astype(np.float32))


V_NW, V_FN, V_QN, V_KVN, V_SUB = 0, 96, 112, 124, 132
NV = 134


def rope_table(rq):
    theta = 10000.0
    quarter = 16
    freqs = (np.float32(theta) ** (-np.arange(quarter, dtype=np.float32) / np.float32(quarter))).astype(np.float32)
    g = rq * TL + np.arange(TL)
    rows = (g // 64).astype(np.float32)
    colsp = (g % 64).astype(np.float32)
    tab = np.zeros((64, 2, T), np.float32)
    tab[:, 0, TL:] = 1.0
    for d in range(64):
        pos = rows if d < 32 else colsp
        ang = (pos * freqs[d % 16]).astype(np.float32)
        tab[d, 0, :TL] = np.cos(ang)
        sn = np.sin(ang)
        tab[d, 1, :TL] = -sn if (d % 32) < 16 else sn
    return np.ascontiguousarray(np.concatenate([tab, tab], axis=0))


def na_bias(rpb, rq):
    out = np.full((2, 6, 128, NA_OFF[-1], 64), NEG, np.float32)
    qc = np.arange(64)
    kc = np.arange(64)
    c0 = np.clip(qc - 8, 0, 48)
    col_ok = (kc[:, None] >= c0[None, :]) & (kc[:, None] < c0[None, :] + 16)
    col_off = np.clip(kc[:, None] - qc[None, :], -15, 15) + 15
    for r in range(16):
        gr = 16 * rq + r
        r0 = min(max(gr - 4, 0), 56)
        clo, chi = NA_CH[r]
        for ci, c in enumerate(range(clo, chi + 1)):
            for kk in range(2):
                kr = 16 * rq + (-4 + 2 * c + kk)
                if kr < 0 or kr >= 64 or kr < r0 or kr >= r0 + 8:
                    continue
                ro = kr - gr + 7
                vals = rpb[:, :, ro, :][:, :, col_off]
                blk = out[:, :, kk * 64:(kk + 1) * 64, NA_OFF[r] + ci, :]
                blk[...] = np.where(col_ok[None, None], vals, NEG)
    return out


class Op:
    __slots__ = ("eng", "fn", "deps", "pos", "awaited", "kind", "idx", "waits", "sem", "semval", "ccg")

    def __init__(self, eng, fn, kind):
        self.eng = eng
        self.fn = fn
        self.kind = kind
        self.deps = []
        self.awaited = False
        self.waits = []
        self.sem = None
        self.semval = 0


ENGS = ("pe", "act", "dve", "pool", "sp")
NDSEM = 16
SEM_CH = 4000


class Prog:
    def __init__(self, nc):
        self.nc = nc
        self.ops = []
        self.eng_ops = {e: [] for e in ENGS}
        self.recs = {}
        self.ndma = 0
        self.dma_ops = []
        self.skip = set()

    @staticmethod
    def _dsz(dt):
        return mybir.dt.size(dt)

    def region(self, ap):
        t = ap.tensor
        name = t.name
        if name in self.skip:
            return None
        pat = ap.ap
        rng = t.manual_sbuf_range
        if rng is not None:
            ps = 1
            for s in t.shape[1:]:
                ps *= int(s)
            lo = int(ap.offset) % ps
            ext = 1
            for (st, cnt) in pat[1:]:
                ext += (int(cnt) - 1) * abs(int(st))
            esz = self._dsz(t.dtype)
            return ("sb", rng[0] + lo * esz, rng[0] + (lo + ext) * esz)
        tn = type(t).__name__
        if tn.startswith("PSum") or tn.startswith("SB"):
            ps = 1
            for s in t.shape[1:]:
                ps *= int(s)
            lo = int(ap.offset) % ps
            ext = 1
            for (st, cnt) in pat[1:]:
                ext += (int(cnt) - 1) * abs(int(st))
            return (name, lo, lo + ext)
        lo = int(ap.offset)
        ext = 1
        for (st, cnt) in pat:
            ext += (int(cnt) - 1) * abs(int(st))
        return (name, lo, lo + ext)

    def _pages(self, sp, lo, hi):
        pg = 2048 if sp == "sb" else 65536
        return range(lo // pg, (hi - 1) // pg + 1)

    def add(self, eng, fn, reads=(), writes=(), kind="c", ccg=0):
        op = Op(eng, fn, kind)
        op.ccg = ccg
        op.idx = len(self.ops)
        deps = {}
        rkey = (eng + kind) if kind == "c" else ("d", op.idx)
        for ap in reads:
            rg = self.region(ap)
            if rg is None:
                continue
            sp, lo, hi = rg
            pages = self.recs.setdefault(sp, {})
            found = None
            for pgi in self._pages(sp, lo, hi):
                for rec in pages.get(pgi, ()):
                    if rec[0] < hi and lo < rec[1]:
                        w = rec[2]
                        if w is not None:
                            deps[w.idx] = (w, "raw")
                        if rec[0] == lo and rec[1] == hi:
                            found = rec
            if found is None:
                found = [lo, hi, None, {}]
                for pgi in self._pages(sp, lo, hi):
                    pages.setdefault(pgi, []).append(found)
            found[3][rkey] = op
        for ap in writes:
            rg = self.region(ap)
            if rg is None:
                continue
            sp, lo, hi = rg
            pages = self.recs.setdefault(sp, {})
            newrec = [lo, hi, op, {}]
            for pgi in self._pages(sp, lo, hi):
                lst = pages.get(pgi)
                if lst is None:
                    pages[pgi] = [newrec]
                    continue
                keep = []
                for rec in lst:
                    if rec[0] < hi and lo < rec[1]:
                        w = rec[2]
                        if w is not None and w.idx not in deps:
                            deps[w.idx] = (w, "waw")
                        for rd in rec[3].values():
                            if rd is not op and rd.idx not in deps:
                                deps[rd.idx] = (rd, "war")
                        if lo <= rec[0] and rec[1] <= hi:
                            continue
                    keep.append(rec)
                keep.append(newrec)
                pages[pgi] = keep
        if kind != "c":
            if kind == "d":
                op.sem = ("d", self.ndma % NDSEM)
                op.semval = 16 * (self.ndma // NDSEM + 1)
                if self.ndma >= NDSEM:
                    prev = self.dma_ops[self.ndma - NDSEM]
                    deps.setdefault(prev.idx, (prev, "raw"))
                self.dma_ops.append(op)
                self.ndma += 1
        for (d, typ) in deps.values():
            if d.kind == "c" and d.eng == eng:
                if eng == "pe":
                    continue
                if typ != "raw" and kind == "c":
                    continue
            op.deps.append(d)
        op.pos = len(self.eng_ops[eng])
        self.eng_ops[eng].append(op)
        self.ops.append(op)
        return op

    def finalize(self, stack):
        nc = self.nc
        known = {e: {f: -1 for f in ENGS} for e in ENGS}
        knownd = {e: set() for e in ENGS}
        for op in self.ops:
            e = op.eng
            for d in op.deps:
                if d.kind == "c":
                    if d.pos <= known[e][d.eng]:
                        continue
                    known[e][d.eng] = d.pos
                    d.awaited = True
                    op.waits.append(d)
                else:
                    if d.idx in knownd[e]:
                        continue
                    knownd[e].add(d.idx)
                    op.waits.append(d)
        esems = {}
        for e in ENGS:
            cnt = 0
            for op in self.eng_ops[e]:
                if op.kind == "c" and op.awaited:
                    op.sem = (e, cnt // SEM_CH)
                    op.semval = cnt % SEM_CH + 1
                    cnt += 1
            esems[e] = [stack.enter_context(nc.semaphore(f"s_{e}_{i}")) for i in range(cnt // SEM_CH + 1)]
        dsems = [stack.enter_context(nc.semaphore(f"s_dma_{i}")) for i in range(NDSEM)]
        ccsems = [stack.enter_context(nc.semaphore(f"s_cc_{i}")) for i in range(2)]
        ccops = [op for op in self.ops if op.kind == "cc"]
        for op in ccops:
            if op.ccg == 0:
                op.sem = ("cc", 0)
                op.semval = sum(1 for o in ccops if o.ccg == 0)
            else:
                op.sem = ("cc", 1)
                op.semval = sum(1 for o in ccops if 0 < o.ccg <= op.ccg)

        def semof(op):
            kind, i = op.sem
            if kind == "d":
                return dsems[i]
            if kind == "cc":
                return ccsems[i]
            return esems[kind][i]

        def emit(ename, eng):
            for op in self.eng_ops[ename]:
                for d in op.waits:
                    eng.wait_ge(semof(d), d.semval)
                if op.fn is None:
                    continue
                inst = op.fn(eng)
                if op.kind == "d":
                    inst.then_inc(semof(op), 16)
                elif op.kind == "cc":
                    inst.then_inc(semof(op))
                elif op.awaited:
                    inst.then_inc(semof(op), 1)

        block = stack.enter_context(nc.Block())

        @block.tensor
        def _(eng):
            emit("pe", eng)

        @block.scalar
        def _(eng):
            emit("act", eng)

        @block.vector
        def _(eng):
            emit("dve", eng)

        @block.gpsimd
        def _(eng):
            emit("pool", eng)

        @block.sync
        def _(eng):
            emit("sp", eng)


class StopBuild(Exception):
    pass


class Builder:
    def __init__(self, dirn, wtot):
        self.dirn = dirn
        self.wtot = wtot
        self.nc = nc = bass.Bass("TRN2", target_bir_lowering=False)
        self.P = Prog(nc)
        P = self.P
        dt = nc.dram_tensor
        self.xT = dt("xT", [D, T], F32, kind="ExternalInput").ap()
        self.cT = dt("cT", [128, KC * 2], F32, kind="ExternalInput").ap()
        if FAKE:
            wtot = self.wtot = 8192
        self.adaw = dt("adaw", [1 if FAKE else 72, 128, 2048], F32, kind="ExternalInput").ap()
        self.adab = dt("adab", [1, 72 * 128], F32, kind="ExternalInput").ap()
        self.wbig = dt("wbig", [128, wtot], F32, kind="ExternalInput").ap()
        self.vecs_d = dt("vecs", [128, NV], F32, kind="ExternalInput").ap()
        self.dlam_d = dt("dlam", [1, 512], F32, kind="ExternalInput").ap()
        self.rope_d = dt("rope", [128, 2 * T], F32, kind="ExternalInput").ap()
        self.nab_d = dt("nab", [1 if FAKE else 12, 128, NA_OFF[-1] * 64], F32, kind="ExternalInput").ap()
        self.sel_d = dt("sel", [128, 16], F32, kind="ExternalInput").ap()
        self.ident_d = dt("ident", [128, 128], F32, kind="ExternalInput").ap()
        for n in ("xT", "cT", "adaw", "adab", "wbig", "vecs", "dlam", "rope", "nab", "sel", "ident"):
            P.skip.add(n)
        self.out = dt("out", [TL, D], F32, kind="ExternalOutput").ap()
        self.dbg = None
        if DEBUG_STOP:
            self.dbg = dt("dbg", [128, KC * T], F32, kind="ExternalOutput").ap()
        self.cinK = [dt(f"cinK{i}", [n, T], BF16) for i, n in enumerate(KCH_ROWS)]
        self.coutK = [dt(f"coutK{i}", [4 * n, T], BF16) for i, n in enumerate(KCH_ROWS)]
        self.cinV = [dt(f"cinV{i}", [T, n], BF16) for i, n in enumerate(VCH_COLS)]
        self.coutV = [dt(f"coutV{i}", [4 * T, n], BF16) for i, n in enumerate(VCH_COLS)]
        self.cinA = dt("cinA", [128, 144], F32)
        self.coutA = dt("coutA", [4 * 128, 144], F32)
        arena = nc.alloc_sbuf_tensor("arena", [128, 212736], mybir.dt.uint8)
        self.abase = int(nc.lookup_mloc(arena).addr)
        self.ncnt = 0
        self.psum = [nc.alloc_psum_tensor(f"ps{i}", [128, 512], F32) for i in range(8)]
        self.OX = 0
        self.OB = 69632
        self.OC = self.OB + 34816
        self.OD = self.OC + 47872
        self.X = self.sb("X", [128, KC, T], F32, self.OX)
        self.H = self.sb("H", [128, KC, T], BF16, self.OB)
        self.OT = self.sb("OT", [128, KC, T], BF16, self.OB)
        self.U = self.sb("U", [128, 22, T], BF16, self.OC)
        self.H2 = self.sb("H2", [128, KC, T], BF16, self.OC)
        self.QNA = self.sb("QNA", [128, 6, T], BF16, self.OC)
        self.QMN = self.sb("QMN", [128, 5, T], BF16, self.OC + 13056)
        self.QMR = self.sb("QMR", [128, 5, T], BF16, self.OC + 13056 + 10880)
        self.QD = self.sb("QD", [128, 5, T], BF16, self.OC + 13056 + 21760)
        self.CQN = self.sb("CQN", [128, 6, T], BF16, self.OC + 13056 + 21760)
        o = self.OD
        self.MALL = self.sb("MALL", [128, 4, 144], F32, o); o += 3456
        self.MODS = self.sb("MODS", [128, 2, 144], F32, o); o += 1152
        self.VECS = self.sb("VECS", [128, NV], F32, o); o += 544
        self.AV = self.sb("AV", [128, 2, 16], F32, o); o += 128
        self.BV = self.sb("BV", [128, 2, 16], F32, o); o += 128
        self.GV = self.sb("GV", [128, 2, 16], F32, o); o += 128
        self.SEL = self.sb("SEL", [128, 16], F32, o); o += 64
        self.NEGLAM = self.sb("NEGLAM", [128, 4], F32, o); o += 32
        self.SUBW = self.sb("SUBW", [128, 2], F32, o); o += 32
        self.EPS_AP = self.sb("EPSC", [128, 1], F32, o); o += 32
        self.ONESB = self.sb("ONESB", [128, 128], BF16, o); o += 256
        self.ONESF = self.sb("ONESF", [128, 128], F32, o); o += 512
        self.o_scr = o
        self.RS = self.sb("RS", [128, T], F32, o); o += 4352
        self.TMPF = self.sb("TMPF", [128, T], F32, o); o += 4352
        self.SQ = self.sb("SQ", [128, 2, 512], BF16, o); o += 2048
        self.SA = self.sb("SA", [128, 2, 512], F32, o); o += 4096
        o_x = o
        o += 2048 + 4096
        self.ROPE = self.sb("ROPE", [128, 2, T], BF16, self.o_scr + 4352)
        self.KST = [self.sb(f"KST{i}", [128, T], BF16, self.o_scr + 10752 + i * 2176) for i in range(2)]
        self.VST = [self.sb(f"VST{i}", [128, 384], BF16, self.o_scr + 10752 + 4352 + i * 768) for i in range(2)]
        self.RT = [self.sb(f"RT{i}", [128, 512], F32, o_x + 2048 + i * 2048) for i in range(2)]
        self.CKVN = self.sb("CKVN", [128, 4, T], BF16, self.OC + 13056 + 21760)
        a = self.o_scr
        self.PT = [self.sb(f"PT{i}", [128, 512], BF16, a + i * 1024) for i in range(4)]; a += 4096
        self.RC = [self.sb(f"RC{i}", [128, 512], F32, a + i * 2048) for i in range(2)]; a += 4096
        self.AO = [self.sb(f"AO{i}", [128, 512], F32, a + i * 2048) for i in range(2)]; a += 4096
        self.TBI = self.sb("TBI", [128, 384], F32, a); a += 1536
        self.TBI2 = self.sb("TBI2", [128, 384], F32, self.o_scr + 17920)
        self.KCX = [self.sb(f"KCX{i}", [128, 2, 4, 64], BF16, a + i * 1024) for i in range(2)]; a += 2048
        self.VCX = [self.sb(f"VCX{i}", [64, 4, 128], BF16, a + i * 1024) for i in range(2)]; a += 2048
        assert a <= o, (a, o)
        a = self.o_scr + 4352
        self.SG = [self.sb(f"SG{i}", [128, 512], F32, a + i * 2048) for i in range(3)]; a += 6144
        self.MT = [self.sb(f"MT{i}", [128, 512], F32, a + i * 2048) for i in range(2)]; a += 4096
        assert a <= o
        self.YQ = self.sb("YQ", [128, 4, T], BF16, self.OC + 34816)
        self.o_ring = o
        self.NSLOT = 4
        self.ring = [self.sb(f"ring{i}", [128, 4096], BF16, o + i * 8192) for i in range(self.NSLOT)]
        self.ringf = [self.sb(f"ringf{i}", [128, 2048], F32, o + i * 8192) for i in range(self.NSLOT)]
        o += self.NSLOT * 8192
        self.o_end = o
        assert o <= 212736, o
        self.wi = 0
        self.psi = 0
        self.pending = []

    def sb(self, name, shape, dt, off):
        self.ncnt += 1
        return self.nc.alloc_sbuf_tensor_at(f"{name}_{self.ncnt}", shape, dt, offset=self.abase + off)

    def mm(self, out, lhsT, rhs, start=True, stop=True):
        self.P.add("pe", lambda e: e.matmul(out, lhsT, rhs, start=start, stop=stop), [lhsT, rhs], [out])

    def act(self, out, in_, func, bias=None, scale=None):
        kw = {}
        rd = [in_]
        if bias is not None:
            kw["bias"] = bias
            if not isinstance(bias, (int, float)):
                rd.append(bias)
        if scale is not None:
            kw["scale"] = scale
            if not isinstance(scale, (int, float)):
                rd.append(scale)
        self.P.add("act", lambda e: e.activation(out, in_, func, **kw), rd, [out])

    def tt(self, out, in0, in1, op):
        self.P.add("dve", lambda e: e.tensor_tensor(out, in0, in1, op), [in0, in1], [out])

    def ts(self, out, in0, s1, s2, op0, op1=None):
        rd = [in0] + [s for s in (s1, s2) if s is not None and not isinstance(s, (int, float))]
        if op1 is None:
            self.P.add("dve", lambda e: e.tensor_scalar(out, in0, s1, None, op0), rd, [out])
        else:
            self.P.add("dve", lambda e: e.tensor_scalar(out, in0, s1, s2, op0, op1), rd, [out])

    def stt(self, out, in0, scalar, in1, op0, op1):
        rd = [in0, in1] + ([] if isinstance(scalar, (int, float)) else [scalar])
        self.P.add("dve", lambda e: e.scalar_tensor_tensor(out, in0, scalar, in1, op0, op1), rd, [out])

    def recip(self, out, in_):
        self.P.add("dve", lambda e: e.reciprocal(out, in_), [in_], [out])

    def vcopy(self, out, in_):
        self.P.add("dve", lambda e: e.tensor_copy(out, in_), [in_], [out])

    def memset(self, ap, v):
        self.P.add("dve", lambda e: e.memset(ap, v), [], [ap])

    def dma(self, out, in_, eng="sp"):
        return self.P.add(eng, lambda e: e.dma_start(out=out, in_=in_), [in_], [out], kind="d")

    def bank(self):
        b = self.psum[self.psi % 4]
        self.psi += 1
        return b

    def wload(self, key):
        off, kc, n = self.dirn[key]
        if FAKE:
            off = 0
        slot = self.ring[self.wi % self.NSLOT]
        self.wi += 1
        v = slot[:, 0:kc * n]
        self.dma(v, self.wbig[:, off:off + kc * n], eng="pool")
        self.tick()
        return v.rearrange("p (k c) -> p k c", k=kc)

    def tick(self, flush=False):
        keep = []
        for it in self.pending:
            it[0] -= 1
            if it[0] <= 0 or flush:
                it[1]()
            else:
                keep.append(it)
        self.pending = keep

    def setup(self):
        self.memset(self.ONESB[:, :], 1.0)
        self.memset(self.ONESF[:, :], 1.0)
        self.dma(self.X[:, :, :], self.xT.rearrange("(j p) t -> p j t", p=128))
        self.dma(self.VECS[:, :], self.vecs_d[:, :])
        self.dma(self.SEL[:, :], self.sel_d[:, :])
        CS = self.sb("CS", [128, KC * 2], F32, self.o_scr)
        BAD = self.sb("BAD", [1, 72 * 128], F32, self.OC)
        ADAL = self.sb("ADAL", [128, 144], F32, self.o_scr + 192)
        self.dma(CS[:, :], self.cT[:, :])
        self.dma(BAD[:, :], self.adab[:, :])
        self.act(CS[:, :], CS[:, :], AF.Silu)
        CS3 = CS[:, :].rearrange("p (k v) -> p k v", v=2)
        for t in range(72):
            wt = self.ringf[self.wi % self.NSLOT]
            self.wi += 1
            self.dma(wt[:, :], self.adaw[0 if FAKE else t])
            w3 = wt[:, :].rearrange("p (k c) -> p k c", k=KC)
            ps = self.bank()
            for k in range(KC):
                self.mm(ps[:, 0:2], w3[:, k, :], CS3[:, k, :], start=(k == 0), stop=False)
            self.mm(ps[:, 0:2], BAD[0:1, t * 128:(t + 1) * 128], self.ONESF[0:1, 0:2], start=False, stop=True)
            self.vcopy(ADAL[:, t * 2:(t + 1) * 2], ps[:, 0:2])
        self.dma(self.cinA.ap(), ADAL[:, :])
        self.P.add("pool", lambda e: e.collective_compute(
            "AllGather", ALU.bypass, replica_groups=[[0, 1, 2, 3], [4, 5, 6, 7]],
            ins=[self.cinA.ap().opt()], outs=[self.coutA.ap().opt()]),
            [self.cinA.ap()], [self.coutA.ap()], kind="cc")
        self.dma(self.MALL[:, :, :], self.coutA.ap().rearrange("(i p) c -> p i c", p=128))

    def load_mods(self, l):
        for cls in range(2):
            src = self.MALL[:, :, l * 72 + cls:l * 72 + 72:2]
            dst = self.MODS[:, cls, :].rearrange("p (i j) -> p i j", i=4)
            self.vcopy(dst, src)

    def mod(self, cls, n, j=None):
        if j is None:
            return self.MODS[:, cls, n * 16:(n + 1) * 16]
        return self.MODS[:, cls, n * 16 + j:n * 16 + j + 1]

    def rms_stats(self, nfeat_scale=1.0 / D):
        for (t0, t1) in TBS:
            n = t1 - t0
            ps = self.psum[7]
            for j in range(KC):
                sq = self.SQ[:, j % 2, 0:n]
                self.act(sq, self.X[:, j, t0:t1], AF.Square)
                self.mm(ps[:, 0:n], self.ONESB[:, :], sq, start=(j == 0), stop=(j == KC - 1))
            self.act(self.RS[:, t0:t1], ps[:, 0:n], AF.Sqrt, bias=self.EPS_AP[:, 0:1], scale=nfeat_scale)
            self.recip(self.RS[:, t0:t1], self.RS[:, t0:t1])

    def modulate(self, l, nidx, n_shift, n_scale, dst):
        g = self.VECS[:, V_NW + (l * 3 + nidx) * 16:V_NW + (l * 3 + nidx + 1) * 16]
        for cls in range(2):
            self.ts(self.AV[:, cls, :], self.mod(cls, n_scale), 1.0, None, ALU.add)
            self.tt(self.AV[:, cls, :], self.AV[:, cls, :], g, ALU.mult)
        for j in range(KC):
            for cls, (t0, t1) in ((0, (0, TL)), (1, (TL, T))):
                self.tt(self.TMPF[:, t0:t1], self.X[:, j, t0:t1], self.RS[:, t0:t1], ALU.mult)
                self.act(dst[:, j, t0:t1], self.TMPF[:, t0:t1], AF.Identity,
                         bias=self.mod(cls, n_shift, j), scale=self.AV[:, cls, j:j + 1])

    def gate_vec(self, n_gate, factor):
        for cls in range(2):
            self.ts(self.GV[:, cls, :], self.mod(cls, n_gate), float(factor), None, ALU.mult)

    def resid_add(self, dj, t0, t1, ps):
        for (a, b, cls) in ((t0, min(t1, TL), 0), (max(t0, TL), t1, 1)):
            if b <= a:
                continue
            self.stt(self.X[:, dj, a:b], ps[:, a - t0:b - t0], self.GV[:, cls, dj:dj + 1], self.X[:, dj, a:b],
                     ALU.mult, ALU.add)

    def ffn(self, l, s, nidx, n0):
        self.rms_stats()
        self.modulate(l, nidx, n0, n0 + 1, self.H)
        self.gate_vec(n0 + 2, 0.5)
        for hf in range(2):
            for jj in range(22):
                wt = self.wload(("fi", l, s, hf * 22 + jj))
                for (t0, t1) in TBS:
                    n = t1 - t0
                    pa = self.bank()
                    pb = self.bank()
                    for k in range(KC):
                        self.mm(pa[:, 0:n], wt[:, k, 0:128], self.H[:, k, t0:t1], start=(k == 0), stop=(k == KC - 1))
                    for k in range(KC):
                        self.mm(pb[:, 0:n], wt[:, k, 128:256], self.H[:, k, t0:t1], start=(k == 0), stop=(k == KC - 1))
                    sa = self.SA[:, (self.psi // 2) % 2, 0:n]
                    self.act(sa, pa[:, 0:n], AF.Silu)
                    self.tt(self.U[:, jj, t0:t1], sa, pb[:, 0:n], ALU.mult)
            for dj in range(KC):
                wt = self.wload(("fo", l, s, hf, dj))
                for (t0, t1) in TBS:
                    n = t1 - t0
                    ps = self.bank()
                    for k in range(22):
                        self.mm(ps[:, 0:n], wt[:, k, :], self.U[:, k, t0:t1], start=(k == 0), stop=(k == 21))
                    self.resid_add(dj, t0, t1, ps)


    def proj_fm(self, wt, c0, ncol, kc, src, ps, t0, t1):
        n = t1 - t0
        for k in range(kc):
            self.mm(ps[0:ncol, 0:n], wt[:, k, c0:c0 + ncol], src[:, k, t0:t1], start=(k == 0), stop=(k == kc - 1))

    def rope_evac(self, dst, ps1, ps2, np_, t0, t1):
        n = t1 - t0
        self.tt(self.RT[0][0:np_, 0:n], ps1[0:np_, 0:n], self.ROPE[0:np_, 0, t0:t1], ALU.mult)
        self.tt(self.RT[1][0:np_, 0:n], ps2[0:np_, 0:n], self.ROPE[0:np_, 1, t0:t1], ALU.mult)
        self.tt(dst, self.RT[0][0:np_, 0:n], self.RT[1][0:np_, 0:n], ALU.add)

    def mla_norm(self, l, tiles, nch, wcol0, dst):
        for (t0, t1) in TBS:
            n = t1 - t0
            pss = []
            for c in range(nch):
                ps = self.psum[c]
                wt = tiles[c // 2]
                self.proj_fm(wt, (c % 2) * 128, 128, KC, self.H, ps, t0, t1)
                pss.append(ps)
            pn = self.psum[7]
            for c in range(nch):
                sq = self.SQ[:, c % 2, 0:n]
                self.act(sq, pss[c][:, 0:n], AF.Square)
                self.mm(pn[:, 0:n], self.ONESB[:, :], sq, start=(c == 0), stop=(c == nch - 1))
            self.act(self.RS[:, t0:t1], pn[:, 0:n], AF.Sqrt, bias=self.EPS_AP[:, 0:1], scale=1.0 / (nch * 128))
            self.recip(self.RS[:, t0:t1], self.RS[:, t0:t1])
            for c in range(nch):
                self.stt(dst[:, c, t0:t1], pss[c][:, 0:n], self.VECS[:, wcol0 + c:wcol0 + c + 1],
                         self.RS[:, t0:t1], ALU.mult, ALU.mult)

    def k_phase(self, l):
        def kdst(br, h, nrows=128):
            ci, r0 = kvloc(br, h)
            return self.cinK[ci].ap()[r0:r0 + nrows, :]

        self.dma(self.ROPE[:, :, :].rearrange("p a t -> p (a t)"), self.rope_d[:, :], eng="pool")
        ki = [0]
        vi = [0]
        grp = [[0, 1, 2, 3], [4, 5, 6, 7]]
        nocc = DEBUG_STOP.endswith("_nocc")

        def fire(kind, i):
            ci, co = (self.cinK[i], self.coutK[i]) if kind == "k" else (self.cinV[i], self.coutV[i])
            self.P.add("pool", lambda e, ci=ci, co=co: e.collective_compute(
                "AllGather", ALU.bypass, replica_groups=grp,
                ins=[ci.ap().opt()], outs=[co.ap().opt()]),
                [ci.ap()], [co.ap()], kind="cc", ccg=l + 1)

        def trigger(kind, i):
            if nocc:
                return
            self.pending.append([3, lambda: fire(kind, i)])

        def kst():
            ki[0] += 1
            return self.KST[ki[0] % 2]

        def vproj(key, src, kc, chunk, col0):
            wt = self.wload(key)
            ncol = self.dirn[key][2]
            dst = self.cinV[chunk].ap()
            for tc in range(9):
                t0 = tc * 128
                nt = min(128, T - t0)
                ps = self.bank()
                for k in range(kc):
                    self.mm(ps[0:nt, 0:ncol], src[:, k, t0:t0 + nt], wt[:, k, :], start=(k == 0), stop=(k == kc - 1))
                vi[0] += 1
                vs = self.VST[vi[0] % 2]
                self.act(vs[0:nt, 0:ncol], ps[0:nt, 0:ncol], AF.Copy)
                self.dma(dst[t0:t0 + nt, col0:col0 + ncol], vs[0:nt, 0:ncol])

        for h in range(6):
            wt = self.wload(("nak", l, h))
            st = kst()
            for (t0, t1) in TBS:
                ps = self.bank()
                self.proj_fm(wt, 0, 128, KC, self.H, ps, t0, t1)
                self.act(st[:, t0:t1], ps[:, 0:t1 - t0], AF.Copy)
            self.dma(kdst("na", h), st[:, :])
            if h % 3 == 2:
                trigger("k", h // 3)
        for t, (a, n) in enumerate(VT_NA):
            vproj(("nav", l, t), self.H, KC, a // 384, a % 384)
            if t % 2 == 1:
                trigger("v", t // 2)
        tiles = [self.wload(("ckv", l, t)) for t in range(2)]
        self.mla_norm(l, tiles, 4, V_KVN + l * 4, self.CKVN)
        for h in range(5):
            wt = self.wload(("ukvk", l, h))
            st = kst()
            for (t0, t1) in TBS:
                ps = self.bank()
                self.proj_fm(wt, 0, 128, 4, self.CKVN, ps, t0, t1)
                self.act(st[:, t0:t1], ps[:, 0:t1 - t0], AF.Copy)
            self.dma(kdst("mla", h), st[:, :])
            if h == 2:
                trigger("k", 2)
        wt = self.wload(("kr", l))
        st = kst()
        for (t0, t1) in TBS:
            p1 = self.bank()
            p2 = self.bank()
            self.proj_fm(wt, 0, 64, KC, self.H, p1, t0, t1)
            self.proj_fm(wt, 64, 64, KC, self.H, p2, t0, t1)
            self.rope_evac(st[0:64, t0:t1], p1, p2, 64, t0, t1)
        self.dma(kdst("kr", 0, 64), st[0:64, :])
        trigger("k", 3)
        for t, (a, n) in enumerate(VT_M):
            vproj(("mv", l, t), self.CKVN, 4, 2 + a // 384, a % 384)
            trigger("v", 2 + t)
        for h in range(5):
            w0 = self.wload(("dk", l, h, 0))
            w1 = self.wload(("dk", l, h, 1))
            st = kst()
            for (t0, t1) in TBS:
                p1 = self.bank()
                p2 = self.bank()
                self.proj_fm(w0, 0, 128, KC, self.H, p1, t0, t1)
                self.proj_fm(w1, 0, 128, KC, self.H, p2, t0, t1)
                self.rope_evac(st[:, t0:t1], p1, p2, 128, t0, t1)
            self.dma(kdst("diff", h), st[:, :])
            if h == 2 or h == 4:
                trigger("k", 4 + h // 3)
        for t, (a, n) in enumerate(VT_D):
            vproj(("dv", l, t), self.H, KC, 4 + a // 384, a % 384)
            if t >= 1:
                trigger("v", 4 + t - 1)

    def q_phase(self, l):
        for h in range(6):
            wt = self.wload(("naq", l, h))
            for (t0, t1) in TBS:
                ps = self.bank()
                self.proj_fm(wt, 0, 128, KC, self.H, ps, t0, t1)
                self.act(self.QNA[:, h, t0:t1], ps[:, 0:t1 - t0], AF.Copy)
        tiles = [self.wload(("cq", l, t)) for t in range(3)]
        self.mla_norm(l, tiles, 6, V_QN + l * 6, self.CQN)
        for h in range(5):
            wn = self.wload(("uqn", l, h))
            wr = self.wload(("uqr", l, h))
            for (t0, t1) in TBS:
                ps = self.bank()
                self.proj_fm(wn, 0, 128, 6, self.CQN, ps, t0, t1)
                self.act(self.QMN[:, h, t0:t1], ps[:, 0:t1 - t0], AF.Copy)
                p1 = self.bank()
                p2 = self.bank()
                self.proj_fm(wr, 0, 64, 6, self.CQN, p1, t0, t1)
                self.proj_fm(wr, 64, 64, 6, self.CQN, p2, t0, t1)
                self.rope_evac(self.QMR[0:64, h, t0:t1], p1, p2, 64, t0, t1)
        for h in range(5):
            w0 = self.wload(("dq", l, h, 0))
            w1 = self.wload(("dq", l, h, 1))
            for (t0, t1) in TBS:
                p1 = self.bank()
                p2 = self.bank()
                self.proj_fm(w0, 0, 128, KC, self.H, p1, t0, t1)
                self.proj_fm(w1, 0, 128, KC, self.H, p2, t0, t1)
                self.rope_evac(self.QD[:, h, t0:t1], p1, p2, 128, t0, t1)

    def attend(self, nq, chunks, maps, v_fn, scale, obanks, lbanks):
        nm = len(maps)
        sb_i = [0]

        def scores(ci):
            ch = chunks[ci]
            nk = ch[0]
            out = []
            for m in range(nm):
                ps = self.psum[sb_i[0] % 4]
                sb_i[0] += 1
                pieces = maps[m]
                for pi, (kf, q) in enumerate(pieces):
                    self.mm(ps[0:nk, 0:nq], kf(ch), q, start=(pi == 0), stop=(pi == len(pieces) - 1))
                out.append(ps)
            return out

        pend = scores(0)
        pti = 0
        for ci in range(len(chunks)):
            cur = pend
            if ci + 1 < len(chunks):
                pend = scores(ci + 1)
            nk = chunks[ci][0]
            for m in range(nm):
                pt = self.PT[pti % 4]
                pti += 1
                self.act(pt[0:nk, 0:nq], cur[m][0:nk, 0:nq], AF.Exp, scale=float(scale))
                first = ci == 0
                last = ci == len(chunks) - 1
                self.mm(obanks[m][:, 0:nq], v_fn(chunks[ci]), pt[0:nk, 0:nq], start=first, stop=last)
                self.mm(lbanks[m][:, 0:nq], self.ONESB[0:nk, :], pt[0:nk, 0:nq], start=first, stop=last)

    def kview(self, br, h, nrows=128):
        ci, r0 = kvloc(br, h)
        return self.coutK[ci].ap().rearrange("(r k) t -> r k t", r=4)[:, r0:r0 + nrows, :]

    def vview(self, br, h):
        ci, c0 = kvloc(br, h)
        return self.coutV[ci].ap().rearrange("(r t) c -> r t c", r=4)[:, :, c0:c0 + 128]

    def mla_attn(self, l):
        KR = self.ring[0][0:64, :].rearrange("p (r t) -> p r t", r=4)
        ckr = self.kview("kr", 0, 64)
        self.dma(KR, ckr[:, :, 0:TL].rearrange("r d t -> d r t"))
        KRC = self.sb("KRC", [64, 4, 64], BF16, self.o_scr + 20992 - 512)
        self.dma(KRC[:, :, :], ckr[:, :, TL:T].rearrange("r d t -> d r t"))
        for h in range(5):
            KN = self.ring[1 if h % 2 == 0 else 3][:, :].rearrange("p (r t) -> p r t", r=4)
            VV = self.ring[2][:, :].rearrange("p (r n c) -> p r n c", r=4, n=8)
            KC_ = self.KCX[h % 2]
            VC_ = self.VCX[h % 2]
            ck = self.kview("mla", h)
            cv = self.vview("mla", h)
            self.dma(KN, ck[:, :, 0:TL].rearrange("r d t -> d r t"))
            self.dma(KC_[:, 0, :, :], ck[:, :, TL:T].rearrange("r d t -> d r t"))
            for r_ in range(4):
                self.dma(VV[:, r_, :, :], cv[r_, 0:TL, :].rearrange("(n p) c -> p n c", p=128))
            self.dma(VC_[:, :, :], cv[:, TL:T, :].rearrange("r p c -> p r c"))
            lat = [(128, "l", r, n) for r in range(4) for n in range(8)]
            ctxc = [(64, "c", r, 0) for r in range(4)]

            def kn(ch):
                return KN[:, ch[2], ch[3] * 128:(ch[3] + 1) * 128] if ch[1] == "l" else KC_[:, 0, ch[2], :]

            def kr(ch):
                return KR[:, ch[2], ch[3] * 128:(ch[3] + 1) * 128] if ch[1] == "l" else KRC[:, ch[2], :]

            def vf(ch):
                return VV[:, ch[2], ch[3], :] if ch[1] == "l" else VC_[:, ch[2], :]

            for (t0, t1) in QBS:
                nq = t1 - t0
                chunks = (lat + ctxc) if t0 < TL else ctxc
                maps = [[(kn, self.QMN[:, h, t0:t1]), (kr, self.QMR[0:64, h, t0:t1])]]
                self.attend(nq, chunks, maps, vf, MLA_SCALE, [self.psum[4]], [self.psum[5]])
                self.act(self.AO[0][:, 0:nq], self.psum[4][:, 0:nq], AF.Copy)
                self.act(self.RC[0][:, 0:nq], self.psum[5][:, 0:nq], AF.Copy)
                self.recip(self.RC[0][:, 0:nq], self.RC[0][:, 0:nq])
                self.tt(self.OT[:, 6 + h, t0:t1], self.AO[0][:, 0:nq], self.RC[0][:, 0:nq], ALU.mult)

    def diff_attn(self, l, lam_init):
        DL = self.sb("DL", [1, 512], F32, self.o_scr + 16384)
        LS = self.sb("LS", [1, 8], F32, self.o_scr + 16384 + 2048)
        self.dma(DL[:, :], self.dlam_d[:, :])
        b0 = l * 256
        self.tt(DL[0:1, b0:b0 + 64], DL[0:1, b0:b0 + 64], DL[0:1, b0 + 64:b0 + 128], ALU.mult)
        self.tt(DL[0:1, b0 + 128:b0 + 192], DL[0:1, b0 + 128:b0 + 192], DL[0:1, b0 + 192:b0 + 256], ALU.mult)
        self.P.add("dve", lambda e: e.reduce_sum(LS[0:1, 0:1], DL[0:1, b0:b0 + 64], mybir.AxisListType.X),
                   [DL[0:1, b0:b0 + 64]], [LS[0:1, 0:1]])
        self.P.add("dve", lambda e: e.reduce_sum(LS[0:1, 1:2], DL[0:1, b0 + 128:b0 + 192], mybir.AxisListType.X),
                   [DL[0:1, b0 + 128:b0 + 192]], [LS[0:1, 1:2]])
        self.act(LS[0:1, 2:4], LS[0:1, 0:2], AF.Exp)
        self.tt(LS[0:1, 4:5], LS[0:1, 3:4], LS[0:1, 2:3], ALU.subtract)
        self.ts(LS[0:1, 4:5], LS[0:1, 4:5], float(-lam_init), None, ALU.add)
        ps = self.psum[0]
        self.mm(ps[:, 0:1], self.ONESF[0:1, :], LS[0:1, 4:5], start=True, stop=True)
        self.vcopy(self.NEGLAM[:, 0:1], ps[:, 0:1])
        self.ts(self.SUBW[:, 0:1], self.VECS[:, V_SUB + l:V_SUB + l + 1], float(1.0 - lam_init), None, ALU.mult)
        for h in range(5):
            KD = self.ring[h % 2][:, :].rearrange("p (r t) -> p r t", r=4)
            VV = self.ring[2 + h % 2][:, :].rearrange("p (r n c) -> p r n c", r=4, n=8)
            KC_ = self.KCX[h % 2]
            VC_ = self.VCX[h % 2]
            ck = self.kview("diff", h)
            cv = self.vview("diff", h)
            self.dma(KD, ck[:, :, 0:TL].rearrange("r d t -> d r t"))
            self.dma(KC_[:, 0, :, :], ck[:, :, TL:T].rearrange("r d t -> d r t"))
            for r_ in range(4):
                self.dma(VV[:, r_, :, :], cv[r_, 0:TL, :].rearrange("(n p) c -> p n c", p=128))
            self.dma(VC_[:, :, :], cv[:, TL:T, :].rearrange("r p c -> p r c"))
            lat = [(128, "l", r, n) for r in range(4) for n in range(8)]
            ctxc = [(64, "c", r, 0) for r in range(4)]

            def k1(ch):
                return KD[0:64, ch[2], ch[3] * 128:(ch[3] + 1) * 128] if ch[1] == "l" else KC_[0:64, 0, ch[2], :]

            def k2(ch):
                return KD[64:128, ch[2], ch[3] * 128:(ch[3] + 1) * 128] if ch[1] == "l" else KC_[64:128, 0, ch[2], :]

            def vf(ch):
                return VV[:, ch[2], ch[3], :] if ch[1] == "l" else VC_[:, ch[2], :]

            for (t0, t1) in QBS:
                nq = t1 - t0
                chunks = (lat + ctxc) if t0 < TL else ctxc
                maps = [[(k1, self.QD[0:64, h, t0:t1])], [(k2, self.QD[64:128, h, t0:t1])]]
                O1, L1, O2, L2 = self.psum[4], self.psum[5], self.psum[6], self.psum[7]
                self.attend(nq, chunks, maps, vf, DIFF_SCALE, [O1, O2], [L1, L2])
                self.act(self.AO[0][:, 0:nq], O1[:, 0:nq], AF.Copy)
                self.act(self.AO[1][:, 0:nq], O2[:, 0:nq], AF.Copy)
                self.act(self.RC[0][:, 0:nq], L1[:, 0:nq], AF.Copy)
                self.act(self.RC[1][:, 0:nq], L2[:, 0:nq], AF.Copy)
                self.recip(self.RC[0][:, 0:nq], self.RC[0][:, 0:nq])
                self.recip(self.RC[1][:, 0:nq], self.RC[1][:, 0:nq])
                self.tt(self.AO[0][:, 0:nq], self.AO[0][:, 0:nq], self.RC[0][:, 0:nq], ALU.mult)
                self.tt(self.AO[1][:, 0:nq], self.AO[1][:, 0:nq], self.RC[1][:, 0:nq], ALU.mult)
                self.stt(self.AO[0][:, 0:nq], self.AO[1][:, 0:nq], self.NEGLAM[:, 0:1], self.AO[0][:, 0:nq],
                         ALU.mult, ALU.add)
                sq = self.PT[0][:, 0:nq]
                self.act(sq, self.AO[0][:, 0:nq], AF.Square)
                pn = self.psum[0]
                self.mm(pn[:, 0:nq], self.ONESB[:, :], sq, start=True, stop=True)
                self.act(self.RC[0][:, 0:nq], pn[:, 0:nq], AF.Sqrt, bias=self.EPS_AP[:, 0:1], scale=1.0 / 128)
                self.recip(self.RC[0][:, 0:nq], self.RC[0][:, 0:nq])
                self.stt(self.OT[:, 11 + h, t0:t1], self.AO[0][:, 0:nq], self.SUBW[:, 0:1], self.RC[0][:, 0:nq],
                         ALU.mult, ALU.mult)

    def na_attn(self, l):
        base = self.o_ring
        KW = self.sb("KW", [128, 1536], BF16, base)
        KCN = self.sb("KCN", [128, 4, 64], BF16, base + 3072)
        VW = self.sb("VW", [128, 12, 128], BF16, base + 3584)
        VCN = self.sb("VCN", [64, 4, 128], BF16, base + 6656)
        CKP = self.sb("CKP", [128, 4, 256], BF16, base + 8192)
        CKN = self.sb("CKN", [128, 4, 256], BF16, base + 8192 + 2048)
        CVP = self.sb("CVP", [128, 4, 2, 128], BF16, base + 8192 + 4096)
        CVN = self.sb("CVN", [128, 4, 2, 128], BF16, base + 8192 + 6144)
        NB = [self.sb("NB0", [128, 40 * 64], BF16, base + 16384), self.sb("NB1", [128, 38 * 64], BF16, base + 24576)]
        assert NA_OFF[8] == 40 and NA_OFF[16] - NA_OFF[8] == 38
        for h in range(6):
            ci_, o_ = kvloc("na", h)
            ck = self.kview("na", h)
            cv = self.vview("na", h)
            self.dma(KW[:, 256:1280], self.cinK[ci_].ap()[o_:o_ + 128, 0:TL])
            self.dma(CKP[:, :, :], ck[:, :, 768:1024].rearrange("r d t -> d r t"))
            self.dma(CKN[:, :, :], ck[:, :, 0:256].rearrange("r d t -> d r t"))
            self.dma(KCN[:, :, :], ck[:, :, TL:T].rearrange("r d t -> d r t"))
            self.dma(VW[:, 2:10, :], self.cinV[ci_].ap()[0:TL, o_:o_ + 128].rearrange("(n p) c -> p n c", p=128))
            for r_ in range(4):
                self.dma(CVP[:, r_, :, :], cv[r_, 768:1024, :].rearrange("(n p) c -> p n c", p=128))
                self.dma(CVN[:, r_, :, :], cv[r_, 0:256, :].rearrange("(n p) c -> p n c", p=128))
            self.dma(VCN[:, :, :], cv[:, TL:T, :].rearrange("r p c -> p r c"))
            nbsrc = self.nab_d[0 if FAKE else l * 6 + h]
            self.dma(NB[0][:, :], nbsrc[:, 0:40 * 64], eng="pool")
            self.dma(NB[1][:, :], nbsrc[:, 40 * 64:78 * 64], eng="pool")
            for (dst, cand, so) in ((KW[:, 0:256], CKP, 0), (KW[:, 1280:1536], CKN, 4)):
                self.ts(dst, cand[:, 0, :], self.SEL[:, so:so + 1], None, ALU.mult)
                for r in range(1, 4):
                    self.stt(dst, cand[:, r, :], self.SEL[:, so + r:so + r + 1], dst, ALU.mult, ALU.add)
            for (dst, cand, so) in ((VW[:, 0:2, :], CVP, 0), (VW[:, 10:12, :], CVN, 4)):
                self.ts(dst, cand[:, 0, :, :], self.SEL[:, so:so + 1], None, ALU.mult)
                for r in range(1, 4):
                    self.stt(dst, cand[:, r, :, :], self.SEL[:, so + r:so + r + 1], dst, ALU.mult, ALU.add)
            TB2 = [self.TBI, self.TBI2]

            def na_scores(r):
                q = self.QNA[:, h, r * 64:(r + 1) * 64]
                clo, chi = NA_CH[r]
                nch = chi - clo + 1
                ps = self.psum[(2 * r) % 4]
                pc = self.psum[(2 * r + 1) % 4]
                for ci in range(nch):
                    c = clo + ci
                    self.mm(ps[:, ci * 64:(ci + 1) * 64], KW[:, c * 128:(c + 1) * 128], q, start=True, stop=True)
                for rk in range(4):
                    self.mm(pc[0:64, rk * 64:(rk + 1) * 64], KCN[:, rk, :], q, start=True, stop=True)
                return ps, pc

            def na_rest(r, ps, pc):
                r8, rr = r // 8, r % 8
                O, L = self.psum[4 + 2 * r8], self.psum[5 + 2 * r8]
                clo, chi = NA_CH[r]
                nch = chi - clo + 1
                boff = (NA_OFF[r] - NA_OFF[r8 * 8]) * 64
                tb = TB2[r % 2][:, 0:nch * 64]
                self.stt(tb, ps[:, 0:nch * 64], float(NA_SCALE), NB[r8][:, boff:boff + nch * 64], ALU.mult, ALU.add)
                pt = self.PT[r % 2]
                ptc = self.PT[2 + r % 2]
                self.act(pt[:, 0:nch * 64], tb, AF.Exp)
                self.act(ptc[0:64, 0:256], pc[0:64, 0:256], AF.Exp, scale=float(NA_SCALE))
                oc = slice(rr * 64, (rr + 1) * 64)
                for ci in range(nch):
                    c = clo + ci
                    self.mm(O[:, oc], VW[:, c, :], pt[:, ci * 64:(ci + 1) * 64], start=(ci == 0), stop=False)
                    self.mm(L[:, oc], self.ONESB[:, :], pt[:, ci * 64:(ci + 1) * 64], start=(ci == 0), stop=False)
                for rk in range(4):
                    self.mm(O[:, oc], VCN[:, rk, :], ptc[0:64, rk * 64:(rk + 1) * 64], start=False, stop=(rk == 3))
                    self.mm(L[:, oc], self.ONESB[0:64, :], ptc[0:64, rk * 64:(rk + 1) * 64], start=False, stop=(rk == 3))
                if rr == 7:
                    self.recip(self.RC[r8][:, :], L[:, :])
                    self.tt(self.OT[:, h, r8 * 512:(r8 + 1) * 512], O[:, :], self.RC[r8][:, :], ALU.mult)

            pend = na_scores(0)
            for r in range(16):
                cur = pend
                if r + 1 < 16:
                    pend = na_scores(r + 1)
                na_rest(r, *cur)
            O, L = self.psum[4], self.psum[5]
            q = self.QNA[:, h, TL:T]
            pc = self.psum[0]
            for rk in range(4):
                self.mm(pc[0:64, rk * 64:(rk + 1) * 64], KCN[:, rk, :], q, start=True, stop=True)
            ptc = self.PT[2]
            self.act(ptc[0:64, 0:256], pc[0:64, 0:256], AF.Exp, scale=float(NA_SCALE))
            for rk in range(4):
                self.mm(O[:, 0:64], VCN[:, rk, :], ptc[0:64, rk * 64:(rk + 1) * 64], start=(rk == 0), stop=(rk == 3))
                self.mm(L[:, 0:64], self.ONESB[0:64, :], ptc[0:64, rk * 64:(rk + 1) * 64], start=(rk == 0), stop=(rk == 3))
            self.recip(self.RC[0][:, 0:64], L[:, 0:64])
            self.tt(self.OT[:, h, TL:T], O[:, 0:64], self.RC[0][:, 0:64], ALU.mult)

    def merge(self, l):
        self.rms_stats()
        self.modulate(l, 1, 3, 4, self.H2)
        self.gate_vec(5, 1.0)
        for q4 in range(4):
            for dq in range(4):
                dj = q4 * 4 + dq
                wg = [self.wload(("g", l, dj, br)) for br in range(3)]
                wb = self.wload(("wb", l, dj))
                for (t0, t1) in TBS:
                    n = t1 - t0
                    pg = [self.psum[i] for i in range(3)]
                    pb = [self.psum[3 + i] for i in range(3)]
                    for br, (ka, kb) in enumerate(((0, 6), (6, 11), (11, 16))):
                        self.proj_fm(wg[br], 0, 128, KC, self.H2, pg[br], t0, t1)
                        for k in range(ka, kb):
                            self.mm(pb[br][:, 0:n], wb[:, k, :], self.OT[:, k, t0:t1], start=(k == ka), stop=(k == kb - 1))
                        self.act(self.SG[br][:, 0:n], pg[br][:, 0:n], AF.Sigmoid)
                        if br == 0:
                            self.tt(self.MT[0][:, 0:n], self.SG[0][:, 0:n], pb[0][:, 0:n], ALU.mult)
                        elif br == 1:
                            self.tt(self.MT[1][:, 0:n], self.SG[1][:, 0:n], pb[1][:, 0:n], ALU.mult)
                            self.tt(self.MT[0][:, 0:n], self.MT[0][:, 0:n], self.MT[1][:, 0:n], ALU.add)
                        else:
                            self.tt(self.MT[1][:, 0:n], self.SG[2][:, 0:n], pb[2][:, 0:n], ALU.mult)
                            self.tt(self.YQ[:, dq, t0:t1], self.MT[0][:, 0:n], self.MT[1][:, 0:n], ALU.add)
            for t in range(8):
                wt = self.wload(("wo", l, q4 // 2, t))
                ko = (q4 % 2) * 4
                for dd in range(2):
                    dj2 = t * 2 + dd
                    for (t0, t1) in TBS:
                        n = t1 - t0
                        ps = self.psum[6 + (dd % 2)]
                        for k in range(4):
                            self.mm(ps[:, 0:n], wt[:, ko + k, dd * 128:(dd + 1) * 128], self.YQ[:, k, t0:t1],
                                    start=(k == 0), stop=(k == 3))
                        self.resid_add(dj2, t0, t1, ps)

    def mixer(self, l):
        lam_init = 0.8 - 0.6 * math.exp(-0.3 * l)
        self.rms_stats()
        self.modulate(l, 1, 3, 4, self.H)
        self.k_phase(l)
        if DEBUG_STOP.startswith("raw_kphase"):
            self.tick(flush=True)
        self.chk("kphase")
        self.q_phase(l)
        self.tick(flush=True)
        self.chk("qphase")
        self.na_attn(l)
        self.chk("na")
        self.mla_attn(l)
        self.chk("mla")
        self.diff_attn(l, lam_init)
        self.chk("diff")
        self.merge(l)

    def chk(self, name):
        if DEBUG_STOP in ("raw_" + name, "raw_" + name + "_nocc"):
            raise StopBuild()

    def dump(self, src_f32_ap):
        self.dma(self.dbg, src_f32_ap)

    def finish(self, out_ops):
        op = self.P.add("sp", None, [], [], kind="c")
        op.deps = list(out_ops) + [o for o in self.P.ops if o.kind == "cc"]

    def final(self):
        self.rms_stats()
        g = V_FN
        IDF = self.sb("IDF", [128, 128], F32, self.o_ring)
        OST = [self.sb(f"OST{i}", [128, D], F32, self.o_ring + 8192 * (1 + i)) for i in range(2)]
        YF = [self.sb(f"YF{i}", [128, 128], F32, self.o_ring + 512 + 512 * i) for i in range(4)]
        self.dma(IDF[:, :], self.ident_d[:, :])
        outs = []
        for tc in range(8):
            ost = OST[tc % 2]
            for j4 in range(4):
                ps = self.bank()
                for jj in range(4):
                    j = j4 * 4 + jj
                    yf = YF[j % 4]
                    self.stt(yf[:, :], self.X[:, j, tc * 128:(tc + 1) * 128], self.VECS[:, g + j:g + j + 1],
                             self.RS[:, tc * 128:(tc + 1) * 128], ALU.mult, ALU.mult)
                    self.P.add("pe", lambda e, o=ps[:, jj * 128:(jj + 1) * 128], i=yf[:, :]: e.transpose(o, i, IDF[:, :]),
                               [yf[:, :], IDF[:, :]], [ps[:, jj * 128:(jj + 1) * 128]])
                self.act(ost[:, j4 * 512:(j4 + 1) * 512], ps[:, :], AF.Copy)
            outs.append(self.dma(self.out[tc * 128:(tc + 1) * 128, :], ost[:, :]))
        return outs

    def build(self):
        nc = self.nc
        self.memset(self.EPS_AP[:, :], EPS)
        self.setup()
        outs = []
        stop = DEBUG_STOP
        done = False
        for l in range(2):
            self.load_mods(l)
            if stop == "mods" and l == 0:
                outs.append(self.dump_small(self.MODS[:, :, :].rearrange("p a b -> p (a b)"), 288))
                done = True
                break
            self.ffn(l, 0, 0, 0)
            if stop == "ffn1" and l == 0:
                outs.append(self.dma(self.dbg, self.X[:, :, :].rearrange("p j t -> p (j t)")))
                done = True
                break
            try:
                self.mixer(l)
            except StopBuild:
                src = self.OT if stop in ("raw_na", "raw_mla", "raw_diff") else self.H
                for j in range(KC):
                    self.vcopy(self.X[:, j, :], src[:, j, :])
                outs.append(self.dma(self.dbg, self.X[:, :, :].rearrange("p j t -> p (j t)")))
                done = True
                break
            if stop == "mixer" and l == 0:
                outs.append(self.dma(self.dbg, self.X[:, :, :].rearrange("p j t -> p (j t)")))
                done = True
                break
            self.ffn(l, 1, 2, 6)
            if stop == "layer0" and l == 0:
                outs.append(self.dma(self.dbg, self.X[:, :, :].rearrange("p j t -> p (j t)")))
                done = True
                break
        if not done:
            outs += self.final()
        self.finish(outs)
        with ExitStack() as stack:
            self.P.finalize(stack)
        return nc

    def dump_small(self, ap, n):
        return self.dma(self.dbg[:, 0:n], ap)


_CACHE = {}


def prep_inputs(inputs):
    inp = {k: np.asarray(v) for k, v in inputs.items()}
    wbig, dirn = build_wbig(inp)
    vecs = build_vecs(inp)
    dlam = np.ascontiguousarray(inp["diff_lambda"].reshape(1, 512).astype(np.float32))
    ident = np.eye(128, dtype=np.float32)
    maps = []
    for core in range(8):
        b, rq = core // 4, core % 4
        xT = np.concatenate([inp["x"][b, rq * TL:(rq + 1) * TL, :].T, inp["ctx"][b, rq * TCX:(rq + 1) * TCX, :].T], axis=1)
        cv = np.stack([inp["c"][b], inp["c_ctx"]], axis=-1)
        cT = cv.reshape(KC, 128, 2).transpose(1, 0, 2).reshape(128, KC * 2)
        adaw = np.empty((72, 128, 2048), np.float32)
        adab = np.empty((1, 72 * 128), np.float32)
        for l in range(2):
            for jj in range(36):
                c0 = (36 * rq + jj) * 128
                W = inp["w_ada"][l][:, c0:c0 + 128]
                adaw[l * 36 + jj] = W.reshape(KC, 128, 128).transpose(1, 0, 2).reshape(128, 2048)
                adab[0, (l * 36 + jj) * 128:(l * 36 + jj + 1) * 128] = inp["b_ada"][l, c0:c0 + 128]
        sel = np.zeros((128, 16), np.float32)
        sel[:, 8 + b] = 1.0
        if rq > 0:
            sel[:, rq - 1] = 1.0
        if rq < 3:
            sel[:, 4 + rq + 1] = 1.0
        nab = na_bias(inp["na_rpb"], rq).reshape(12, 128, NA_OFF[-1] * 64)
        maps.append({
            "xT": np.ascontiguousarray(xT.astype(np.float32)),
            "cT": np.ascontiguousarray(cT.astype(np.float32)),
            "adaw": adaw, "adab": adab, "wbig": wbig, "vecs": vecs, "dlam": dlam,
            "rope": rope_table(rq).reshape(128, 2 * T), "nab": np.ascontiguousarray(nab),
            "sel": sel, "ident": ident,
        })
    return maps, dirn, wbig.shape[1]


def kernel(**inputs):
    maps, dirn, wtot = prep_inputs(inputs)
    b = Builder(dirn, wtot)
    nc = b.build()
    res = run_bass_kernel_spmd(nc, maps, core_ids=list(range(8)))
    kernel.last = res
    out = np.empty((2, 4096, D), np.float32)
    for core in range(8):
        bb, rq = core // 4, core % 4
        out[bb, rq * TL:(rq + 1) * TL, :] = res.results[core]["out"]
    return out
```

```python
import math
import os
from contextlib import ExitStack

import numpy as np
import concourse.bass as bass
import concourse.mybir as mybir
from concourse.bass_utils import run_bass_kernel_spmd

F32 = mybir.dt.float32
BF16 = mybir.dt.bfloat16
AF = mybir.ActivationFunctionType
ALU = mybir.AluOpType

D = 2048
KC = 16
TL = 1024
TCX = 64
T = TL + TCX
FFN = 5632
NEG = -30000.0
EPS = 1e-6
NA_SCALE = 128 ** -0.5
MLA_SCALE = 192 ** -0.5
DIFF_SCALE = 64 ** -0.5
TBS = [(0, 384), (384, 768), (768, 1088)]
QBS = [(0, 512), (512, 1024), (1024, 1088)]
TBS_LAT = [(0, 384), (384, 768), (768, 1024)]
O_NAQ, O_NAK, O_NAV, O_CQ, O_CKV, O_KR, O_DQ, O_DK, O_DV, O_GA, O_GB, O_GD = (
    0, 768, 1536, 2304, 3072, 3584, 3648, 4288, 4928, 5568, 7616, 9664)
KROWS = 16 * 128 + 64
DEBUG_STOP = os.environ.get("MK_STOP", "")
FAKE = bool(os.environ.get("MK_FAKE"))


VT_NA = [(0, 256), (256, 128), (384, 256), (640, 128)]
VT_D = [(0, 256), (256, 128), (384, 256)]
VT_M = [(0, 384), (384, 256)]
KCH_ROWS = [384, 384, 384, 320, 384, 256]
VCH_COLS = [384, 384, 384, 256, 384, 256]


def kvloc(br, h):
    if br == "na":
        return h // 3, (h % 3) * 128
    if br == "mla":
        return 2 + h // 3, (h % 3) * 128
    if br == "kr":
        return 3, 256
    return 4 + h // 3, (h % 3) * 128


def na_chunks(r):
    lo = min(r - 4, 8)
    hi = max(r + 4, max(r - 4, 0) + 8, min(r - 4, 8) + 8)
    return (lo + 4) // 2, (hi - 1 + 4) // 2


NA_CH = [na_chunks(r) for r in range(16)]
NA_OFF = np.cumsum([0] + [b - a + 1 for a, b in NA_CH]).tolist()


def partner64(d):
    return d + 16 if (d % 32) < 16 else d - 16


def wtile(W, cols):
    K = W.shape[0]
    kc = K // 128
    a = W[:, cols].reshape(kc, 128, len(cols)).transpose(1, 0, 2).reshape(128, kc * len(cols))
    return a, kc, len(cols)


def build_wbig(inp):
    tiles = []
    dirn = {}
    off = [0]

    def put(key, W, cols):
        a, kc, n = wtile(W, np.asarray(cols))
        dirn[key] = (off[0], kc, n)
        off[0] += a.shape[1]
        tiles.append(a)

    ar = np.arange
    for l in range(2):
        w_in = inp["w_in"][l]
        uq = inp["mla_w_uq"][l]
        ukv = inp["mla_w_ukv"][l]
        p64 = np.array([partner64(d) for d in range(64)])
        p128 = np.concatenate([p64, 64 + p64])
        for s in range(2):
            fwi = inp["ffn_w_in"][l, s]
            fwo = inp["ffn_w_out"][l, s]
            for hf in range(2):
                for jj in range(22):
                    j = hf * 22 + jj
                    put(("fi", l, s, j), fwi, np.concatenate([ar(j * 128, j * 128 + 128), FFN + ar(j * 128, j * 128 + 128)]))
                for dj in range(16):
                    put(("fo", l, s, hf, dj), fwo[hf * 2816:(hf + 1) * 2816], ar(dj * 128, dj * 128 + 128))
        for h in range(6):
            put(("nak", l, h), w_in, O_NAK + h * 128 + ar(128))
        for t in range(2):
            put(("ckv", l, t), w_in, O_CKV + t * 256 + ar(256))
        for h in range(5):
            put(("ukvk", l, h), ukv, h * 256 + ar(128))
        put(("kr", l), w_in, np.concatenate([O_KR + ar(64), O_KR + p64]))
        for h in range(5):
            put(("dk", l, h, 0), w_in, O_DK + h * 128 + ar(128))
            put(("dk", l, h, 1), w_in, O_DK + h * 128 + p128)
        for t, (a, n) in enumerate(VT_NA):
            put(("nav", l, t), w_in, O_NAV + a + ar(n))
        for t, (a, n) in enumerate(VT_D):
            put(("dv", l, t), w_in, O_DV + a + ar(n))
        vcols = np.concatenate([h * 256 + 128 + ar(128) for h in range(5)])
        for t, (a, n) in enumerate(VT_M):
            put(("mv", l, t), ukv, vcols[a:a + n])
        for h in range(6):
            put(("naq", l, h), w_in, O_NAQ + h * 128 + ar(128))
        for t in range(3):
            put(("cq", l, t), w_in, O_CQ + t * 256 + ar(256))
        for h in range(5):
            put(("uqn", l, h), uq, h * 192 + ar(128))
            put(("uqr", l, h), uq, np.concatenate([h * 192 + 128 + ar(64), h * 192 + 128 + p64]))
        for h in range(5):
            put(("dq", l, h, 0), w_in, O_DQ + h * 128 + ar(128))
            put(("dq", l, h, 1), w_in, O_DQ + h * 128 + p128)
        wb = inp["w_branch"][l]
        wo = inp["w_out"][l]
        for dj in range(16):
            for br, o in enumerate((O_GA, O_GB, O_GD)):
                put(("g", l, dj, br), w_in, o + dj * 128 + ar(128))
            put(("wb", l, dj), wb, dj * 128 + ar(128))
        for half in range(2):
            for t in range(8):
                put(("wo", l, half, t), wo[half * 1024:(half + 1) * 1024], t * 256 + ar(256))
    return np.ascontiguousarray(np.concatenate(tiles, axis=1)), dirn


def fm(v):
    return np.ascontiguousarray(v.reshape(-1, 128).T)


def build_vecs(inp):
    cols = []
    for l in range(2):
        for n in range(3):
            cols.append(fm(inp["norm_w"][l, n]))
    cols.append(fm(inp["final_norm"]))
    for l in range(2):
        cols.append(fm(inp["mla_q_norm"][l]))
    for l in range(2):
        cols.append(fm(inp["mla_kv_norm"][l]))
    for l in range(2):
        cols.append(fm(inp["diff_subln"][l]))
    return np.ascontiguousarray(np.concatenate(cols, axis=1).astype(np.float32))


V_NW, V_FN, V_QN, V_KVN, V_SUB = 0, 96, 112, 124, 132
NV = 134


def rope_table(rq):
    theta = 10000.0
    quarter = 16
    freqs = (np.float32(theta) ** (-np.arange(quarter, dtype=np.float32) / np.float32(quarter))).astype(np.float32)
    g = rq * TL + np.arange(TL)
    rows = (g // 64).astype(np.float32)
    colsp = (g % 64).astype(np.float32)
    tab = np.zeros((64, 2, T), np.float32)
    tab[:, 0, TL:] = 1.0
    for d in range(64):
        pos = rows if d < 32 else colsp
        ang = (pos * freqs[d % 16]).astype(np.float32)
        tab[d, 0, :TL] = np.cos(ang)
        sn = np.sin(ang)
        tab[d, 1, :TL] = -sn if (d % 32) < 16 else sn
    return np.ascontiguousarray(np.concatenate([tab, tab], axis=0))


def na_bias(rpb, rq):
    out = np.full((2, 6, 128, NA_OFF[-1], 64), NEG, np.float32)
    qc = np.arange(64)
    kc = np.arange(64)
    c0 = np.clip(qc - 8, 0, 48)
    col_ok = (kc[:, None] >= c0[None, :]) & (kc[:, None] < c0[None, :] + 16)
    col_off = np.clip(kc[:, None] - qc[None, :], -15, 15) + 15
    for r in range(16):
        gr = 16 * rq + r
        r0 = min(max(gr - 4, 0), 56)
        clo, chi = NA_CH[r]
        for ci, c in enumerate(range(clo, chi + 1)):
            for kk in range(2):
                kr = 16 * rq + (-4 + 2 * c + kk)
                if kr < 0 or kr >= 64 or kr < r0 or kr >= r0 + 8:
                    continue
                ro = kr - gr + 7
                vals = rpb[:, :, ro, :][:, :, col_off]
                blk = out[:, :, kk * 64:(kk + 1) * 64, NA_OFF[r] + ci, :]
                blk[...] = np.where(col_ok[None, None], vals, NEG)
    return out


class Op:
    __slots__ = ("eng", "fn", "deps", "pos", "awaited", "kind", "idx", "waits", "sem", "semval", "ccg")

    def __init__(self, eng, fn, kind):
        self.eng = eng
        self.fn = fn
        self.kind = kind
        self.deps = []
        self.awaited = False
        self.waits = []
        self.sem = None
        self.semval = 0


ENGS = ("pe", "act", "dve", "pool", "sp")
NDSEM = 16
SEM_CH = 4000


class Prog:
    def __init__(self, nc):
        self.nc = nc
        self.ops = []
        self.eng_ops = {e: [] for e in ENGS}
        self.recs = {}
        self.ndma = 0
        self.dma_ops = []
        self.skip = set()

    @staticmethod
    def _dsz(dt):
        return mybir.dt.size(dt)

    def region(self, ap):
        t = ap.tensor
        name = t.name
        if name in self.skip:
            return None
        pat = ap.ap
        rng = t.manual_sbuf_range
        if rng is not None:
            ps = 1
            for s in t.shape[1:]:
                ps *= int(s)
            lo = int(ap.offset) % ps
            ext = 1
            for (st, cnt) in pat[1:]:
                ext += (int(cnt) - 1) * abs(int(st))
            esz = self._dsz(t.dtype)
            return ("sb", rng[0] + lo * esz, rng[0] + (lo + ext) * esz)
        tn = type(t).__name__
        if tn.startswith("PSum") or tn.startswith("SB"):
            ps = 1
            for s in t.shape[1:]:
                ps *= int(s)
            lo = int(ap.offset) % ps
            ext = 1
            for (st, cnt) in pat[1:]:
                ext += (int(cnt) - 1) * abs(int(st))
            return (name, lo, lo + ext)
        lo = int(ap.offset)
        ext = 1
        for (st, cnt) in pat:
            ext += (int(cnt) - 1) * abs(int(st))
        return (name, lo, lo + ext)

    def _pages(self, sp, lo, hi):
        pg = 2048 if sp == "sb" else 65536
        return range(lo // pg, (hi - 1) // pg + 1)

    def add(self, eng, fn, reads=(), writes=(), kind="c", ccg=0):
        op = Op(eng, fn, kind)
        op.ccg = ccg
        op.idx = len(self.ops)
        deps = {}
        rkey = (eng + kind) if kind == "c" else ("d", op.idx)
        for ap in reads:
            rg = self.region(ap)
            if rg is None:
                continue
            sp, lo, hi = rg
            pages = self.recs.setdefault(sp, {})
            found = None
            for pgi in self._pages(sp, lo, hi):
                for rec in pages.get(pgi, ()):
                    if rec[0] < hi and lo < rec[1]:
                        w = rec[2]
                        if w is not None:
                            deps[w.idx] = (w, "raw")
                        if rec[0] == lo and rec[1] == hi:
                            found = rec
            if found is None:
                found = [lo, hi, None, {}]
                for pgi in self._pages(sp, lo, hi):
                    pages.setdefault(pgi, []).append(found)
            found[3][rkey] = op
        for ap in writes:
            rg = self.region(ap)
            if rg is None:
                continue
            sp, lo, hi = rg
            pages = self.recs.setdefault(sp, {})
            newrec = [lo, hi, op, {}]
            for pgi in self._pages(sp, lo, hi):
                lst = pages.get(pgi)
                if lst is None:
                    pages[pgi] = [newrec]
                    continue
                keep = []
                for rec in lst:
                    if rec[0] < hi and lo < rec[1]:
                        w = rec[2]
                        if w is not None and w.idx not in deps:
                            deps[w.idx] = (w, "waw")
                        for rd in rec[3].values():
                            if rd is not op and rd.idx not in deps:
                                deps[rd.idx] = (rd, "war")
                        if lo <= rec[0] and rec[1] <= hi:
                            continue
                    keep.append(rec)
                keep.append(newrec)
                pages[pgi] = keep
        if kind != "c":
            if kind == "d":
                op.sem = ("d", self.ndma % NDSEM)
                op.semval = 16 * (self.ndma // NDSEM + 1)
                if self.ndma >= NDSEM:
                    prev = self.dma_ops[self.ndma - NDSEM]
                    deps.setdefault(prev.idx, (prev, "raw"))
                self.dma_ops.append(op)
                self.ndma += 1
        for (d, typ) in deps.values():
            if d.kind == "c" and d.eng == eng:
                if eng == "pe":
                    continue
                if typ != "raw" and kind == "c":
                    continue
            op.deps.append(d)
        op.pos = len(self.eng_ops[eng])
        self.eng_ops[eng].append(op)
        self.ops.append(op)
        return op

    def finalize(self, stack):
        nc = self.nc
        known = {e: {f: -1 for f in ENGS} for e in ENGS}
        knownd = {e: set() for e in ENGS}
        for op in self.ops:
            e = op.eng
            for d in op.deps:
                if d.kind == "c":
                    if d.pos <= known[e][d.eng]:
                        continue
                    known[e][d.eng] = d.pos
                    d.awaited = True
                    op.waits.append(d)
                else:
                    if d.idx in knownd[e]:
                        continue
                    knownd[e].add(d.idx)
                    op.waits.append(d)
        esems = {}
        for e in ENGS:
            cnt = 0
            for op in self.eng_ops[e]:
                if op.kind == "c" and op.awaited:
                    op.sem = (e, cnt // SEM_CH)
                    op.semval = cnt % SEM_CH + 1
                    cnt += 1
            esems[e] = [stack.enter_context(nc.semaphore(f"s_{e}_{i}")) for i in range(cnt // SEM_CH + 1)]
        dsems = [stack.enter_context(nc.semaphore(f"s_dma_{i}")) for i in range(NDSEM)]
        ccsems = [stack.enter_context(nc.semaphore(f"s_cc_{i}")) for i in range(2)]
        ccops = [op for op in self.ops if op.kind == "cc"]
        for op in ccops:
            if op.ccg == 0:
                op.sem = ("cc", 0)
                op.semval = sum(1 for o in ccops if o.ccg == 0)
            else:
                op.sem = ("cc", 1)
                op.semval = sum(1 for o in ccops if 0 < o.ccg <= op.ccg)

        def semof(op):
            kind, i = op.sem
            if kind == "d":
                return dsems[i]
            if kind == "cc":
                return ccsems[i]
            return esems[kind][i]

        def emit(ename, eng):
            for op in self.eng_ops[ename]:
                for d in op.waits:
                    eng.wait_ge(semof(d), d.semval)
                if op.fn is None:
                    continue
                inst = op.fn(eng)
                if op.kind == "d":
                    inst.then_inc(semof(op), 16)
                elif op.kind == "cc":
                    inst.then_inc(semof(op))
                elif op.awaited:
                    inst.then_inc(semof(op), 1)

        block = stack.enter_context(nc.Block())

        @block.tensor
        def _(eng):
            emit("pe", eng)

        @block.scalar
        def _(eng):
            emit("act", eng)

        @block.vector
        def _(eng):
            emit("dve", eng)

        @block.gpsimd
        def _(eng):
            emit("pool", eng)

        @block.sync
        def _(eng):
            emit("sp", eng)


class StopBuild(Exception):
    pass


class Builder:
    def __init__(self, dirn, wtot):
        self.dirn = dirn
        self.wtot = wtot
        self.nc = nc = bass.Bass("TRN2", target_bir_lowering=False)
        self.P = Prog(nc)
        P = self.P
        dt = nc.dram_tensor
        self.xT = dt("xT", [D, T], F32, kind="ExternalInput").ap()
        self.cT = dt("cT", [128, KC * 2], F32, kind="ExternalInput").ap()
        if FAKE:
            wtot = self.wtot = 8192
        self.adaw = dt("adaw", [1 if FAKE else 72, 128, 2048], F32, kind="ExternalInput").ap()
        self.adab = dt("adab", [1, 72 * 128], F32, kind="ExternalInput").ap()
        self.wbig = dt("wbig", [128, wtot], F32, kind="ExternalInput").ap()
        self.vecs_d = dt("vecs", [128, NV], F32, kind="ExternalInput").ap()
        self.dlam_d = dt("dlam", [1, 512], F32, kind="ExternalInput").ap()
        self.rope_d = dt("rope", [128, 2 * T], F32, kind="ExternalInput").ap()
        self.nab_d = dt("nab", [1 if FAKE else 12, 128, NA_OFF[-1] * 64], F32, kind="ExternalInput").ap()
        self.sel_d = dt("sel", [128, 16], F32, kind="ExternalInput").ap()
        self.ident_d = dt("ident", [128, 128], F32, kind="ExternalInput").ap()
        for n in ("xT", "cT", "adaw", "adab", "wbig", "vecs", "dlam", "rope", "nab", "sel", "ident"):
            P.skip.add(n)
        self.out = dt("out", [TL, D], F32, kind="ExternalOutput").ap()
        self.dbg = None
        if DEBUG_STOP:
            self.dbg = dt("dbg", [128, KC * T], F32, kind="ExternalOutput").ap()
        self.cinK = [dt(f"cinK{i}", [n, T], BF16) for i, n in enumerate(KCH_ROWS)]
        self.coutK = [dt(f"coutK{i}", [4 * n, T], BF16) for i, n in enumerate(KCH_ROWS)]
        self.cinV = [dt(f"cinV{i}", [T, n], BF16) for i, n in enumerate(VCH_COLS)]
        self.coutV = [dt(f"coutV{i}", [4 * T, n], BF16) for i, n in enumerate(VCH_COLS)]
        self.cinA = dt("cinA", [128, 144], F32)
        self.coutA = dt("coutA", [4 * 128, 144], F32)
        arena = nc.alloc_sbuf_tensor("arena", [128, 212736], mybir.dt.uint8)
        self.abase = int(nc.lookup_mloc(arena).addr)
        self.ncnt = 0
        self.psum = [nc.alloc_psum_tensor(f"ps{i}", [128, 512], F32) for i in range(8)]
        self.OX = 0
        self.OB = 69632
        self.OC = self.OB + 34816
        self.OD = self.OC + 47872
        self.X = self.sb("X", [128, KC, T], F32, self.OX)
        self.H = self.sb("H", [128, KC, T], BF16, self.OB)
        self.OT = self.sb("OT", [128, KC, T], BF16, self.OB)
        self.U = self.sb("U", [128, 22, T], BF16, self.OC)
        self.H2 = self.sb("H2", [128, KC, T], BF16, self.OC)
        self.QNA = self.sb("QNA", [128, 6, T], BF16, self.OC)
        self.QMN = self.sb("QMN", [128, 5, T], BF16, self.OC + 13056)
        self.QMR = self.sb("QMR", [128, 5, T], BF16, self.OC + 13056 + 10880)
        self.QD = self.sb("QD", [128, 5, T], BF16, self.OC + 13056 + 21760)
        self.CQN = self.sb("CQN", [128, 6, T], BF16, self.OC + 13056 + 21760)
        o = self.OD
        self.MALL = self.sb("MALL", [128, 4, 144], F32, o); o += 3456
        self.MODS = self.sb("MODS", [128, 2, 144], F32, o); o += 1152
        self.VECS = self.sb("VECS", [128, NV], F32, o); o += 544
        self.AV = self.sb("AV", [128, 2, 16], F32, o); o += 128
        self.BV = self.sb("BV", [128, 2, 16], F32, o); o += 128
        self.GV = self.sb("GV", [128, 2, 16], F32, o); o += 128
        self.SEL = self.sb("SEL", [128, 16], F32, o); o += 64
        self.NEGLAM = self.sb("NEGLAM", [128, 4], F32, o); o += 32
        self.SUBW = self.sb("SUBW", [128, 2], F32, o); o += 32
        self.EPS_AP = self.sb("EPSC", [128, 1], F32, o); o += 32
        self.ONESB = self.sb("ONESB", [128, 128], BF16, o); o += 256
        self.ONESF = self.sb("ONESF", [128, 128], F32, o); o += 512
        self.o_scr = o
        self.RS = self.sb("RS", [128, T], F32, o); o += 4352
        self.TMPF = self.sb("TMPF", [128, T], F32, o); o += 4352
        self.SQ = self.sb("SQ", [128, 2, 512], BF16, o); o += 2048
        self.SA = self.sb("SA", [128, 2, 512], F32, o); o += 4096
        o_x = o
        o += 2048 + 4096
        self.ROPE = self.sb("ROPE", [128, 2, T], BF16, self.o_scr + 4352)
        self.KST = [self.sb(f"KST{i}", [128, T], BF16, self.o_scr + 10752 + i * 2176) for i in range(2)]
        self.VST = [self.sb(f"VST{i}", [128, 384], BF16, self.o_scr + 10752 + 4352 + i * 768) for i in range(2)]
        self.RT = [self.sb(f"RT{i}", [128, 512], F32, o_x + 2048 + i * 2048) for i in range(2)]
        self.CKVN = self.sb("CKVN", [128, 4, T], BF16, self.OC + 13056 + 21760)
        a = self.o_scr
        self.PT = [self.sb(f"PT{i}", [128, 512], BF16, a + i * 1024) for i in range(4)]; a += 4096
        self.RC = [self.sb(f"RC{i}", [128, 512], F32, a + i * 2048) for i in range(2)]; a += 4096
        self.AO = [self.sb(f"AO{i}", [128, 512], F32, a + i * 2048) for i in range(2)]; a += 4096
        self.TBI = self.sb("TBI", [128, 384], F32, a); a += 1536
        self.TBI2 = self.sb("TBI2", [128, 384], F32, self.o_scr + 17920)
        self.KCX = [self.sb(f"KCX{i}", [128, 2, 4, 64], BF16, a + i * 1024) for i in range(2)]; a += 2048
        self.VCX = [self.sb(f"VCX{i}", [64, 4, 128], BF16, a + i * 1024) for i in range(2)]; a += 2048
        assert a <= o, (a, o)
        a = self.o_scr + 4352
        self.SG = [self.sb(f"SG{i}", [128, 512], F32, a + i * 2048) for i in range(3)]; a += 6144
        self.MT = [self.sb(f"MT{i}", [128, 512], F32, a + i * 2048) for i in range(2)]; a += 4096
        assert a <= o
        self.YQ = self.sb("YQ", [128, 4, T], BF16, self.OC + 34816)
        self.o_ring = o
        self.NSLOT = 4
        self.ring = [self.sb(f"ring{i}", [128, 4096], BF16, o + i * 8192) for i in range(self.NSLOT)]
        self.ringf = [self.sb(f"ringf{i}", [128, 2048], F32, o + i * 8192) for i in range(self.NSLOT)]
        o += self.NSLOT * 8192
        self.o_end = o
        assert o <= 212736, o
        self.wi = 0
        self.psi = 0
        self.pending = []

    def sb(self, name, shape, dt, off):
        self.ncnt += 1
        return self.nc.alloc_sbuf_tensor_at(f"{name}_{self.ncnt}", shape, dt, offset=self.abase + off)

    def mm(self, out, lhsT, rhs, start=True, stop=True):
        self.P.add("pe", lambda e: e.matmul(out, lhsT, rhs, start=start, stop=stop), [lhsT, rhs], [out])

    def act(self, out, in_, func, bias=None, scale=None):
        kw = {}
        rd = [in_]
        if bias is not None:
            kw["bias"] = bias
            if not isinstance(bias, (int, float)):
                rd.append(bias)
        if scale is not None:
            kw["scale"] = scale
            if not isinstance(scale, (int, float)):
                rd.append(scale)
        self.P.add("act", lambda e: e.activation(out, in_, func, **kw), rd, [out])

    def tt(self, out, in0, in1, op):
        self.P.add("dve", lambda e: e.tensor_tensor(out, in0, in1, op), [in0, in1], [out])

    def ts(self, out, in0, s1, s2, op0, op1=None):
        rd = [in0] + [s for s in (s1, s2) if s is not None and not isinstance(s, (int, float))]
        if op1 is None:
            self.P.add("dve", lambda e: e.tensor_scalar(out, in0, s1, None, op0), rd, [out])
        else:
            self.P.add("dve", lambda e: e.tensor_scalar(out, in0, s1, s2, op0, op1), rd, [out])

    def stt(self, out, in0, scalar, in1, op0, op1):
        rd = [in0, in1] + ([] if isinstance(scalar, (int, float)) else [scalar])
        self.P.add("dve", lambda e: e.scalar_tensor_tensor(out, in0, scalar, in1, op0, op1), rd, [out])

    def recip(self, out, in_):
        self.P.add("dve", lambda e: e.reciprocal(out, in_), [in_], [out])

    def vcopy(self, out, in_):
        self.P.add("dve", lambda e: e.tensor_copy(out, in_), [in_], [out])

    def memset(self, ap, v):
        self.P.add("dve", lambda e: e.memset(ap, v), [], [ap])

    def dma(self, out, in_, eng="sp"):
        return self.P.add(eng, lambda e: e.dma_start(out=out, in_=in_), [in_], [out], kind="d")

    def bank(self):
        b = self.psum[self.psi % 4]
        self.psi += 1
        return b

    def wload(self, key):
        off, kc, n = self.dirn[key]
        if FAKE:
            off = 0
        slot = self.ring[self.wi % self.NSLOT]
        self.wi += 1
        v = slot[:, 0:kc * n]
        self.dma(v, self.wbig[:, off:off + kc * n], eng="pool")
        self.tick()
        return v.rearrange("p (k c) -> p k c", k=kc)

    def tick(self, flush=False):
        keep = []
        for it in self.pending:
            it[0] -= 1
            if it[0] <= 0 or flush:
                it[1]()
            else:
                keep.append(it)
        self.pending = keep

    def setup(self):
        self.memset(self.ONESB[:, :], 1.0)
        self.memset(self.ONESF[:, :], 1.0)
        self.dma(self.X[:, :, :], self.xT.rearrange("(j p) t -> p j t", p=128))
        self.dma(self.VECS[:, :], self.vecs_d[:, :])
        self.dma(self.SEL[:, :], self.sel_d[:, :])
        CS = self.sb("CS", [128, KC * 2], F32, self.o_scr)
        BAD = self.sb("BAD", [1, 72 * 128], F32, self.OC)
        ADAL = self.sb("ADAL", [128, 144], F32, self.o_scr + 192)
        self.dma(CS[:, :], self.cT[:, :])
        self.dma(BAD[:, :], self.adab[:, :])
        self.act(CS[:, :], CS[:, :], AF.Silu)
        CS3 = CS[:, :].rearrange("p (k v) -> p k v", v=2)
        for t in range(72):
            wt = self.ringf[self.wi % self.NSLOT]
            self.wi += 1
            self.dma(wt[:, :], self.adaw[0 if FAKE else t])
            w3 = wt[:, :].rearrange("p (k c) -> p k c", k=KC)
            ps = self.bank()
            for k in range(KC):
                self.mm(ps[:, 0:2], w3[:, k, :], CS3[:, k, :], start=(k == 0), stop=False)
            self.mm(ps[:, 0:2], BAD[0:1, t * 128:(t + 1) * 128], self.ONESF[0:1, 0:2], start=False, stop=True)
            self.vcopy(ADAL[:, t * 2:(t + 1) * 2], ps[:, 0:2])
        self.dma(self.cinA.ap(), ADAL[:, :])
        self.P.add("pool", lambda e: e.collective_compute(
            "AllGather", ALU.bypass, replica_groups=[[0, 1, 2, 3], [4, 5, 6, 7]],
            ins=[self.cinA.ap().opt()], outs=[self.coutA.ap().opt()]),
            [self.cinA.ap()], [self.coutA.ap()], kind="cc")
        self.dma(self.MALL[:, :, :], self.coutA.ap().rearrange("(i p) c -> p i c", p=128))

    def load_mods(self, l):
        for cls in range(2):
            src = self.MALL[:, :, l * 72 + cls:l * 72 + 72:2]
            dst = self.MODS[:, cls, :].rearrange("p (i j) -> p i j", i=4)
            self.vcopy(dst, src)

    def mod(self, cls, n, j=None):
        if j is None:
            return self.MODS[:, cls, n * 16:(n + 1) * 16]
        return self.MODS[:, cls, n * 16 + j:n * 16 + j + 1]

    def rms_stats(self, nfeat_scale=1.0 / D):
        for (t0, t1) in TBS:
            n = t1 - t0
            ps = self.psum[7]
            for j in range(KC):
                sq = self.SQ[:, j % 2, 0:n]
                self.act(sq, self.X[:, j, t0:t1], AF.Square)
                self.mm(ps[:, 0:n], self.ONESB[:, :], sq, start=(j == 0), stop=(j == KC - 1))
            self.act(self.RS[:, t0:t1], ps[:, 0:n], AF.Sqrt, bias=self.EPS_AP[:, 0:1], scale=nfeat_scale)
            self.recip(self.RS[:, t0:t1], self.RS[:, t0:t1])

    def modulate(self, l, nidx, n_shift, n_scale, dst):
        g = self.VECS[:, V_NW + (l * 3 + nidx) * 16:V_NW + (l * 3 + nidx + 1) * 16]
        for cls in range(2):
            self.ts(self.AV[:, cls, :], self.mod(cls, n_scale), 1.0, None, ALU.add)
            self.tt(self.AV[:, cls, :], self.AV[:, cls, :], g, ALU.mult)
        for j in range(KC):
            for cls, (t0, t1) in ((0, (0, TL)), (1, (TL, T))):
                self.tt(self.TMPF[:, t0:t1], self.X[:, j, t0:t1], self.RS[:, t0:t1], ALU.mult)
                self.act(dst[:, j, t0:t1], self.TMPF[:, t0:t1], AF.Identity,
                         bias=self.mod(cls, n_shift, j), scale=self.AV[:, cls, j:j + 1])

    def gate_vec(self, n_gate, factor):
        for cls in range(2):
            self.ts(self.GV[:, cls, :], self.mod(cls, n_gate), float(factor), None, ALU.mult)

    def resid_add(self, dj, t0, t1, ps):
        for (a, b, cls) in ((t0, min(t1, TL), 0), (max(t0, TL), t1, 1)):
            if b <= a:
                continue
            self.stt(self.X[:, dj, a:b], ps[:, a - t0:b - t0], self.GV[:, cls, dj:dj + 1], self.X[:, dj, a:b],
                     ALU.mult, ALU.add)

    def ffn(self, l, s, nidx, n0, tbs=None):
        tbs = tbs or TBS
        self.rms_stats()
        self.modulate(l, nidx, n0, n0 + 1, self.H)
        self.gate_vec(n0 + 2, 0.5)
        for hf in range(2):
            for jj in range(22):
                wt = self.wload(("fi", l, s, hf * 22 + jj))
                for (t0, t1) in tbs:
                    n = t1 - t0
                    pa = self.bank()
                    pb = self.bank()
                    for k in range(KC):
                        self.mm(pa[:, 0:n], wt[:, k, 0:128], self.H[:, k, t0:t1], start=(k == 0), stop=(k == KC - 1))
                    for k in range(KC):
                        self.mm(pb[:, 0:n], wt[:, k, 128:256], self.H[:, k, t0:t1], start=(k == 0), stop=(k == KC - 1))
                    sa = self.SA[:, (self.psi // 2) % 2, 0:n]
                    self.act(sa, pa[:, 0:n], AF.Silu)
                    self.tt(self.U[:, jj, t0:t1], sa, pb[:, 0:n], ALU.mult)
            for dj in range(KC):
                wt = self.wload(("fo", l, s, hf, dj))
                for (t0, t1) in tbs:
                    n = t1 - t0
                    ps = self.bank()
                    for k in range(22):
                        self.mm(ps[:, 0:n], wt[:, k, :], self.U[:, k, t0:t1], start=(k == 0), stop=(k == 21))
                    self.resid_add(dj, t0, t1, ps)


    def proj_fm(self, wt, c0, ncol, kc, src, ps, t0, t1):
        n = t1 - t0
        for k in range(kc):
            self.mm(ps[0:ncol, 0:n], wt[:, k, c0:c0 + ncol], src[:, k, t0:t1], start=(k == 0), stop=(k == kc - 1))

    def rope_evac(self, dst, ps1, ps2, np_, t0, t1):
        n = t1 - t0
        self.tt(self.RT[0][0:np_, 0:n], ps1[0:np_, 0:n], self.ROPE[0:np_, 0, t0:t1], ALU.mult)
        self.tt(self.RT[1][0:np_, 0:n], ps2[0:np_, 0:n], self.ROPE[0:np_, 1, t0:t1], ALU.mult)
        self.tt(dst, self.RT[0][0:np_, 0:n], self.RT[1][0:np_, 0:n], ALU.add)

    def mla_norm(self, l, tiles, nch, wcol0, dst, tbs=None):
        tbs = tbs or TBS
        for (t0, t1) in tbs:
            n = t1 - t0
            pss = []
            for c in range(nch):
                ps = self.psum[c]
                wt = tiles[c // 2]
                self.proj_fm(wt, (c % 2) * 128, 128, KC, self.H, ps, t0, t1)
                pss.append(ps)
            pn = self.psum[7]
            for c in range(nch):
                sq = self.SQ[:, c % 2, 0:n]
                self.act(sq, pss[c][:, 0:n], AF.Square)
                self.mm(pn[:, 0:n], self.ONESB[:, :], sq, start=(c == 0), stop=(c == nch - 1))
            self.act(self.RS[:, t0:t1], pn[:, 0:n], AF.Sqrt, bias=self.EPS_AP[:, 0:1], scale=1.0 / (nch * 128))
            self.recip(self.RS[:, t0:t1], self.RS[:, t0:t1])
            for c in range(nch):
                self.stt(dst[:, c, t0:t1], pss[c][:, 0:n], self.VECS[:, wcol0 + c:wcol0 + c + 1],
                         self.RS[:, t0:t1], ALU.mult, ALU.mult)

    def k_phase(self, l):
        def kdst(br, h, nrows=128):
            ci, r0 = kvloc(br, h)
            return self.cinK[ci].ap()[r0:r0 + nrows, :]

        self.dma(self.ROPE[:, :, :].rearrange("p a t -> p (a t)"), self.rope_d[:, :], eng="pool")
        ki = [0]
        vi = [0]
        grp = [[0, 1, 2, 3], [4, 5, 6, 7]]
        nocc = DEBUG_STOP.endswith("_nocc")

        def fire(kind, i):
            ci, co = (self.cinK[i], self.coutK[i]) if kind == "k" else (self.cinV[i], self.coutV[i])
            self.P.add("pool", lambda e, ci=ci, co=co: e.collective_compute(
                "AllGather", ALU.bypass, replica_groups=grp,
                ins=[ci.ap().opt()], outs=[co.ap().opt()]),
                [ci.ap()], [co.ap()], kind="cc", ccg=l + 1)

        def trigger(kind, i):
            if nocc:
                return
            self.pending.append([3, lambda: fire(kind, i)])

        def kst():
            ki[0] += 1
            return self.KST[ki[0] % 2]

        def vproj(key, src, kc, chunk, col0):
            wt = self.wload(key)
            ncol = self.dirn[key][2]
            dst = self.cinV[chunk].ap()
            for tc in range(9):
                t0 = tc * 128
                nt = min(128, T - t0)
                ps = self.bank()
                for k in range(kc):
                    self.mm(ps[0:nt, 0:ncol], src[:, k, t0:t0 + nt], wt[:, k, :], start=(k == 0), stop=(k == kc - 1))
                vi[0] += 1
                vs = self.VST[vi[0] % 2]
                self.act(vs[0:nt, 0:ncol], ps[0:nt, 0:ncol], AF.Copy)
                self.dma(dst[t0:t0 + nt, col0:col0 + ncol], vs[0:nt, 0:ncol])

        for h in range(6):
            wt = self.wload(("nak", l, h))
            st = kst()
            for (t0, t1) in TBS:
                ps = self.bank()
                self.proj_fm(wt, 0, 128, KC, self.H, ps, t0, t1)
                self.act(st[:, t0:t1], ps[:, 0:t1 - t0], AF.Copy)
            self.dma(kdst("na", h), st[:, :])
            if h % 3 == 2:
                trigger("k", h // 3)
        for t, (a, n) in enumerate(VT_NA):
            vproj(("nav", l, t), self.H, KC, a // 384, a % 384)
            if t % 2 == 1:
                trigger("v", t // 2)
        tiles = [self.wload(("ckv", l, t)) for t in range(2)]
        self.mla_norm(l, tiles, 4, V_KVN + l * 4, self.CKVN)
        for h in range(5):
            wt = self.wload(("ukvk", l, h))
            st = kst()
            for (t0, t1) in TBS:
                ps = self.bank()
                self.proj_fm(wt, 0, 128, 4, self.CKVN, ps, t0, t1)
                self.act(st[:, t0:t1], ps[:, 0:t1 - t0], AF.Copy)
            self.dma(kdst("mla", h), st[:, :])
            if h == 2:
                trigger("k", 2)
        wt = self.wload(("kr", l))
        st = kst()
        for (t0, t1) in TBS:
            p1 = self.bank()
            p2 = self.bank()
            self.proj_fm(wt, 0, 64, KC, self.H, p1, t0, t1)
            self.proj_fm(wt, 64, 64, KC, self.H, p2, t0, t1)
            self.rope_evac(st[0:64, t0:t1], p1, p2, 64, t0, t1)
        self.dma(kdst("kr", 0, 64), st[0:64, :])
        trigger("k", 3)
        for t, (a, n) in enumerate(VT_M):
            vproj(("mv", l, t), self.CKVN, 4, 2 + a // 384, a % 384)
            trigger("v", 2 + t)
        for h in range(5):
            w0 = self.wload(("dk", l, h, 0))
            w1 = self.wload(("dk", l, h, 1))
            st = kst()
            for (t0, t1) in TBS:
                p1 = self.bank()
                p2 = self.bank()
                self.proj_fm(w0, 0, 128, KC, self.H, p1, t0, t1)
                self.proj_fm(w1, 0, 128, KC, self.H, p2, t0, t1)
                self.rope_evac(st[:, t0:t1], p1, p2, 128, t0, t1)
            self.dma(kdst("diff", h), st[:, :])
            if h == 2 or h == 4:
                trigger("k", 4 + h // 3)
        for t, (a, n) in enumerate(VT_D):
            vproj(("dv", l, t), self.H, KC, 4 + a // 384, a % 384)
            if t >= 1:
                trigger("v", 4 + t - 1)

    def q_phase(self, l):
        for h in range(6):
            wt = self.wload(("naq", l, h))
            for (t0, t1) in self.cur_tbs:
                ps = self.bank()
                self.proj_fm(wt, 0, 128, KC, self.H, ps, t0, t1)
                self.act(self.QNA[:, h, t0:t1], ps[:, 0:t1 - t0], AF.Copy)
        tiles = [self.wload(("cq", l, t)) for t in range(3)]
        self.mla_norm(l, tiles, 6, V_QN + l * 6, self.CQN, self.cur_tbs)
        for h in range(5):
            wn = self.wload(("uqn", l, h))
            wr = self.wload(("uqr", l, h))
            for (t0, t1) in self.cur_tbs:
                ps = self.bank()
                self.proj_fm(wn, 0, 128, 6, self.CQN, ps, t0, t1)
                self.act(self.QMN[:, h, t0:t1], ps[:, 0:t1 - t0], AF.Copy)
                p1 = self.bank()
                p2 = self.bank()
                self.proj_fm(wr, 0, 64, 6, self.CQN, p1, t0, t1)
                self.proj_fm(wr, 64, 64, 6, self.CQN, p2, t0, t1)
                self.rope_evac(self.QMR[0:64, h, t0:t1], p1, p2, 64, t0, t1)
        for h in range(5):
            w0 = self.wload(("dq", l, h, 0))
            w1 = self.wload(("dq", l, h, 1))
            for (t0, t1) in self.cur_tbs:
                p1 = self.bank()
                p2 = self.bank()
                self.proj_fm(w0, 0, 128, KC, self.H, p1, t0, t1)
                self.proj_fm(w1, 0, 128, KC, self.H, p2, t0, t1)
                self.rope_evac(self.QD[:, h, t0:t1], p1, p2, 128, t0, t1)

    def attend(self, nq, chunks, maps, v_fn, scale, obanks, lbanks):
        nm = len(maps)
        sb_i = [0]

        def scores(ci):
            ch = chunks[ci]
            nk = ch[0]
            out = []
            for m in range(nm):
                ps = self.psum[sb_i[0] % 4]
                sb_i[0] += 1
                pieces = maps[m]
                for pi, (kf, q) in enumerate(pieces):
                    self.mm(ps[0:nk, 0:nq], kf(ch), q, start=(pi == 0), stop=(pi == len(pieces) - 1))
                out.append(ps)
            return out

        pend = scores(0)
        pti = 0
        for ci in range(len(chunks)):
            cur = pend
            if ci + 1 < len(chunks):
                pend = scores(ci + 1)
            nk = chunks[ci][0]
            for m in range(nm):
                pt = self.PT[pti % 4]
                pti += 1
                self.act(pt[0:nk, 0:nq], cur[m][0:nk, 0:nq], AF.Exp, scale=float(scale))
                first = ci == 0
                last = ci == len(chunks) - 1
                self.mm(obanks[m][:, 0:nq], v_fn(chunks[ci]), pt[0:nk, 0:nq], start=first, stop=last)
                self.mm(lbanks[m][:, 0:nq], self.ONESB[0:nk, :], pt[0:nk, 0:nq], start=first, stop=last)

    def kview(self, br, h, nrows=128):
        ci, r0 = kvloc(br, h)
        return self.coutK[ci].ap().rearrange("(r k) t -> r k t", r=4)[:, r0:r0 + nrows, :]

    def vview(self, br, h):
        ci, c0 = kvloc(br, h)
        return self.coutV[ci].ap().rearrange("(r t) c -> r t c", r=4)[:, :, c0:c0 + 128]

    def mla_attn(self, l):
        KR = self.ring[0][0:64, :].rearrange("p (r t) -> p r t", r=4)
        ckr = self.kview("kr", 0, 64)
        self.dma(KR, ckr[:, :, 0:TL].rearrange("r d t -> d r t"))
        KRC = self.sb("KRC", [64, 4, 64], BF16, self.o_scr + 20992 - 512)
        self.dma(KRC[:, :, :], ckr[:, :, TL:T].rearrange("r d t -> d r t"))
        for h in range(5):
            KN = self.ring[1 if h % 2 == 0 else 3][:, :].rearrange("p (r t) -> p r t", r=4)
            VV = self.ring[2][:, :].rearrange("p (r n c) -> p r n c", r=4, n=8)
            KC_ = self.KCX[h % 2]
            VC_ = self.VCX[h % 2]
            ck = self.kview("mla", h)
            cv = self.vview("mla", h)
            self.dma(KN, ck[:, :, 0:TL].rearrange("r d t -> d r t"))
            self.dma(KC_[:, 0, :, :], ck[:, :, TL:T].rearrange("r d t -> d r t"))
            for r_ in range(4):
                self.dma(VV[:, r_, :, :], cv[r_, 0:TL, :].rearrange("(n p) c -> p n c", p=128))
            self.dma(VC_[:, :, :], cv[:, TL:T, :].rearrange("r p c -> p r c"))
            lat = [(128, "l", r, n) for r in range(4) for n in range(8)]
            ctxc = [(64, "c", r, 0) for r in range(4)]

            def kn(ch):
                return KN[:, ch[2], ch[3] * 128:(ch[3] + 1) * 128] if ch[1] == "l" else KC_[:, 0, ch[2], :]

            def kr(ch):
                return KR[:, ch[2], ch[3] * 128:(ch[3] + 1) * 128] if ch[1] == "l" else KRC[:, ch[2], :]

            def vf(ch):
                return VV[:, ch[2], ch[3], :] if ch[1] == "l" else VC_[:, ch[2], :]

            for (t0, t1) in (QBS[:2] if l == 1 else QBS):
                nq = t1 - t0
                chunks = (lat + ctxc) if t0 < TL else ctxc
                maps = [[(kn, self.QMN[:, h, t0:t1]), (kr, self.QMR[0:64, h, t0:t1])]]
                self.attend(nq, chunks, maps, vf, MLA_SCALE, [self.psum[4]], [self.psum[5]])
                self.act(self.AO[0][:, 0:nq], self.psum[4][:, 0:nq], AF.Copy)
                self.act(self.RC[0][:, 0:nq], self.psum[5][:, 0:nq], AF.Copy)
                self.recip(self.RC[0][:, 0:nq], self.RC[0][:, 0:nq])
                self.tt(self.OT[:, 6 + h, t0:t1], self.AO[0][:, 0:nq], self.RC[0][:, 0:nq], ALU.mult)

    def diff_attn(self, l, lam_init):
        DL = self.sb("DL", [1, 512], F32, self.o_scr + 16384)
        LS = self.sb("LS", [1, 8], F32, self.o_scr + 16384 + 2048)
        self.dma(DL[:, :], self.dlam_d[:, :])
        b0 = l * 256
        self.tt(DL[0:1, b0:b0 + 64], DL[0:1, b0:b0 + 64], DL[0:1, b0 + 64:b0 + 128], ALU.mult)
        self.tt(DL[0:1, b0 + 128:b0 + 192], DL[0:1, b0 + 128:b0 + 192], DL[0:1, b0 + 192:b0 + 256], ALU.mult)
        self.P.add("dve", lambda e: e.reduce_sum(LS[0:1, 0:1], DL[0:1, b0:b0 + 64], mybir.AxisListType.X),
                   [DL[0:1, b0:b0 + 64]], [LS[0:1, 0:1]])
        self.P.add("dve", lambda e: e.reduce_sum(LS[0:1, 1:2], DL[0:1, b0 + 128:b0 + 192], mybir.AxisListType.X),
                   [DL[0:1, b0 + 128:b0 + 192]], [LS[0:1, 1:2]])
        self.act(LS[0:1, 2:4], LS[0:1, 0:2], AF.Exp)
        self.tt(LS[0:1, 4:5], LS[0:1, 3:4], LS[0:1, 2:3], ALU.subtract)
        self.ts(LS[0:1, 4:5], LS[0:1, 4:5], float(-lam_init), None, ALU.add)
        ps = self.psum[0]
        self.mm(ps[:, 0:1], self.ONESF[0:1, :], LS[0:1, 4:5], start=True, stop=True)
        self.vcopy(self.NEGLAM[:, 0:1], ps[:, 0:1])
        self.ts(self.SUBW[:, 0:1], self.VECS[:, V_SUB + l:V_SUB + l + 1], float(1.0 - lam_init), None, ALU.mult)
        for h in range(5):
            KD = self.ring[h % 2][:, :].rearrange("p (r t) -> p r t", r=4)
            VV = self.ring[2 + h % 2][:, :].rearrange("p (r n c) -> p r n c", r=4, n=8)
            KC_ = self.KCX[h % 2]
            VC_ = self.VCX[h % 2]
            ck = self.kview("diff", h)
            cv = self.vview("diff", h)
            self.dma(KD, ck[:, :, 0:TL].rearrange("r d t -> d r t"))
            self.dma(KC_[:, 0, :, :], ck[:, :, TL:T].rearrange("r d t -> d r t"))
            for r_ in range(4):
                self.dma(VV[:, r_, :, :], cv[r_, 0:TL, :].rearrange("(n p) c -> p n c", p=128))
            self.dma(VC_[:, :, :], cv[:, TL:T, :].rearrange("r p c -> p r c"))
            lat = [(128, "l", r, n) for r in range(4) for n in range(8)]
            ctxc = [(64, "c", r, 0) for r in range(4)]

            def k1(ch):
                return KD[0:64, ch[2], ch[3] * 128:(ch[3] + 1) * 128] if ch[1] == "l" else KC_[0:64, 0, ch[2], :]

            def k2(ch):
                return KD[64:128, ch[2], ch[3] * 128:(ch[3] + 1) * 128] if ch[1] == "l" else KC_[64:128, 0, ch[2], :]

            def vf(ch):
                return VV[:, ch[2], ch[3], :] if ch[1] == "l" else VC_[:, ch[2], :]

            for (t0, t1) in (QBS[:2] if l == 1 else QBS):
                nq = t1 - t0
                chunks = (lat + ctxc) if t0 < TL else ctxc
                maps = [[(k1, self.QD[0:64, h, t0:t1])], [(k2, self.QD[64:128, h, t0:t1])]]
                O1, L1, O2, L2 = self.psum[4], self.psum[5], self.psum[6], self.psum[7]
                self.attend(nq, chunks, maps, vf, DIFF_SCALE, [O1, O2], [L1, L2])
                self.act(self.AO[0][:, 0:nq], O1[:, 0:nq], AF.Copy)
                self.act(self.AO[1][:, 0:nq], O2[:, 0:nq], AF.Copy)
                self.act(self.RC[0][:, 0:nq], L1[:, 0:nq], AF.Copy)
                self.act(self.RC[1][:, 0:nq], L2[:, 0:nq], AF.Copy)
                self.recip(self.RC[0][:, 0:nq], self.RC[0][:, 0:nq])
                self.recip(self.RC[1][:, 0:nq], self.RC[1][:, 0:nq])
                self.tt(self.AO[0][:, 0:nq], self.AO[0][:, 0:nq], self.RC[0][:, 0:nq], ALU.mult)
                self.tt(self.AO[1][:, 0:nq], self.AO[1][:, 0:nq], self.RC[1][:, 0:nq], ALU.mult)
                self.stt(self.AO[0][:, 0:nq], self.AO[1][:, 0:nq], self.NEGLAM[:, 0:1], self.AO[0][:, 0:nq],
                         ALU.mult, ALU.add)
                sq = self.PT[0][:, 0:nq]
                self.act(sq, self.AO[0][:, 0:nq], AF.Square)
                pn = self.psum[0]
                self.mm(pn[:, 0:nq], self.ONESB[:, :], sq, start=True, stop=True)
                self.act(self.RC[0][:, 0:nq], pn[:, 0:nq], AF.Sqrt, bias=self.EPS_AP[:, 0:1], scale=1.0 / 128)
                self.recip(self.RC[0][:, 0:nq], self.RC[0][:, 0:nq])
                self.stt(self.OT[:, 11 + h, t0:t1], self.AO[0][:, 0:nq], self.SUBW[:, 0:1], self.RC[0][:, 0:nq],
                         ALU.mult, ALU.mult)

    def na_attn(self, l):
        base = self.o_ring
        KW = self.sb("KW", [128, 1536], BF16, base)
        KCN = self.sb("KCN", [128, 4, 64], BF16, base + 3072)
        VW = self.sb("VW", [128, 12, 128], BF16, base + 3584)
        VCN = self.sb("VCN", [64, 4, 128], BF16, base + 6656)
        CKP = self.sb("CKP", [128, 4, 256], BF16, base + 8192)
        CKN = self.sb("CKN", [128, 4, 256], BF16, base + 8192 + 2048)
        CVP = self.sb("CVP", [128, 4, 2, 128], BF16, base + 8192 + 4096)
        CVN = self.sb("CVN", [128, 4, 2, 128], BF16, base + 8192 + 6144)
        NB = [self.sb("NB0", [128, 40 * 64], BF16, base + 16384), self.sb("NB1", [128, 38 * 64], BF16, base + 24576)]
        assert NA_OFF[8] == 40 and NA_OFF[16] - NA_OFF[8] == 38
        for h in range(6):
            ci_, o_ = kvloc("na", h)
            ck = self.kview("na", h)
            cv = self.vview("na", h)
            self.dma(KW[:, 256:1280], self.cinK[ci_].ap()[o_:o_ + 128, 0:TL])
            self.dma(CKP[:, :, :], ck[:, :, 768:1024].rearrange("r d t -> d r t"))
            self.dma(CKN[:, :, :], ck[:, :, 0:256].rearrange("r d t -> d r t"))
            self.dma(KCN[:, :, :], ck[:, :, TL:T].rearrange("r d t -> d r t"))
            self.dma(VW[:, 2:10, :], self.cinV[ci_].ap()[0:TL, o_:o_ + 128].rearrange("(n p) c -> p n c", p=128))
            for r_ in range(4):
                self.dma(CVP[:, r_, :, :], cv[r_, 768:1024, :].rearrange("(n p) c -> p n c", p=128))
                self.dma(CVN[:, r_, :, :], cv[r_, 0:256, :].rearrange("(n p) c -> p n c", p=128))
            self.dma(VCN[:, :, :], cv[:, TL:T, :].rearrange("r p c -> p r c"))
            nbsrc = self.nab_d[0 if FAKE else l * 6 + h]
            self.dma(NB[0][:, :], nbsrc[:, 0:40 * 64], eng="pool")
            self.dma(NB[1][:, :], nbsrc[:, 40 * 64:78 * 64], eng="pool")
            for (dst, cand, so) in ((KW[:, 0:256], CKP, 0), (KW[:, 1280:1536], CKN, 4)):
                self.ts(dst, cand[:, 0, :], self.SEL[:, so:so + 1], None, ALU.mult)
                for r in range(1, 4):
                    self.stt(dst, cand[:, r, :], self.SEL[:, so + r:so + r + 1], dst, ALU.mult, ALU.add)
            for (dst, cand, so) in ((VW[:, 0:2, :], CVP, 0), (VW[:, 10:12, :], CVN, 4)):
                self.ts(dst, cand[:, 0, :, :], self.SEL[:, so:so + 1], None, ALU.mult)
                for r in range(1, 4):
                    self.stt(dst, cand[:, r, :, :], self.SEL[:, so + r:so + r + 1], dst, ALU.mult, ALU.add)
            TB2 = [self.TBI, self.TBI2]

            def na_scores(r):
                q = self.QNA[:, h, r * 64:(r + 1) * 64]
                clo, chi = NA_CH[r]
                nch = chi - clo + 1
                ps = self.psum[(2 * r) % 4]
                pc = self.psum[(2 * r + 1) % 4]
                for ci in range(nch):
                    c = clo + ci
                    self.mm(ps[:, ci * 64:(ci + 1) * 64], KW[:, c * 128:(c + 1) * 128], q, start=True, stop=True)
                for rk in range(4):
                    self.mm(pc[0:64, rk * 64:(rk + 1) * 64], KCN[:, rk, :], q, start=True, stop=True)
                return ps, pc

            def na_rest(r, ps, pc):
                r8, rr = r // 8, r % 8
                O, L = self.psum[4 + 2 * r8], self.psum[5 + 2 * r8]
                clo, chi = NA_CH[r]
                nch = chi - clo + 1
                boff = (NA_OFF[r] - NA_OFF[r8 * 8]) * 64
                tb = TB2[r % 2][:, 0:nch * 64]
                self.stt(tb, ps[:, 0:nch * 64], float(NA_SCALE), NB[r8][:, boff:boff + nch * 64], ALU.mult, ALU.add)
                pt = self.PT[r % 2]
                ptc = self.PT[2 + r % 2]
                self.act(pt[:, 0:nch * 64], tb, AF.Exp)
                self.act(ptc[0:64, 0:256], pc[0:64, 0:256], AF.Exp, scale=float(NA_SCALE))
                oc = slice(rr * 64, (rr + 1) * 64)
                for ci in range(nch):
                    c = clo + ci
                    self.mm(O[:, oc], VW[:, c, :], pt[:, ci * 64:(ci + 1) * 64], start=(ci == 0), stop=False)
                    self.mm(L[:, oc], self.ONESB[:, :], pt[:, ci * 64:(ci + 1) * 64], start=(ci == 0), stop=False)
                for rk in range(4):
                    self.mm(O[:, oc], VCN[:, rk, :], ptc[0:64, rk * 64:(rk + 1) * 64], start=False, stop=(rk == 3))
                    self.mm(L[:, oc], self.ONESB[0:64, :], ptc[0:64, rk * 64:(rk + 1) * 64], start=False, stop=(rk == 3))
                if rr == 7:
                    self.recip(self.RC[r8][:, :], L[:, :])
                    self.tt(self.OT[:, h, r8 * 512:(r8 + 1) * 512], O[:, :], self.RC[r8][:, :], ALU.mult)

            pend = na_scores(0)
            for r in range(16):
                cur = pend
                if r + 1 < 16:
                    pend = na_scores(r + 1)
                na_rest(r, *cur)
            O, L = self.psum[4], self.psum[5]
            if l == 1:
                continue
            q = self.QNA[:, h, TL:T]
            pc = self.psum[0]
            for rk in range(4):
                self.mm(pc[0:64, rk * 64:(rk + 1) * 64], KCN[:, rk, :], q, start=True, stop=True)
            ptc = self.PT[2]
            self.act(ptc[0:64, 0:256], pc[0:64, 0:256], AF.Exp, scale=float(NA_SCALE))
            for rk in range(4):
                self.mm(O[:, 0:64], VCN[:, rk, :], ptc[0:64, rk * 64:(rk + 1) * 64], start=(rk == 0), stop=(rk == 3))
                self.mm(L[:, 0:64], self.ONESB[0:64, :], ptc[0:64, rk * 64:(rk + 1) * 64], start=(rk == 0), stop=(rk == 3))
            self.recip(self.RC[0][:, 0:64], L[:, 0:64])
            self.tt(self.OT[:, h, TL:T], O[:, 0:64], self.RC[0][:, 0:64], ALU.mult)

    def merge(self, l):
        self.rms_stats()
        self.modulate(l, 1, 3, 4, self.H2)
        self.gate_vec(5, 1.0)
        for q4 in range(4):
            for dq in range(4):
                dj = q4 * 4 + dq
                wg = [self.wload(("g", l, dj, br)) for br in range(3)]
                wb = self.wload(("wb", l, dj))
                for (t0, t1) in self.cur_tbs:
                    n = t1 - t0
                    pg = [self.psum[i] for i in range(3)]
                    pb = [self.psum[3 + i] for i in range(3)]
                    for br, (ka, kb) in enumerate(((0, 6), (6, 11), (11, 16))):
                        self.proj_fm(wg[br], 0, 128, KC, self.H2, pg[br], t0, t1)
                        for k in range(ka, kb):
                            self.mm(pb[br][:, 0:n], wb[:, k, :], self.OT[:, k, t0:t1], start=(k == ka), stop=(k == kb - 1))
                        self.act(self.SG[br][:, 0:n], pg[br][:, 0:n], AF.Sigmoid)
                        if br == 0:
                            self.tt(self.MT[0][:, 0:n], self.SG[0][:, 0:n], pb[0][:, 0:n], ALU.mult)
                        elif br == 1:
                            self.tt(self.MT[1][:, 0:n], self.SG[1][:, 0:n], pb[1][:, 0:n], ALU.mult)
                            self.tt(self.MT[0][:, 0:n], self.MT[0][:, 0:n], self.MT[1][:, 0:n], ALU.add)
                        else:
                            self.tt(self.MT[1][:, 0:n], self.SG[2][:, 0:n], pb[2][:, 0:n], ALU.mult)
                            self.tt(self.YQ[:, dq, t0:t1], self.MT[0][:, 0:n], self.MT[1][:, 0:n], ALU.add)
            for t in range(8):
                wt = self.wload(("wo", l, q4 // 2, t))
                ko = (q4 % 2) * 4
                for dd in range(2):
                    dj2 = t * 2 + dd
                    for (t0, t1) in self.cur_tbs:
                        n = t1 - t0
                        ps = self.psum[6 + (dd % 2)]
                        for k in range(4):
                            self.mm(ps[:, 0:n], wt[:, ko + k, dd * 128:(dd + 1) * 128], self.YQ[:, k, t0:t1],
                                    start=(k == 0), stop=(k == 3))
                        self.resid_add(dj2, t0, t1, ps)

    def mixer(self, l):
        lam_init = 0.8 - 0.6 * math.exp(-0.3 * l)
        self.rms_stats()
        self.modulate(l, 1, 3, 4, self.H)
        self.k_phase(l)
        if DEBUG_STOP.startswith("raw_kphase"):
            self.tick(flush=True)
        self.chk("kphase")
        self.cur_tbs = TBS_LAT if l == 1 else TBS
        self.q_phase(l)
        self.tick(flush=True)
        self.chk("qphase")
        self.na_attn(l)
        self.chk("na")
        self.mla_attn(l)
        self.chk("mla")
        self.diff_attn(l, lam_init)
        self.chk("diff")
        self.merge(l)

    def chk(self, name):
        if DEBUG_STOP in ("raw_" + name, "raw_" + name + "_nocc"):
            raise StopBuild()

    def dump(self, src_f32_ap):
        self.dma(self.dbg, src_f32_ap)

    def finish(self, out_ops):
        op = self.P.add("sp", None, [], [], kind="c")
        op.deps = list(out_ops) + [o for o in self.P.ops if o.kind == "cc"]

    def final(self):
        self.rms_stats()
        g = V_FN
        IDF = self.sb("IDF", [128, 128], F32, self.o_ring)
        OST = [self.sb(f"OST{i}", [128, D], F32, self.o_ring + 8192 * (1 + i)) for i in range(2)]
        YF = [self.sb(f"YF{i}", [128, 128], F32, self.o_ring + 512 + 512 * i) for i in range(4)]
        self.dma(IDF[:, :], self.ident_d[:, :])
        outs = []
        for tc in range(8):
            ost = OST[tc % 2]
            for j4 in range(4):
                ps = self.bank()
                for jj in range(4):
                    j = j4 * 4 + jj
                    yf = YF[j % 4]
                    self.stt(yf[:, :], self.X[:, j, tc * 128:(tc + 1) * 128], self.VECS[:, g + j:g + j + 1],
                             self.RS[:, tc * 128:(tc + 1) * 128], ALU.mult, ALU.mult)
                    self.P.add("pe", lambda e, o=ps[:, jj * 128:(jj + 1) * 128], i=yf[:, :]: e.transpose(o, i, IDF[:, :]),
                               [yf[:, :], IDF[:, :]], [ps[:, jj * 128:(jj + 1) * 128]])
                self.act(ost[:, j4 * 512:(j4 + 1) * 512], ps[:, :], AF.Copy)
            outs.append(self.dma(self.out[tc * 128:(tc + 1) * 128, :], ost[:, :]))
        return outs

    def build(self):
        nc = self.nc
        self.memset(self.EPS_AP[:, :], EPS)
        self.setup()
        outs = []
        stop = DEBUG_STOP
        done = False
        for l in range(2):
            self.load_mods(l)
            if stop == "mods" and l == 0:
                outs.append(self.dump_small(self.MODS[:, :, :].rearrange("p a b -> p (a b)"), 288))
                done = True
                break
            self.ffn(l, 0, 0, 0)
            if stop == "ffn1" and l == 0:
                outs.append(self.dma(self.dbg, self.X[:, :, :].rearrange("p j t -> p (j t)")))
                done = True
                break
            try:
                self.mixer(l)
            except StopBuild:
                src = self.OT if stop in ("raw_na", "raw_mla", "raw_diff") else self.H
                for j in range(KC):
                    self.vcopy(self.X[:, j, :], src[:, j, :])
                outs.append(self.dma(self.dbg, self.X[:, :, :].rearrange("p j t -> p (j t)")))
                done = True
                break
            if stop == "mixer" and l == 0:
                outs.append(self.dma(self.dbg, self.X[:, :, :].rearrange("p j t -> p (j t)")))
                done = True
                break
            self.ffn(l, 1, 2, 6, TBS_LAT if l == 1 else TBS)
            if stop == "layer0" and l == 0:
                outs.append(self.dma(self.dbg, self.X[:, :, :].rearrange("p j t -> p (j t)")))
                done = True
                break
        if not done:
            outs += self.final()
        self.finish(outs)
        with ExitStack() as stack:
            self.P.finalize(stack)
        return nc

    def dump_small(self, ap, n):
        return self.dma(self.dbg[:, 0:n], ap)


_CACHE = {}


def prep_inputs(inputs):
    inp = {k: np.asarray(v) for k, v in inputs.items()}
    wbig, dirn = build_wbig(inp)
    vecs = build_vecs(inp)
    dlam = np.ascontiguousarray(inp["diff_lambda"].reshape(1, 512).astype(np.float32))
    ident = np.eye(128, dtype=np.float32)
    maps = []
    for core in range(8):
        b, rq = core // 4, core % 4
        xT = np.concatenate([inp["x"][b, rq * TL:(rq + 1) * TL, :].T, inp["ctx"][b, rq * TCX:(rq + 1) * TCX, :].T], axis=1)
        cv = np.stack([inp["c"][b], inp["c_ctx"]], axis=-1)
        cT = cv.reshape(KC, 128, 2).transpose(1, 0, 2).reshape(128, KC * 2)
        adaw = np.empty((72, 128, 2048), np.float32)
        adab = np.empty((1, 72 * 128), np.float32)
        for l in range(2):
            for jj in range(36):
                c0 = (36 * rq + jj) * 128
                W = inp["w_ada"][l][:, c0:c0 + 128]
                adaw[l * 36 + jj] = W.reshape(KC, 128, 128).transpose(1, 0, 2).reshape(128, 2048)
                adab[0, (l * 36 + jj) * 128:(l * 36 + jj + 1) * 128] = inp["b_ada"][l, c0:c0 + 128]
        sel = np.zeros((128, 16), np.float32)
        sel[:, 8 + b] = 1.0
        if rq > 0:
            sel[:, rq - 1] = 1.0
        if rq < 3:
            sel[:, 4 + rq + 1] = 1.0
        nab = na_bias(inp["na_rpb"], rq).reshape(12, 128, NA_OFF[-1] * 64)
        maps.append({
            "xT": np.ascontiguousarray(xT.astype(np.float32)),
            "cT": np.ascontiguousarray(cT.astype(np.float32)),
            "adaw": adaw, "adab": adab, "wbig": wbig, "vecs": vecs, "dlam": dlam,
            "rope": rope_table(rq).reshape(128, 2 * T), "nab": np.ascontiguousarray(nab),
            "sel": sel, "ident": ident,
        })
    return maps, dirn, wbig.shape[1]


def kernel(**inputs):
    maps, dirn, wtot = prep_inputs(inputs)
    b = Builder(dirn, wtot)
    nc = b.build()
    res = run_bass_kernel_spmd(nc, maps, core_ids=list(range(8)))
    kernel.last = res
    out = np.empty((2, 4096, D), np.float32)
    for core in range(8):
        bb, rq = core // 4, core % 4
        out[bb, rq * TL:(rq + 1) * TL, :] = res.results[core]["out"]
    return out
```

```python
import math
import os
from contextlib import ExitStack

import numpy as np
import concourse.bass as bass
import concourse.mybir as mybir
from concourse.bass_utils import run_bass_kernel_spmd

F32 = mybir.dt.float32
BF16 = mybir.dt.bfloat16
AF = mybir.ActivationFunctionType
ALU = mybir.AluOpType

D = 2048
KC = 16
TL = 1024
TCX = 64
T = TL + TCX
FFN = 5632
NEG = -30000.0
EPS = 1e-6
NA_SCALE = 128 ** -0.5
MLA_SCALE = 192 ** -0.5
DIFF_SCALE = 64 ** -0.5
TBS = [(0, 384), (384, 768), (768, 1088)]
QBS = [(0, 512), (512, 1024), (1024, 1088)]
TBS_LAT = [(0, 384), (384, 768), (768, 1024)]
O_NAQ, O_NAK, O_NAV, O_CQ, O_CKV, O_KR, O_DQ, O_DK, O_DV, O_GA, O_GB, O_GD = (
    0, 768, 1536, 2304, 3072, 3584, 3648, 4288, 4928, 5568, 7616, 9664)
KROWS = 16 * 128 + 64
DEBUG_STOP = os.environ.get("MK_STOP", "")
FAKE = bool(os.environ.get("MK_FAKE"))


VT_NA = [(0, 256), (256, 128), (384, 256), (640, 128)]
VT_D = [(0, 256), (256, 128), (384, 256)]
VT_M = [(0, 384), (384, 256)]
KCH_ROWS = [384, 384, 384, 320, 384, 256]
VCH_COLS = [384, 384, 384, 256, 384, 256]


def kvloc(br, h):
    if br == "na":
        return h // 3, (h % 3) * 128
    if br == "mla":
        return 2 + h // 3, (h % 3) * 128
    if br == "kr":
        return 3, 256
    return 4 + h // 3, (h % 3) * 128


def na_chunks(r):
    lo = min(r - 4, 8)
    hi = max(r + 4, max(r - 4, 0) + 8, min(r - 4, 8) + 8)
    return (lo + 4) // 2, (hi - 1 + 4) // 2


NA_CH = [na_chunks(r) for r in range(16)]
NA_OFF = np.cumsum([0] + [b - a + 1 for a, b in NA_CH]).tolist()


def partner64(d):
    return d + 16 if (d % 32) < 16 else d - 16


def wtile(W, cols):
    K = W.shape[0]
    kc = K // 128
    a = W[:, cols].reshape(kc, 128, len(cols)).transpose(1, 0, 2).reshape(128, kc * len(cols))
    return a, kc, len(cols)


def build_wbig(inp):
    tiles = []
    dirn = {}
    off = [0]

    def put(key, W, cols):
        a, kc, n = wtile(W, np.asarray(cols))
        dirn[key] = (off[0], kc, n)
        off[0] += a.shape[1]
        tiles.append(a)

    ar = np.arange
    for l in range(2):
        w_in = inp["w_in"][l]
        uq = inp["mla_w_uq"][l]
        ukv = inp["mla_w_ukv"][l]
        p64 = np.array([partner64(d) for d in range(64)])
        p128 = np.concatenate([p64, 64 + p64])
        for s in range(2):
            fwi = inp["ffn_w_in"][l, s]
            fwo = inp["ffn_w_out"][l, s]
            for hf in range(2):
                for jj in range(22):
                    j = hf * 22 + jj
                    put(("fi", l, s, j), fwi, np.concatenate([ar(j * 128, j * 128 + 128), FFN + ar(j * 128, j * 128 + 128)]))
                for dj in range(16):
                    put(("fo", l, s, hf, dj), fwo[hf * 2816:(hf + 1) * 2816], ar(dj * 128, dj * 128 + 128))
        for h in range(6):
            put(("nak", l, h), w_in, O_NAK + h * 128 + ar(128))
        for t in range(2):
            put(("ckv", l, t), w_in, O_CKV + t * 256 + ar(256))
        for h in range(5):
            put(("ukvk", l, h), ukv, h * 256 + ar(128))
        put(("kr", l), w_in, np.concatenate([O_KR + ar(64), O_KR + p64]))
        for h in range(5):
            put(("dk", l, h, 0), w_in, O_DK + h * 128 + ar(128))
            put(("dk", l, h, 1), w_in, O_DK + h * 128 + p128)
        for t, (a, n) in enumerate(VT_NA):
            put(("nav", l, t), w_in, O_NAV + a + ar(n))
        for t, (a, n) in enumerate(VT_D):
            put(("dv", l, t), w_in, O_DV + a + ar(n))
        vcols = np.concatenate([h * 256 + 128 + ar(128) for h in range(5)])
        for t, (a, n) in enumerate(VT_M):
            put(("mv", l, t), ukv, vcols[a:a + n])
        for h in range(6):
            put(("naq", l, h), w_in, O_NAQ + h * 128 + ar(128))
        for t in range(3):
            put(("cq", l, t), w_in, O_CQ + t * 256 + ar(256))
        for h in range(5):
            put(("uqn", l, h), uq, h * 192 + ar(128))
            put(("uqr", l, h), uq, np.concatenate([h * 192 + 128 + ar(64), h * 192 + 128 + p64]))
        for h in range(5):
            put(("dq", l, h, 0), w_in, O_DQ + h * 128 + ar(128))
            put(("dq", l, h, 1), w_in, O_DQ + h * 128 + p128)
        wb = inp["w_branch"][l]
        wo = inp["w_out"][l]
        for dj in range(16):
            for br, o in enumerate((O_GA, O_GB, O_GD)):
                put(("g", l, dj, br), w_in, o + dj * 128 + ar(128))
            put(("wb", l, dj), wb, dj * 128 + ar(128))
        for half in range(2):
            for t in range(8):
                put(("wo", l, half, t), wo[half * 1024:(half + 1) * 1024], t * 256 + ar(256))
    return np.ascontiguousarray(np.concatenate(tiles, axis=1)), dirn


def fm(v):
    return np.ascontiguousarray(v.reshape(-1, 128).T)


def build_vecs(inp):
    cols = []
    for l in range(2):
        for n in range(3):
            cols.append(fm(inp["norm_w"][l, n]))
    cols.append(fm(inp["final_norm"]))
    for l in range(2):
        cols.append(fm(inp["mla_q_norm"][l]))
    for l in range(2):
        cols.append(fm(inp["mla_kv_norm"][l]))
    for l in range(2):
        cols.append(fm(inp["diff_subln"][l]))
    return np.ascontiguousarray(np.concatenate(cols, axis=1).astype(np.float32))


V_NW, V_FN, V_QN, V_KVN, V_SUB = 0, 96, 112, 124, 132
NV = 134


def rope_table(rq):
    theta = 10000.0
    quarter = 16
    freqs = (np.float32(theta) ** (-np.arange(quarter, dtype=np.float32) / np.float32(quarter))).astype(np.float32)
    g = rq * TL + np.arange(TL)
    rows = (g // 64).astype(np.float32)
    colsp = (g % 64).astype(np.float32)
    tab = np.zeros((64, 2, T), np.float32)
    tab[:, 0, TL:] = 1.0
    for d in range(64):
        pos = rows if d < 32 else colsp
        ang = (pos * freqs[d % 16]).astype(np.float32)
        tab[d, 0, :TL] = np.cos(ang)
        sn = np.sin(ang)
        tab[d, 1, :TL] = -sn if (d % 32) < 16 else sn
    return np.ascontiguousarray(np.concatenate([tab, tab], axis=0))


def na_bias(rpb, rq):
    out = np.full((2, 6, 128, NA_OFF[-1], 64), NEG, np.float32)
    qc = np.arange(64)
    kc = np.arange(64)
    c0 = np.clip(qc - 8, 0, 48)
    col_ok = (kc[:, None] >= c0[None, :]) & (kc[:, None] < c0[None, :] + 16)
    col_off = np.clip(kc[:, None] - qc[None, :], -15, 15) + 15
    for r in range(16):
        gr = 16 * rq + r
        r0 = min(max(gr - 4, 0), 56)
        clo, chi = NA_CH[r]
        for ci, c in enumerate(range(clo, chi + 1)):
            for kk in range(2):
                kr = 16 * rq + (-4 + 2 * c + kk)
                if kr < 0 or kr >= 64 or kr < r0 or kr >= r0 + 8:
                    continue
                ro = kr - gr + 7
                vals = rpb[:, :, ro, :][:, :, col_off]
                blk = out[:, :, kk * 64:(kk + 1) * 64, NA_OFF[r] + ci, :]
                blk[...] = np.where(col_ok[None, None], vals, NEG)
    return out


class Op:
    __slots__ = ("eng", "fn", "deps", "pos", "awaited", "kind", "idx", "waits", "sem", "semval", "ccg")

    def __init__(self, eng, fn, kind):
        self.eng = eng
        self.fn = fn
        self.kind = kind
        self.deps = []
        self.awaited = False
        self.waits = []
        self.sem = None
        self.semval = 0


ENGS = ("pe", "act", "dve", "pool", "sp")
NDSEM = 16
SEM_CH = 4000


class Prog:
    def __init__(self, nc):
        self.nc = nc
        self.ops = []
        self.eng_ops = {e: [] for e in ENGS}
        self.recs = {}
        self.ndma = 0
        self.dma_ops = []
        self.skip = set()

    @staticmethod
    def _dsz(dt):
        return mybir.dt.size(dt)

    def region(self, ap):
        t = ap.tensor
        name = t.name
        if name in self.skip:
            return None
        pat = ap.ap
        rng = t.manual_sbuf_range
        if rng is not None:
            ps = 1
            for s in t.shape[1:]:
                ps *= int(s)
            lo = int(ap.offset) % ps
            ext = 1
            for (st, cnt) in pat[1:]:
                ext += (int(cnt) - 1) * abs(int(st))
            esz = self._dsz(t.dtype)
            return ("sb", rng[0] + lo * esz, rng[0] + (lo + ext) * esz)
        tn = type(t).__name__
        if tn.startswith("PSum") or tn.startswith("SB"):
            ps = 1
            for s in t.shape[1:]:
                ps *= int(s)
            lo = int(ap.offset) % ps
            ext = 1
            for (st, cnt) in pat[1:]:
                ext += (int(cnt) - 1) * abs(int(st))
            return (name, lo, lo + ext)
        lo = int(ap.offset)
        ext = 1
        for (st, cnt) in pat:
            ext += (int(cnt) - 1) * abs(int(st))
        return (name, lo, lo + ext)

    def _pages(self, sp, lo, hi):
        pg = 2048 if sp == "sb" else 65536
        return range(lo // pg, (hi - 1) // pg + 1)

    def add(self, eng, fn, reads=(), writes=(), kind="c", ccg=0):
        op = Op(eng, fn, kind)
        op.ccg = ccg
        op.idx = len(self.ops)
        deps = {}
        rkey = (eng + kind) if kind == "c" else ("d", op.idx)
        for ap in reads:
            rg = self.region(ap)
            if rg is None:
                continue
            sp, lo, hi = rg
            pages = self.recs.setdefault(sp, {})
            found = None
            for pgi in self._pages(sp, lo, hi):
                for rec in pages.get(pgi, ()):
                    if rec[0] < hi and lo < rec[1]:
                        w = rec[2]
                        if w is not None:
                            deps[w.idx] = (w, "raw")
                        if rec[0] == lo and rec[1] == hi:
                            found = rec
            if found is None:
                found = [lo, hi, None, {}]
                for pgi in self._pages(sp, lo, hi):
                    pages.setdefault(pgi, []).append(found)
            found[3][rkey] = op
        for ap in writes:
            rg = self.region(ap)
            if rg is None:
                continue
            sp, lo, hi = rg
            pages = self.recs.setdefault(sp, {})
            newrec = [lo, hi, op, {}]
            for pgi in self._pages(sp, lo, hi):
                lst = pages.get(pgi)
                if lst is None:
                    pages[pgi] = [newrec]
                    continue
                keep = []
                for rec in lst:
                    if rec[0] < hi and lo < rec[1]:
                        w = rec[2]
                        if w is not None and w.idx not in deps:
                            deps[w.idx] = (w, "waw")
                        for rd in rec[3].values():
                            if rd is not op and rd.idx not in deps:
                                deps[rd.idx] = (rd, "war")
                        if lo <= rec[0] and rec[1] <= hi:
                            continue
                    keep.append(rec)
                keep.append(newrec)
                pages[pgi] = keep
        if kind != "c":
            if kind == "d":
                op.sem = ("d", self.ndma % NDSEM)
                op.semval = 16 * (self.ndma // NDSEM + 1)
                if self.ndma >= NDSEM:
                    prev = self.dma_ops[self.ndma - NDSEM]
                    deps.setdefault(prev.idx, (prev, "raw"))
                self.dma_ops.append(op)
                self.ndma += 1
        for (d, typ) in deps.values():
            if d.kind == "c" and d.eng == eng:
                if eng == "pe":
                    continue
                if typ != "raw" and kind == "c":
                    continue
            op.deps.append(d)
        op.pos = len(self.eng_ops[eng])
        self.eng_ops[eng].append(op)
        self.ops.append(op)
        return op

    def finalize(self, stack):
        nc = self.nc
        known = {e: {f: -1 for f in ENGS} for e in ENGS}
        knownd = {e: set() for e in ENGS}
        for op in self.ops:
            e = op.eng
            for d in op.deps:
                if d.kind == "c":
                    if d.pos <= known[e][d.eng]:
                        continue
                    known[e][d.eng] = d.pos
                    d.awaited = True
                    op.waits.append(d)
                else:
                    if d.idx in knownd[e]:
                        continue
                    knownd[e].add(d.idx)
                    op.waits.append(d)
        esems = {}
        for e in ENGS:
            cnt = 0
            for op in self.eng_ops[e]:
                if op.kind == "c" and op.awaited:
                    op.sem = (e, cnt // SEM_CH)
                    op.semval = cnt % SEM_CH + 1
                    cnt += 1
            esems[e] = [stack.enter_context(nc.semaphore(f"s_{e}_{i}")) for i in range(cnt // SEM_CH + 1)]
        dsems = [stack.enter_context(nc.semaphore(f"s_dma_{i}")) for i in range(NDSEM)]
        ccsems = [stack.enter_context(nc.semaphore(f"s_cc_{i}")) for i in range(2)]
        ccops = [op for op in self.ops if op.kind == "cc"]
        for op in ccops:
            if op.ccg == 0:
                op.sem = ("cc", 0)
                op.semval = sum(1 for o in ccops if o.ccg == 0)
            else:
                op.sem = ("cc", 1)
                op.semval = sum(1 for o in ccops if 0 < o.ccg <= op.ccg)

        def semof(op):
            kind, i = op.sem
            if kind == "d":
                return dsems[i]
            if kind == "cc":
                return ccsems[i]
            return esems[kind][i]

        def emit(ename, eng):
            for op in self.eng_ops[ename]:
                for d in op.waits:
                    eng.wait_ge(semof(d), d.semval)
                if op.fn is None:
                    continue
                inst = op.fn(eng)
                if op.kind == "d":
                    inst.then_inc(semof(op), 16)
                elif op.kind == "cc":
                    inst.then_inc(semof(op))
                elif op.awaited:
                    inst.then_inc(semof(op), 1)

        block = stack.enter_context(nc.Block())

        @block.tensor
        def _(eng):
            emit("pe", eng)

        @block.scalar
        def _(eng):
            emit("act", eng)

        @block.vector
        def _(eng):
            emit("dve", eng)

        @block.gpsimd
        def _(eng):
            emit("pool", eng)

        @block.sync
        def _(eng):
            emit("sp", eng)


class StopBuild(Exception):
    pass


class Builder:
    def __init__(self, dirn, wtot):
        self.dirn = dirn
        self.wtot = wtot
        self.nc = nc = bass.Bass("TRN2", target_bir_lowering=False)
        self.P = Prog(nc)
        P = self.P
        dt = nc.dram_tensor
        self.xT = dt("xT", [D, T], F32, kind="ExternalInput").ap()
        self.cT = dt("cT", [128, KC * 2], F32, kind="ExternalInput").ap()
        if FAKE:
            wtot = self.wtot = 8192
        self.adaw = dt("adaw", [1 if FAKE else 72, 128, 2048], F32, kind="ExternalInput").ap()
        self.adab = dt("adab", [1, 72 * 128], F32, kind="ExternalInput").ap()
        self.wbig = dt("wbig", [128, wtot], F32, kind="ExternalInput").ap()
        self.vecs_d = dt("vecs", [128, NV], F32, kind="ExternalInput").ap()
        self.dlam_d = dt("dlam", [1, 512], F32, kind="ExternalInput").ap()
        self.rope_d = dt("rope", [128, 2 * T], F32, kind="ExternalInput").ap()
        self.nab_d = dt("nab", [1 if FAKE else 12, 128, NA_OFF[-1] * 64], F32, kind="ExternalInput").ap()
        self.sel_d = dt("sel", [128, 16], F32, kind="ExternalInput").ap()
        self.ident_d = dt("ident", [128, 128], F32, kind="ExternalInput").ap()
        for n in ("xT", "cT", "adaw", "adab", "wbig", "vecs", "dlam", "rope", "nab", "sel", "ident"):
            P.skip.add(n)
        self.out = dt("out", [TL, D], F32, kind="ExternalOutput").ap()
        self.dbg = None
        if DEBUG_STOP:
            self.dbg = dt("dbg", [128, KC * T], F32, kind="ExternalOutput").ap()
        self.cinK = [dt(f"cinK{i}", [n, T], BF16) for i, n in enumerate(KCH_ROWS)]
        self.coutK = [dt(f"coutK{i}", [4 * n, T], BF16) for i, n in enumerate(KCH_ROWS)]
        self.cinV = [dt(f"cinV{i}", [T, n], BF16) for i, n in enumerate(VCH_COLS)]
        self.coutV = [dt(f"coutV{i}", [4 * T, n], BF16) for i, n in enumerate(VCH_COLS)]
        self.cinA = dt("cinA", [128, 144], F32)
        self.coutA = dt("coutA", [4 * 128, 144], F32)
        arena = nc.alloc_sbuf_tensor("arena", [128, 212736], mybir.dt.uint8)
        self.abase = int(nc.lookup_mloc(arena).addr)
        self.ncnt = 0
        self.psum = [nc.alloc_psum_tensor(f"ps{i}", [128, 512], F32) for i in range(8)]
        self.OX = 0
        self.OB = 69632
        self.OC = self.OB + 34816
        self.OD = self.OC + 47872
        self.X = self.sb("X", [128, KC, T], F32, self.OX)
        self.H = self.sb("H", [128, KC, T], BF16, self.OB)
        self.OT = self.sb("OT", [128, KC, T], BF16, self.OB)
        self.U = self.sb("U", [128, 22, T], BF16, self.OC)
        self.H2 = self.sb("H2", [128, KC, T], BF16, self.OC)
        self.QNA = self.sb("QNA", [128, 6, T], BF16, self.OC)
        self.QMN = self.sb("QMN", [128, 5, T], BF16, self.OC + 13056)
        self.QMR = self.sb("QMR", [128, 5, T], BF16, self.OC + 13056 + 10880)
        self.QD = self.sb("QD", [128, 5, T], BF16, self.OC + 13056 + 21760)
        self.CQN = self.sb("CQN", [128, 6, T], BF16, self.OC + 13056 + 21760)
        o = self.OD
        self.MALL = self.sb("MALL", [128, 4, 144], F32, o); o += 3456
        self.MODS = self.sb("MODS", [128, 2, 144], F32, o); o += 1152
        self.VECS = self.sb("VECS", [128, NV], F32, o); o += 544
        self.AV = self.sb("AV", [128, 2, 16], F32, o); o += 128
        self.BV = self.sb("BV", [128, 2, 16], F32, o); o += 128
        self.GV = self.sb("GV", [128, 2, 16], F32, o); o += 128
        self.SEL = self.sb("SEL", [128, 16], F32, o); o += 64
        self.NEGLAM = self.sb("NEGLAM", [128, 4], F32, o); o += 32
        self.SUBW = self.sb("SUBW", [128, 2], F32, o); o += 32
        self.EPS_AP = self.sb("EPSC", [128, 1], F32, o); o += 32
        self.ONESB = self.sb("ONESB", [128, 128], BF16, o); o += 256
        self.ONESF = self.sb("ONESF", [128, 128], F32, o); o += 512
        self.o_scr = o
        self.RS = self.sb("RS", [128, T], F32, o); o += 4352
        self.TMPF = self.sb("TMPF", [128, T], F32, o); o += 4352
        self.SQ = self.sb("SQ", [128, 2, 512], BF16, o); o += 2048
        self.SA = self.sb("SA", [128, 2, 512], F32, o); o += 4096
        o_x = o
        o += 2048 + 4096
        self.ROPE = self.sb("ROPE", [128, 2, T], BF16, self.o_scr + 4352)
        self.KST = [self.sb(f"KST{i}", [128, T], BF16, self.o_scr + 10752 + i * 2176) for i in range(2)]
        self.VST = [self.sb(f"VST{i}", [128, 384], BF16, self.o_scr + 10752 + 4352 + i * 768) for i in range(2)]
        self.RT = [self.sb(f"RT{i}", [128, 512], F32, o_x + 2048 + i * 2048) for i in range(2)]
        self.CKVN = self.sb("CKVN", [128, 4, T], BF16, self.OC + 13056 + 21760)
        a = self.o_scr
        self.PT = [self.sb(f"PT{i}", [128, 512], BF16, a + i * 1024) for i in range(4)]; a += 4096
        self.RC = [self.sb(f"RC{i}", [128, 512], F32, a + i * 2048) for i in range(2)]; a += 4096
        self.AO = [self.sb(f"AO{i}", [128, 512], F32, a + i * 2048) for i in range(2)]; a += 4096
        self.TBI = self.sb("TBI", [128, 384], F32, a); a += 1536
        self.TBI2 = self.sb("TBI2", [128, 384], F32, self.o_scr + 17920)
        self.KCX = [self.sb(f"KCX{i}", [128, 2, 4, 64], BF16, a + i * 1024) for i in range(2)]; a += 2048
        self.VCX = [self.sb(f"VCX{i}", [64, 4, 128], BF16, a + i * 1024) for i in range(2)]; a += 2048
        assert a <= o, (a, o)
        a = self.o_scr + 4352
        self.SG = [self.sb(f"SG{i}", [128, 512], F32, a + i * 2048) for i in range(3)]; a += 6144
        self.MT = [self.sb(f"MT{i}", [128, 512], F32, a + i * 2048) for i in range(2)]; a += 4096
        assert a <= o
        self.YQ = self.sb("YQ", [128, 4, T], BF16, self.OC + 34816)
        self.o_ring = o
        self.NSLOT = 4
        self.ring = [self.sb(f"ring{i}", [128, 4096], BF16, o + i * 8192) for i in range(self.NSLOT)]
        self.ringf = [self.sb(f"ringf{i}", [128, 2048], F32, o + i * 8192) for i in range(self.NSLOT)]
        o += self.NSLOT * 8192
        self.o_end = o
        assert o <= 212736, o
        self.wi = 0
        self.psi = 0
        self.pending = []

    def sb(self, name, shape, dt, off):
        self.ncnt += 1
        return self.nc.alloc_sbuf_tensor_at(f"{name}_{self.ncnt}", shape, dt, offset=self.abase + off)

    def mm(self, out, lhsT, rhs, start=True, stop=True):
        self.P.add("pe", lambda e: e.matmul(out, lhsT, rhs, start=start, stop=stop), [lhsT, rhs], [out])

    def act(self, out, in_, func, bias=None, scale=None):
        kw = {}
        rd = [in_]
        if bias is not None:
            kw["bias"] = bias
            if not isinstance(bias, (int, float)):
                rd.append(bias)
        if scale is not None:
            kw["scale"] = scale
            if not isinstance(scale, (int, float)):
                rd.append(scale)
        self.P.add("act", lambda e: e.activation(out, in_, func, **kw), rd, [out])

    def tt(self, out, in0, in1, op):
        self.P.add("dve", lambda e: e.tensor_tensor(out, in0, in1, op), [in0, in1], [out])

    def ts(self, out, in0, s1, s2, op0, op1=None):
        rd = [in0] + [s for s in (s1, s2) if s is not None and not isinstance(s, (int, float))]
        if op1 is None:
            self.P.add("dve", lambda e: e.tensor_scalar(out, in0, s1, None, op0), rd, [out])
        else:
            self.P.add("dve", lambda e: e.tensor_scalar(out, in0, s1, s2, op0, op1), rd, [out])

    def stt(self, out, in0, scalar, in1, op0, op1):
        rd = [in0, in1] + ([] if isinstance(scalar, (int, float)) else [scalar])
        self.P.add("dve", lambda e: e.scalar_tensor_tensor(out, in0, scalar, in1, op0, op1), rd, [out])

    def recip(self, out, in_):
        self.P.add("dve", lambda e: e.reciprocal(out, in_), [in_], [out])

    def vcopy(self, out, in_):
        self.P.add("dve", lambda e: e.tensor_copy(out, in_), [in_], [out])

    def memset(self, ap, v):
        self.P.add("dve", lambda e: e.memset(ap, v), [], [ap])

    def dma(self, out, in_, eng="sp"):
        return self.P.add(eng, lambda e: e.dma_start(out=out, in_=in_), [in_], [out], kind="d")

    def bank(self):
        b = self.psum[self.psi % 4]
        self.psi += 1
        return b

    def wload(self, key):
        off, kc, n = self.dirn[key]
        if FAKE:
            off = 0
        slot = self.ring[self.wi % self.NSLOT]
        self.wi += 1
        v = slot[:, 0:kc * n]
        self.dma(v, self.wbig[:, off:off + kc * n], eng="pool")
        self.tick()
        return v.rearrange("p (k c) -> p k c", k=kc)

    def tick(self, flush=False):
        keep = []
        for it in self.pending:
            it[0] -= 1
            if it[0] <= 0 or flush:
                it[1]()
            else:
                keep.append(it)
        self.pending = keep

    def setup(self):
        self.memset(self.ONESB[:, :], 1.0)
        self.memset(self.ONESF[:, :], 1.0)
        self.dma(self.X[:, :, :], self.xT.rearrange("(j p) t -> p j t", p=128))
        self.dma(self.VECS[:, :], self.vecs_d[:, :])
        self.dma(self.SEL[:, :], self.sel_d[:, :])
        CS = self.sb("CS", [128, KC * 2], F32, self.o_scr)
        BAD = self.sb("BAD", [1, 72 * 128], F32, self.OC)
        ADAL = self.sb("ADAL", [128, 144], F32, self.o_scr + 192)
        self.dma(CS[:, :], self.cT[:, :])
        self.dma(BAD[:, :], self.adab[:, :])
        self.act(CS[:, :], CS[:, :], AF.Silu)
        CS3 = CS[:, :].rearrange("p (k v) -> p k v", v=2)
        for t in range(72):
            wt = self.ringf[self.wi % self.NSLOT]
            self.wi += 1
            self.dma(wt[:, :], self.adaw[0 if FAKE else t])
            w3 = wt[:, :].rearrange("p (k c) -> p k c", k=KC)
            ps = self.bank()
            for k in range(KC):
                self.mm(ps[:, 0:2], w3[:, k, :], CS3[:, k, :], start=(k == 0), stop=False)
            self.mm(ps[:, 0:2], BAD[0:1, t * 128:(t + 1) * 128], self.ONESF[0:1, 0:2], start=False, stop=True)
            self.vcopy(ADAL[:, t * 2:(t + 1) * 2], ps[:, 0:2])
        self.dma(self.cinA.ap(), ADAL[:, :])
        self.P.add("pool", lambda e: e.collective_compute(
            "AllGather", ALU.bypass, replica_groups=[[0, 1, 2, 3], [4, 5, 6, 7]],
            ins=[self.cinA.ap().opt()], outs=[self.coutA.ap().opt()]),
            [self.cinA.ap()], [self.coutA.ap()], kind="cc")
        self.dma(self.MALL[:, :, :], self.coutA.ap().rearrange("(i p) c -> p i c", p=128))

    def load_mods(self, l):
        for cls in range(2):
            src = self.MALL[:, :, l * 72 + cls:l * 72 + 72:2]
            dst = self.MODS[:, cls, :].rearrange("p (i j) -> p i j", i=4)
            self.vcopy(dst, src)

    def mod(self, cls, n, j=None):
        if j is None:
            return self.MODS[:, cls, n * 16:(n + 1) * 16]
        return self.MODS[:, cls, n * 16 + j:n * 16 + j + 1]

    def rms_stats(self, nfeat_scale=1.0 / D):
        for (t0, t1) in TBS:
            n = t1 - t0
            ps = self.psum[7]
            for j in range(KC):
                sq = self.SQ[:, j % 2, 0:n]
                self.act(sq, self.X[:, j, t0:t1], AF.Square)
                self.mm(ps[:, 0:n], self.ONESB[:, :], sq, start=(j == 0), stop=(j == KC - 1))
            self.act(self.RS[:, t0:t1], ps[:, 0:n], AF.Sqrt, bias=self.EPS_AP[:, 0:1], scale=nfeat_scale)
            self.recip(self.RS[:, t0:t1], self.RS[:, t0:t1])

    def modulate(self, l, nidx, n_shift, n_scale, dst):
        g = self.VECS[:, V_NW + (l * 3 + nidx) * 16:V_NW + (l * 3 + nidx + 1) * 16]
        for cls in range(2):
            self.ts(self.AV[:, cls, :], self.mod(cls, n_scale), 1.0, None, ALU.add)
            self.tt(self.AV[:, cls, :], self.AV[:, cls, :], g, ALU.mult)
        for (t0, t1) in TBS:
            n = t1 - t0
            ps = self.psum[7]
            for j in range(KC):
                sq = self.SQ[:, j % 2, 0:n]
                self.act(sq, self.X[:, j, t0:t1], AF.Square)
                self.mm(ps[:, 0:n], self.ONESB[:, :], sq, start=(j == 0), stop=(j == KC - 1))
            self.act(self.RS[:, t0:t1], ps[:, 0:n], AF.Sqrt, bias=self.EPS_AP[:, 0:1], scale=1.0 / D)
            self.recip(self.RS[:, t0:t1], self.RS[:, t0:t1])
            for j in range(KC):
                buf = self.TMPF[:, (j % 2) * 384:(j % 2) * 384 + n]
                for (a, b, cls) in ((t0, min(t1, TL), 0), (max(t0, TL), t1, 1)):
                    if b <= a:
                        continue
                    self.tt(buf[:, a - t0:b - t0], self.X[:, j, a:b], self.RS[:, a:b], ALU.mult)
                    self.act(dst[:, j, a:b], buf[:, a - t0:b - t0], AF.Identity,
                             bias=self.mod(cls, n_shift, j), scale=self.AV[:, cls, j:j + 1])

    def gate_vec(self, n_gate, factor):
        for cls in range(2):
            self.ts(self.GV[:, cls, :], self.mod(cls, n_gate), float(factor), None, ALU.mult)

    def resid_add(self, dj, t0, t1, ps):
        for (a, b, cls) in ((t0, min(t1, TL), 0), (max(t0, TL), t1, 1)):
            if b <= a:
                continue
            self.stt(self.X[:, dj, a:b], ps[:, a - t0:b - t0], self.GV[:, cls, dj:dj + 1], self.X[:, dj, a:b],
                     ALU.mult, ALU.add)

    def ffn(self, l, s, nidx, n0, tbs=None):
        tbs = tbs or TBS
        self.modulate(l, nidx, n0, n0 + 1, self.H)
        self.gate_vec(n0 + 2, 0.5)
        for hf in range(2):
            for jj in range(22):
                wt = self.wload(("fi", l, s, hf * 22 + jj))
                for (t0, t1) in tbs:
                    n = t1 - t0
                    pa = self.bank()
                    pb = self.bank()
                    for k in range(KC):
                        self.mm(pa[:, 0:n], wt[:, k, 0:128], self.H[:, k, t0:t1], start=(k == 0), stop=(k == KC - 1))
                    for k in range(KC):
                        self.mm(pb[:, 0:n], wt[:, k, 128:256], self.H[:, k, t0:t1], start=(k == 0), stop=(k == KC - 1))
                    sa = self.SA[:, (self.psi // 2) % 2, 0:n]
                    self.act(sa, pa[:, 0:n], AF.Silu)
                    self.tt(self.U[:, jj, t0:t1], sa, pb[:, 0:n], ALU.mult)
            for dj in range(KC):
                wt = self.wload(("fo", l, s, hf, dj))
                for (t0, t1) in tbs:
                    n = t1 - t0
                    ps = self.bank()
                    for k in range(22):
                        self.mm(ps[:, 0:n], wt[:, k, :], self.U[:, k, t0:t1], start=(k == 0), stop=(k == 21))
                    self.resid_add(dj, t0, t1, ps)


    def proj_fm(self, wt, c0, ncol, kc, src, ps, t0, t1):
        n = t1 - t0
        for k in range(kc):
            self.mm(ps[0:ncol, 0:n], wt[:, k, c0:c0 + ncol], src[:, k, t0:t1], start=(k == 0), stop=(k == kc - 1))

    def rope_evac(self, dst, ps1, ps2, np_, t0, t1):
        n = t1 - t0
        self.tt(self.RT[0][0:np_, 0:n], ps1[0:np_, 0:n], self.ROPE[0:np_, 0, t0:t1], ALU.mult)
        self.tt(self.RT[1][0:np_, 0:n], ps2[0:np_, 0:n], self.ROPE[0:np_, 1, t0:t1], ALU.mult)
        self.tt(dst, self.RT[0][0:np_, 0:n], self.RT[1][0:np_, 0:n], ALU.add)

    def mla_norm(self, l, tiles, nch, wcol0, dst, tbs=None):
        tbs = tbs or TBS
        for (t0, t1) in tbs:
            n = t1 - t0
            pss = []
            for c in range(nch):
                ps = self.psum[c]
                wt = tiles[c // 2]
                self.proj_fm(wt, (c % 2) * 128, 128, KC, self.H, ps, t0, t1)
                pss.append(ps)
            pn = self.psum[7]
            for c in range(nch):
                sq = self.SQ[:, c % 2, 0:n]
                self.act(sq, pss[c][:, 0:n], AF.Square)
                self.mm(pn[:, 0:n], self.ONESB[:, :], sq, start=(c == 0), stop=(c == nch - 1))
            self.act(self.RS[:, t0:t1], pn[:, 0:n], AF.Sqrt, bias=self.EPS_AP[:, 0:1], scale=1.0 / (nch * 128))
            self.recip(self.RS[:, t0:t1], self.RS[:, t0:t1])
            for c in range(nch):
                self.stt(dst[:, c, t0:t1], pss[c][:, 0:n], self.VECS[:, wcol0 + c:wcol0 + c + 1],
                         self.RS[:, t0:t1], ALU.mult, ALU.mult)

    def k_phase(self, l):
        def kdst(br, h, nrows=128):
            ci, r0 = kvloc(br, h)
            return self.cinK[ci].ap()[r0:r0 + nrows, :]

        self.dma(self.ROPE[:, :, :].rearrange("p a t -> p (a t)"), self.rope_d[:, :], eng="pool")
        ki = [0]
        vi = [0]
        grp = [[0, 1, 2, 3], [4, 5, 6, 7]]
        nocc = DEBUG_STOP.endswith("_nocc")

        def fire(kind, i):
            ci, co = (self.cinK[i], self.coutK[i]) if kind == "k" else (self.cinV[i], self.coutV[i])
            self.P.add("pool", lambda e, ci=ci, co=co: e.collective_compute(
                "AllGather", ALU.bypass, replica_groups=grp,
                ins=[ci.ap().opt()], outs=[co.ap().opt()]),
                [ci.ap()], [co.ap()], kind="cc", ccg=l + 1)

        def trigger(kind, i):
            if nocc:
                return
            self.pending.append([3, lambda: fire(kind, i)])

        def kst():
            ki[0] += 1
            return self.KST[ki[0] % 2]

        def vproj(key, src, kc, chunk, col0):
            wt = self.wload(key)
            ncol = self.dirn[key][2]
            dst = self.cinV[chunk].ap()
            for tc in range(9):
                t0 = tc * 128
                nt = min(128, T - t0)
                ps = self.bank()
                for k in range(kc):
                    self.mm(ps[0:nt, 0:ncol], src[:, k, t0:t0 + nt], wt[:, k, :], start=(k == 0), stop=(k == kc - 1))
                vi[0] += 1
                vs = self.VST[vi[0] % 2]
                self.act(vs[0:nt, 0:ncol], ps[0:nt, 0:ncol], AF.Copy)
                self.dma(dst[t0:t0 + nt, col0:col0 + ncol], vs[0:nt, 0:ncol])

        for h in range(6):
            wt = self.wload(("nak", l, h))
            st = kst()
            for (t0, t1) in TBS:
                ps = self.bank()
                self.proj_fm(wt, 0, 128, KC, self.H, ps, t0, t1)
                self.act(st[:, t0:t1], ps[:, 0:t1 - t0], AF.Copy)
            self.dma(kdst("na", h), st[:, :])
            if h % 3 == 2:
                trigger("k", h // 3)
        for t, (a, n) in enumerate(VT_NA):
            vproj(("nav", l, t), self.H, KC, a // 384, a % 384)
            if t % 2 == 1:
                trigger("v", t // 2)
        tiles = [self.wload(("ckv", l, t)) for t in range(2)]
        self.mla_norm(l, tiles, 4, V_KVN + l * 4, self.CKVN)
        for h in range(5):
            wt = self.wload(("ukvk", l, h))
            st = kst()
            for (t0, t1) in TBS:
                ps = self.bank()
                self.proj_fm(wt, 0, 128, 4, self.CKVN, ps, t0, t1)
                self.act(st[:, t0:t1], ps[:, 0:t1 - t0], AF.Copy)
            self.dma(kdst("mla", h), st[:, :])
            if h == 2:
                trigger("k", 2)
        wt = self.wload(("kr", l))
        st = kst()
        for (t0, t1) in TBS:
            p1 = self.bank()
            p2 = self.bank()
            self.proj_fm(wt, 0, 64, KC, self.H, p1, t0, t1)
            self.proj_fm(wt, 64, 64, KC, self.H, p2, t0, t1)
            self.rope_evac(st[0:64, t0:t1], p1, p2, 64, t0, t1)
        self.dma(kdst("kr", 0, 64), st[0:64, :])
        trigger("k", 3)
        for t, (a, n) in enumerate(VT_M):
            vproj(("mv", l, t), self.CKVN, 4, 2 + a // 384, a % 384)
            trigger("v", 2 + t)
        for h in range(5):
            w0 = self.wload(("dk", l, h, 0))
            w1 = self.wload(("dk", l, h, 1))
            st = kst()
            for (t0, t1) in TBS:
                p1 = self.bank()
                p2 = self.bank()
                self.proj_fm(w0, 0, 128, KC, self.H, p1, t0, t1)
                self.proj_fm(w1, 0, 128, KC, self.H, p2, t0, t1)
                self.rope_evac(st[:, t0:t1], p1, p2, 128, t0, t1)
            self.dma(kdst("diff", h), st[:, :])
            if h == 2 or h == 4:
                trigger("k", 4 + h // 3)
        for t, (a, n) in enumerate(VT_D):
            vproj(("dv", l, t), self.H, KC, 4 + a // 384, a % 384)
            if t >= 1:
                trigger("v", 4 + t - 1)

    def q_phase(self, l):
        for h in range(6):
            wt = self.wload(("naq", l, h))
            for (t0, t1) in self.cur_tbs:
                ps = self.bank()
                self.proj_fm(wt, 0, 128, KC, self.H, ps, t0, t1)
                self.act(self.QNA[:, h, t0:t1], ps[:, 0:t1 - t0], AF.Copy)
        tiles = [self.wload(("cq", l, t)) for t in range(3)]
        self.mla_norm(l, tiles, 6, V_QN + l * 6, self.CQN, self.cur_tbs)
        for h in range(5):
            wn = self.wload(("uqn", l, h))
            wr = self.wload(("uqr", l, h))
            for (t0, t1) in self.cur_tbs:
                ps = self.bank()
                self.proj_fm(wn, 0, 128, 6, self.CQN, ps, t0, t1)
                self.act(self.QMN[:, h, t0:t1], ps[:, 0:t1 - t0], AF.Copy)
                p1 = self.bank()
                p2 = self.bank()
                self.proj_fm(wr, 0, 64, 6, self.CQN, p1, t0, t1)
                self.proj_fm(wr, 64, 64, 6, self.CQN, p2, t0, t1)
                self.rope_evac(self.QMR[0:64, h, t0:t1], p1, p2, 64, t0, t1)
        for h in range(5):
            w0 = self.wload(("dq", l, h, 0))
            w1 = self.wload(("dq", l, h, 1))
            for (t0, t1) in self.cur_tbs:
                p1 = self.bank()
                p2 = self.bank()
                self.proj_fm(w0, 0, 128, KC, self.H, p1, t0, t1)
                self.proj_fm(w1, 0, 128, KC, self.H, p2, t0, t1)
                self.rope_evac(self.QD[:, h, t0:t1], p1, p2, 128, t0, t1)

    def attend(self, nq, chunks, maps, v_fn, scale, obanks, lbanks):
        nm = len(maps)
        sb_i = [0]

        def scores(ci):
            ch = chunks[ci]
            nk = ch[0]
            out = []
            for m in range(nm):
                ps = self.psum[sb_i[0] % 4]
                sb_i[0] += 1
                pieces = maps[m]
                for pi, (kf, q) in enumerate(pieces):
                    self.mm(ps[0:nk, 0:nq], kf(ch), q, start=(pi == 0), stop=(pi == len(pieces) - 1))
                out.append(ps)
            return out

        pend = scores(0)
        pti = 0
        for ci in range(len(chunks)):
            cur = pend
            if ci + 1 < len(chunks):
                pend = scores(ci + 1)
            nk = chunks[ci][0]
            for m in range(nm):
                pt = self.PT[pti % 4]
                pti += 1
                self.act(pt[0:nk, 0:nq], cur[m][0:nk, 0:nq], AF.Exp, scale=float(scale))
                first = ci == 0
                last = ci == len(chunks) - 1
                self.mm(obanks[m][:, 0:nq], v_fn(chunks[ci]), pt[0:nk, 0:nq], start=first, stop=last)
                self.mm(lbanks[m][:, 0:nq], self.ONESB[0:nk, :], pt[0:nk, 0:nq], start=first, stop=last)

    def kview(self, br, h, nrows=128):
        ci, r0 = kvloc(br, h)
        return self.coutK[ci].ap().rearrange("(r k) t -> r k t", r=4)[:, r0:r0 + nrows, :]

    def vview(self, br, h):
        ci, c0 = kvloc(br, h)
        return self.coutV[ci].ap().rearrange("(r t) c -> r t c", r=4)[:, :, c0:c0 + 128]

    def mla_attn(self, l):
        KR = self.ring[0][0:64, :].rearrange("p (r t) -> p r t", r=4)
        ckr = self.kview("kr", 0, 64)
        self.dma(KR, ckr[:, :, 0:TL].rearrange("r d t -> d r t"))
        KRC = self.sb("KRC", [64, 4, 64], BF16, self.o_scr + 20992 - 512)
        self.dma(KRC[:, :, :], ckr[:, :, TL:T].rearrange("r d t -> d r t"))
        for h in range(5):
            KN = self.ring[1 if h % 2 == 0 else 3][:, :].rearrange("p (r t) -> p r t", r=4)
            VV = self.ring[2][:, :].rearrange("p (r n c) -> p r n c", r=4, n=8)
            KC_ = self.KCX[h % 2]
            VC_ = self.VCX[h % 2]
            ck = self.kview("mla", h)
            cv = self.vview("mla", h)
            self.dma(KN, ck[:, :, 0:TL].rearrange("r d t -> d r t"))
            self.dma(KC_[:, 0, :, :], ck[:, :, TL:T].rearrange("r d t -> d r t"))
            for r_ in range(4):
                self.dma(VV[:, r_, :, :], cv[r_, 0:TL, :].rearrange("(n p) c -> p n c", p=128))
            self.dma(VC_[:, :, :], cv[:, TL:T, :].rearrange("r p c -> p r c"))
            lat = [(128, "l", r, n) for r in range(4) for n in range(8)]
            ctxc = [(64, "c", r, 0) for r in range(4)]

            def kn(ch):
                return KN[:, ch[2], ch[3] * 128:(ch[3] + 1) * 128] if ch[1] == "l" else KC_[:, 0, ch[2], :]

            def kr(ch):
                return KR[:, ch[2], ch[3] * 128:(ch[3] + 1) * 128] if ch[1] == "l" else KRC[:, ch[2], :]

            def vf(ch):
                return VV[:, ch[2], ch[3], :] if ch[1] == "l" else VC_[:, ch[2], :]

            for (t0, t1) in (QBS[:2] if l == 1 else QBS):
                nq = t1 - t0
                chunks = (lat + ctxc) if t0 < TL else ctxc
                maps = [[(kn, self.QMN[:, h, t0:t1]), (kr, self.QMR[0:64, h, t0:t1])]]
                self.attend(nq, chunks, maps, vf, MLA_SCALE, [self.psum[4]], [self.psum[5]])
                self.act(self.AO[0][:, 0:nq], self.psum[4][:, 0:nq], AF.Copy)
                self.act(self.RC[0][:, 0:nq], self.psum[5][:, 0:nq], AF.Copy)
                self.recip(self.RC[0][:, 0:nq], self.RC[0][:, 0:nq])
                self.tt(self.OT[:, 6 + h, t0:t1], self.AO[0][:, 0:nq], self.RC[0][:, 0:nq], ALU.mult)

    def diff_attn(self, l, lam_init):
        DL = self.sb("DL", [1, 512], F32, self.o_scr + 16384)
        LS = self.sb("LS", [1, 8], F32, self.o_scr + 16384 + 2048)
        self.dma(DL[:, :], self.dlam_d[:, :])
        b0 = l * 256
        self.tt(DL[0:1, b0:b0 + 64], DL[0:1, b0:b0 + 64], DL[0:1, b0 + 64:b0 + 128], ALU.mult)
        self.tt(DL[0:1, b0 + 128:b0 + 192], DL[0:1, b0 + 128:b0 + 192], DL[0:1, b0 + 192:b0 + 256], ALU.mult)
        self.P.add("dve", lambda e: e.reduce_sum(LS[0:1, 0:1], DL[0:1, b0:b0 + 64], mybir.AxisListType.X),
                   [DL[0:1, b0:b0 + 64]], [LS[0:1, 0:1]])
        self.P.add("dve", lambda e: e.reduce_sum(LS[0:1, 1:2], DL[0:1, b0 + 128:b0 + 192], mybir.AxisListType.X),
                   [DL[0:1, b0 + 128:b0 + 192]], [LS[0:1, 1:2]])
        self.act(LS[0:1, 2:4], LS[0:1, 0:2], AF.Exp)
        self.tt(LS[0:1, 4:5], LS[0:1, 3:4], LS[0:1, 2:3], ALU.subtract)
        self.ts(LS[0:1, 4:5], LS[0:1, 4:5], float(-lam_init), None, ALU.add)
        ps = self.psum[0]
        self.mm(ps[:, 0:1], self.ONESF[0:1, :], LS[0:1, 4:5], start=True, stop=True)
        self.vcopy(self.NEGLAM[:, 0:1], ps[:, 0:1])
        self.ts(self.SUBW[:, 0:1], self.VECS[:, V_SUB + l:V_SUB + l + 1], float(1.0 - lam_init), None, ALU.mult)
        for h in range(5):
            KD = self.ring[h % 2][:, :].rearrange("p (r t) -> p r t", r=4)
            VV = self.ring[2 + h % 2][:, :].rearrange("p (r n c) -> p r n c", r=4, n=8)
            KC_ = self.KCX[h % 2]
            VC_ = self.VCX[h % 2]
            ck = self.kview("diff", h)
            cv = self.vview("diff", h)
            self.dma(KD, ck[:, :, 0:TL].rearrange("r d t -> d r t"))
            self.dma(KC_[:, 0, :, :], ck[:, :, TL:T].rearrange("r d t -> d r t"))
            for r_ in range(4):
                self.dma(VV[:, r_, :, :], cv[r_, 0:TL, :].rearrange("(n p) c -> p n c", p=128))
            self.dma(VC_[:, :, :], cv[:, TL:T, :].rearrange("r p c -> p r c"))
            lat = [(128, "l", r, n) for r in range(4) for n in range(8)]
            ctxc = [(64, "c", r, 0) for r in range(4)]

            def k1(ch):
                return KD[0:64, ch[2], ch[3] * 128:(ch[3] + 1) * 128] if ch[1] == "l" else KC_[0:64, 0, ch[2], :]

            def k2(ch):
                return KD[64:128, ch[2], ch[3] * 128:(ch[3] + 1) * 128] if ch[1] == "l" else KC_[64:128, 0, ch[2], :]

            def vf(ch):
                return VV[:, ch[2], ch[3], :] if ch[1] == "l" else VC_[:, ch[2], :]

            for (t0, t1) in (QBS[:2] if l == 1 else QBS):
                nq = t1 - t0
                chunks = (lat + ctxc) if t0 < TL else ctxc
                maps = [[(k1, self.QD[0:64, h, t0:t1])], [(k2, self.QD[64:128, h, t0:t1])]]
                O1, L1, O2, L2 = self.psum[4], self.psum[5], self.psum[6], self.psum[7]
                self.attend(nq, chunks, maps, vf, DIFF_SCALE, [O1, O2], [L1, L2])
                self.act(self.AO[0][:, 0:nq], O1[:, 0:nq], AF.Copy)
                self.act(self.AO[1][:, 0:nq], O2[:, 0:nq], AF.Copy)
                self.act(self.RC[0][:, 0:nq], L1[:, 0:nq], AF.Copy)
                self.act(self.RC[1][:, 0:nq], L2[:, 0:nq], AF.Copy)
                self.recip(self.RC[0][:, 0:nq], self.RC[0][:, 0:nq])
                self.recip(self.RC[1][:, 0:nq], self.RC[1][:, 0:nq])
                self.tt(self.AO[0][:, 0:nq], self.AO[0][:, 0:nq], self.RC[0][:, 0:nq], ALU.mult)
                self.tt(self.AO[1][:, 0:nq], self.AO[1][:, 0:nq], self.RC[1][:, 0:nq], ALU.mult)
                self.stt(self.AO[0][:, 0:nq], self.AO[1][:, 0:nq], self.NEGLAM[:, 0:1], self.AO[0][:, 0:nq],
                         ALU.mult, ALU.add)
                sq = self.PT[0][:, 0:nq]
                self.act(sq, self.AO[0][:, 0:nq], AF.Square)
                pn = self.psum[0]
                self.mm(pn[:, 0:nq], self.ONESB[:, :], sq, start=True, stop=True)
                self.act(self.RC[0][:, 0:nq], pn[:, 0:nq], AF.Sqrt, bias=self.EPS_AP[:, 0:1], scale=1.0 / 128)
                self.recip(self.RC[0][:, 0:nq], self.RC[0][:, 0:nq])
                self.stt(self.OT[:, 11 + h, t0:t1], self.AO[0][:, 0:nq], self.SUBW[:, 0:1], self.RC[0][:, 0:nq],
                         ALU.mult, ALU.mult)

    def na_attn(self, l):
        base = self.o_ring
        KW = self.sb("KW", [128, 1536], BF16, base)
        KCN = self.sb("KCN", [128, 4, 64], BF16, base + 3072)
        VW = self.sb("VW", [128, 12, 128], BF16, base + 3584)
        VCN = self.sb("VCN", [64, 4, 128], BF16, base + 6656)
        CKP = self.sb("CKP", [128, 4, 256], BF16, base + 8192)
        CKN = self.sb("CKN", [128, 4, 256], BF16, base + 8192 + 2048)
        CVP = self.sb("CVP", [128, 4, 2, 128], BF16, base + 8192 + 4096)
        CVN = self.sb("CVN", [128, 4, 2, 128], BF16, base + 8192 + 6144)
        NB = [self.sb("NB0", [128, 40 * 64], BF16, base + 16384), self.sb("NB1", [128, 38 * 64], BF16, base + 24576)]
        assert NA_OFF[8] == 40 and NA_OFF[16] - NA_OFF[8] == 38
        for h in range(6):
            ci_, o_ = kvloc("na", h)
            ck = self.kview("na", h)
            cv = self.vview("na", h)
            self.dma(KW[:, 256:1280], self.cinK[ci_].ap()[o_:o_ + 128, 0:TL])
            self.dma(CKP[:, :, :], ck[:, :, 768:1024].rearrange("r d t -> d r t"))
            self.dma(CKN[:, :, :], ck[:, :, 0:256].rearrange("r d t -> d r t"))
            self.dma(KCN[:, :, :], ck[:, :, TL:T].rearrange("r d t -> d r t"))
            self.dma(VW[:, 2:10, :], self.cinV[ci_].ap()[0:TL, o_:o_ + 128].rearrange("(n p) c -> p n c", p=128))
            for r_ in range(4):
                self.dma(CVP[:, r_, :, :], cv[r_, 768:1024, :].rearrange("(n p) c -> p n c", p=128))
                self.dma(CVN[:, r_, :, :], cv[r_, 0:256, :].rearrange("(n p) c -> p n c", p=128))
            self.dma(VCN[:, :, :], cv[:, TL:T, :].rearrange("r p c -> p r c"))
            nbsrc = self.nab_d[0 if FAKE else l * 6 + h]
            self.dma(NB[0][:, :], nbsrc[:, 0:40 * 64], eng="pool")
            self.dma(NB[1][:, :], nbsrc[:, 40 * 64:78 * 64], eng="pool")
            for (dst, cand, so) in ((KW[:, 0:256], CKP, 0), (KW[:, 1280:1536], CKN, 4)):
                self.ts(dst, cand[:, 0, :], self.SEL[:, so:so + 1], None, ALU.mult)
                for r in range(1, 4):
                    self.stt(dst, cand[:, r, :], self.SEL[:, so + r:so + r + 1], dst, ALU.mult, ALU.add)
            for (dst, cand, so) in ((VW[:, 0:2, :], CVP, 0), (VW[:, 10:12, :], CVN, 4)):
                self.ts(dst, cand[:, 0, :, :], self.SEL[:, so:so + 1], None, ALU.mult)
                for r in range(1, 4):
                    self.stt(dst, cand[:, r, :, :], self.SEL[:, so + r:so + r + 1], dst, ALU.mult, ALU.add)
            TB2 = [self.TBI, self.TBI2]

            def na_scores(r):
                q = self.QNA[:, h, r * 64:(r + 1) * 64]
                clo, chi = NA_CH[r]
                nch = chi - clo + 1
                ps = self.psum[(2 * r) % 4]
                pc = self.psum[(2 * r + 1) % 4]
                for ci in range(nch):
                    c = clo + ci
                    self.mm(ps[:, ci * 64:(ci + 1) * 64], KW[:, c * 128:(c + 1) * 128], q, start=True, stop=True)
                for rk in range(4):
                    self.mm(pc[0:64, rk * 64:(rk + 1) * 64], KCN[:, rk, :], q, start=True, stop=True)
                return ps, pc

            def na_rest(r, ps, pc):
                r8, rr = r // 8, r % 8
                O, L = self.psum[4 + 2 * r8], self.psum[5 + 2 * r8]
                clo, chi = NA_CH[r]
                nch = chi - clo + 1
                boff = (NA_OFF[r] - NA_OFF[r8 * 8]) * 64
                tb = TB2[r % 2][:, 0:nch * 64]
                self.stt(tb, ps[:, 0:nch * 64], float(NA_SCALE), NB[r8][:, boff:boff + nch * 64], ALU.mult, ALU.add)
                pt = self.PT[r % 2]
                ptc = self.PT[2 + r % 2]
                self.act(pt[:, 0:nch * 64], tb, AF.Exp)
                self.act(ptc[0:64, 0:256], pc[0:64, 0:256], AF.Exp, scale=float(NA_SCALE))
                oc = slice(rr * 64, (rr + 1) * 64)
                for ci in range(nch):
                    c = clo + ci
                    self.mm(O[:, oc], VW[:, c, :], pt[:, ci * 64:(ci + 1) * 64], start=(ci == 0), stop=False)
                    self.mm(L[:, oc], self.ONESB[:, :], pt[:, ci * 64:(ci + 1) * 64], start=(ci == 0), stop=False)
                for rk in range(4):
                    self.mm(O[:, oc], VCN[:, rk, :], ptc[0:64, rk * 64:(rk + 1) * 64], start=False, stop=(rk == 3))
                    self.mm(L[:, oc], self.ONESB[0:64, :], ptc[0:64, rk * 64:(rk + 1) * 64], start=False, stop=(rk == 3))
                if rr == 7:
                    self.recip(self.RC[r8][:, :], L[:, :])
                    self.tt(self.OT[:, h, r8 * 512:(r8 + 1) * 512], O[:, :], self.RC[r8][:, :], ALU.mult)

            pend = na_scores(0)
            for r in range(16):
                cur = pend
                if r + 1 < 16:
                    pend = na_scores(r + 1)
                na_rest(r, *cur)
            O, L = self.psum[4], self.psum[5]
            if l == 1:
                continue
            q = self.QNA[:, h, TL:T]
            pc = self.psum[0]
            for rk in range(4):
                self.mm(pc[0:64, rk * 64:(rk + 1) * 64], KCN[:, rk, :], q, start=True, stop=True)
            ptc = self.PT[2]
            self.act(ptc[0:64, 0:256], pc[0:64, 0:256], AF.Exp, scale=float(NA_SCALE))
            for rk in range(4):
                self.mm(O[:, 0:64], VCN[:, rk, :], ptc[0:64, rk * 64:(rk + 1) * 64], start=(rk == 0), stop=(rk == 3))
                self.mm(L[:, 0:64], self.ONESB[0:64, :], ptc[0:64, rk * 64:(rk + 1) * 64], start=(rk == 0), stop=(rk == 3))
            self.recip(self.RC[0][:, 0:64], L[:, 0:64])
            self.tt(self.OT[:, h, TL:T], O[:, 0:64], self.RC[0][:, 0:64], ALU.mult)

    def merge(self, l):
        self.modulate(l, 1, 3, 4, self.H2)
        self.gate_vec(5, 1.0)
        for q4 in range(4):
            for dq in range(4):
                dj = q4 * 4 + dq
                wg = [self.wload(("g", l, dj, br)) for br in range(3)]
                wb = self.wload(("wb", l, dj))
                for (t0, t1) in self.cur_tbs:
                    n = t1 - t0
                    pg = [self.psum[i] for i in range(3)]
                    pb = [self.psum[3 + i] for i in range(3)]
                    for br, (ka, kb) in enumerate(((0, 6), (6, 11), (11, 16))):
                        self.proj_fm(wg[br], 0, 128, KC, self.H2, pg[br], t0, t1)
                        for k in range(ka, kb):
                            self.mm(pb[br][:, 0:n], wb[:, k, :], self.OT[:, k, t0:t1], start=(k == ka), stop=(k == kb - 1))
                        self.act(self.SG[br][:, 0:n], pg[br][:, 0:n], AF.Sigmoid)
                        if br == 0:
                            self.tt(self.MT[0][:, 0:n], self.SG[0][:, 0:n], pb[0][:, 0:n], ALU.mult)
                        elif br == 1:
                            self.tt(self.MT[1][:, 0:n], self.SG[1][:, 0:n], pb[1][:, 0:n], ALU.mult)
                            self.tt(self.MT[0][:, 0:n], self.MT[0][:, 0:n], self.MT[1][:, 0:n], ALU.add)
                        else:
                            self.tt(self.MT[1][:, 0:n], self.SG[2][:, 0:n], pb[2][:, 0:n], ALU.mult)
                            self.tt(self.YQ[:, dq, t0:t1], self.MT[0][:, 0:n], self.MT[1][:, 0:n], ALU.add)
            for t in range(8):
                wt = self.wload(("wo", l, q4 // 2, t))
                ko = (q4 % 2) * 4
                for dd in range(2):
                    dj2 = t * 2 + dd
                    for (t0, t1) in self.cur_tbs:
                        n = t1 - t0
                        ps = self.psum[6 + (dd % 2)]
                        for k in range(4):
                            self.mm(ps[:, 0:n], wt[:, ko + k, dd * 128:(dd + 1) * 128], self.YQ[:, k, t0:t1],
                                    start=(k == 0), stop=(k == 3))
                        self.resid_add(dj2, t0, t1, ps)

    def mixer(self, l):
        lam_init = 0.8 - 0.6 * math.exp(-0.3 * l)
        self.modulate(l, 1, 3, 4, self.H)
        self.k_phase(l)
        if DEBUG_STOP.startswith("raw_kphase"):
            self.tick(flush=True)
        self.chk("kphase")
        self.cur_tbs = TBS_LAT if l == 1 else TBS
        self.q_phase(l)
        self.tick(flush=True)
        self.chk("qphase")
        self.na_attn(l)
        self.chk("na")
        self.mla_attn(l)
        self.chk("mla")
        self.diff_attn(l, lam_init)
        self.chk("diff")
        self.merge(l)

    def chk(self, name):
        if DEBUG_STOP in ("raw_" + name, "raw_" + name + "_nocc"):
            raise StopBuild()

    def dump(self, src_f32_ap):
        self.dma(self.dbg, src_f32_ap)

    def finish(self, out_ops):
        op = self.P.add("sp", None, [], [], kind="c")
        op.deps = list(out_ops) + [o for o in self.P.ops if o.kind == "cc"]

    def final(self):
        self.rms_stats()
        g = V_FN
        IDF = self.sb("IDF", [128, 128], F32, self.o_ring)
        OST = [self.sb(f"OST{i}", [128, D], F32, self.o_ring + 8192 * (1 + i)) for i in range(2)]
        YF = [self.sb(f"YF{i}", [128, 128], F32, self.o_ring + 512 + 512 * i) for i in range(4)]
        self.dma(IDF[:, :], self.ident_d[:, :])
        outs = []
        for tc in range(8):
            ost = OST[tc % 2]
            for j4 in range(4):
                ps = self.bank()
                for jj in range(4):
                    j = j4 * 4 + jj
                    yf = YF[j % 4]
                    self.stt(yf[:, :], self.X[:, j, tc * 128:(tc + 1) * 128], self.VECS[:, g + j:g + j + 1],
                             self.RS[:, tc * 128:(tc + 1) * 128], ALU.mult, ALU.mult)
                    self.P.add("pe", lambda e, o=ps[:, jj * 128:(jj + 1) * 128], i=yf[:, :]: e.transpose(o, i, IDF[:, :]),
                               [yf[:, :], IDF[:, :]], [ps[:, jj * 128:(jj + 1) * 128]])
                self.act(ost[:, j4 * 512:(j4 + 1) * 512], ps[:, :], AF.Copy)
            outs.append(self.dma(self.out[tc * 128:(tc + 1) * 128, :], ost[:, :]))
        return outs

    def build(self):
        nc = self.nc
        self.memset(self.EPS_AP[:, :], EPS)
        self.setup()
        outs = []
        stop = DEBUG_STOP
        done = False
        for l in range(2):
            self.load_mods(l)
            if stop == "mods" and l == 0:
                outs.append(self.dump_small(self.MODS[:, :, :].rearrange("p a b -> p (a b)"), 288))
                done = True
                break
            self.ffn(l, 0, 0, 0)
            if stop == "ffn1" and l == 0:
                outs.append(self.dma(self.dbg, self.X[:, :, :].rearrange("p j t -> p (j t)")))
                done = True
                break
            try:
                self.mixer(l)
            except StopBuild:
                src = self.OT if stop in ("raw_na", "raw_mla", "raw_diff") else self.H
                for j in range(KC):
                    self.vcopy(self.X[:, j, :], src[:, j, :])
                outs.append(self.dma(self.dbg, self.X[:, :, :].rearrange("p j t -> p (j t)")))
                done = True
                break
            if stop == "mixer" and l == 0:
                outs.append(self.dma(self.dbg, self.X[:, :, :].rearrange("p j t -> p (j t)")))
                done = True
                break
            self.ffn(l, 1, 2, 6, TBS_LAT if l == 1 else TBS)
            if stop == "layer0" and l == 0:
                outs.append(self.dma(self.dbg, self.X[:, :, :].rearrange("p j t -> p (j t)")))
                done = True
                break
        if not done:
            outs += self.final()
        self.finish(outs)
        with ExitStack() as stack:
            self.P.finalize(stack)
        return nc

    def dump_small(self, ap, n):
        return self.dma(self.dbg[:, 0:n], ap)


_CACHE = {}


def prep_inputs(inputs):
    inp = {k: np.asarray(v) for k, v in inputs.items()}
    wbig, dirn = build_wbig(inp)
    vecs = build_vecs(inp)
    dlam = np.ascontiguousarray(inp["diff_lambda"].reshape(1, 512).astype(np.float32))
    ident = np.eye(128, dtype=np.float32)
    maps = []
    for core in range(8):
        b, rq = core // 4, core % 4
        xT = np.concatenate([inp["x"][b, rq * TL:(rq + 1) * TL, :].T, inp["ctx"][b, rq * TCX:(rq + 1) * TCX, :].T], axis=1)
        cv = np.stack([inp["c"][b], inp["c_ctx"]], axis=-1)
        cT = cv.reshape(KC, 128, 2).transpose(1, 0, 2).reshape(128, KC * 2)
        adaw = np.empty((72, 128, 2048), np.float32)
        adab = np.empty((1, 72 * 128), np.float32)
        for l in range(2):
            for jj in range(36):
                c0 = (36 * rq + jj) * 128
                W = inp["w_ada"][l][:, c0:c0 + 128]
                adaw[l * 36 + jj] = W.reshape(KC, 128, 128).transpose(1, 0, 2).reshape(128, 2048)
                adab[0, (l * 36 + jj) * 128:(l * 36 + jj + 1) * 128] = inp["b_ada"][l, c0:c0 + 128]
        sel = np.zeros((128, 16), np.float32)
        sel[:, 8 + b] = 1.0
        if rq > 0:
            sel[:, rq - 1] = 1.0
        if rq < 3:
            sel[:, 4 + rq + 1] = 1.0
        nab = na_bias(inp["na_rpb"], rq).reshape(12, 128, NA_OFF[-1] * 64)
        maps.append({
            "xT": np.ascontiguousarray(xT.astype(np.float32)),
            "cT": np.ascontiguousarray(cT.astype(np.float32)),
            "adaw": adaw, "adab": adab, "wbig": wbig, "vecs": vecs, "dlam": dlam,
            "rope": rope_table(rq).reshape(128, 2 * T), "nab": np.ascontiguousarray(nab),
            "sel": sel, "ident": ident,
        })
    return maps, dirn, wbig.shape[1]


def kernel(**inputs):
    maps, dirn, wtot = prep_inputs(inputs)
    b = Builder(dirn, wtot)
    nc = b.build()
    res = run_bass_kernel_spmd(nc, maps, core_ids=list(range(8)))
    kernel.last = res
    out = np.empty((2, 4096, D), np.float32)
    for core in range(8):
        bb, rq = core // 4, core % 4
        out[bb, rq * TL:(rq + 1) * TL, :] = res.results[core]["out"]
    return out
```

```python
import math
import os
from contextlib import ExitStack

import numpy as np
import concourse.bass as bass
import concourse.mybir as mybir
from concourse.bass_utils import run_bass_kernel_spmd

F32 = mybir.dt.float32
BF16 = mybir.dt.bfloat16
AF = mybir.ActivationFunctionType
ALU = mybir.AluOpType

D = 2048
KC = 16
TL = 1024
TCX = 64
T = TL + TCX
FFN = 5632
NEG = -30000.0
EPS = 1e-6
NA_SCALE = 128 ** -0.5
MLA_SCALE = 192 ** -0.5
DIFF_SCALE = 64 ** -0.5
TBS = [(0, 384), (384, 768), (768, 1088)]
QBS = [(0, 512), (512, 1024), (1024, 1088)]
TBS_LAT = [(0, 384), (384, 768), (768, 1024)]
O_NAQ, O_NAK, O_NAV, O_CQ, O_CKV, O_KR, O_DQ, O_DK, O_DV, O_GA, O_GB, O_GD = (
    0, 768, 1536, 2304, 3072, 3584, 3648, 4288, 4928, 5568, 7616, 9664)
KROWS = 16 * 128 + 64
DEBUG_STOP = os.environ.get("MK_STOP", "")
FAKE = bool(os.environ.get("MK_FAKE"))


VT_NA = [(0, 256), (256, 128), (384, 256), (640, 128)]
VT_D = [(0, 256), (256, 128), (384, 256)]
VT_M = [(0, 384), (384, 256)]
KCH_ROWS = [384, 384, 384, 320, 384, 256]
VCH_COLS = [384, 384, 384, 256, 384, 256]


def kvloc(br, h):
    if br == "na":
        return h // 3, (h % 3) * 128
    if br == "mla":
        return 2 + h // 3, (h % 3) * 128
    if br == "kr":
        return 3, 256
    return 4 + h // 3, (h % 3) * 128


def na_chunks(r):
    lo = min(r - 4, 8)
    hi = max(r + 4, max(r - 4, 0) + 8, min(r - 4, 8) + 8)
    return (lo + 4) // 2, (hi - 1 + 4) // 2


NA_CH = [na_chunks(r) for r in range(16)]
NA_OFF = np.cumsum([0] + [b - a + 1 for a, b in NA_CH]).tolist()


def partner64(d):
    return d + 16 if (d % 32) < 16 else d - 16


def wtile(W, cols):
    K = W.shape[0]
    kc = K // 128
    a = W[:, cols].reshape(kc, 128, len(cols)).transpose(1, 0, 2).reshape(128, kc * len(cols))
    return a, kc, len(cols)


def build_wbig(inp):
    tiles = []
    dirn = {}
    off = [0]

    def put(key, W, cols):
        a, kc, n = wtile(W, np.asarray(cols))
        dirn[key] = (off[0], kc, n)
        off[0] += a.shape[1]
        tiles.append(a)

    ar = np.arange
    for l in range(2):
        w_in = inp["w_in"][l]
        uq = inp["mla_w_uq"][l]
        ukv = inp["mla_w_ukv"][l]
        p64 = np.array([partner64(d) for d in range(64)])
        p128 = np.concatenate([p64, 64 + p64])
        for s in range(2):
            fwi = inp["ffn_w_in"][l, s]
            fwo = inp["ffn_w_out"][l, s]
            for hf in range(2):
                for jj in range(22):
                    j = hf * 22 + jj
                    put(("fi", l, s, j), fwi, np.concatenate([ar(j * 128, j * 128 + 128), FFN + ar(j * 128, j * 128 + 128)]))
                for dj in range(16):
                    put(("fo", l, s, hf, dj), fwo[hf * 2816:(hf + 1) * 2816], ar(dj * 128, dj * 128 + 128))
        for h in range(6):
            put(("nak", l, h), w_in, O_NAK + h * 128 + ar(128))
        for t in range(2):
            put(("ckv", l, t), w_in, O_CKV + t * 256 + ar(256))
        for h in range(5):
            put(("ukvk", l, h), ukv, h * 256 + ar(128))
        put(("kr", l), w_in, np.concatenate([O_KR + ar(64), O_KR + p64]))
        for h in range(5):
            put(("dk", l, h, 0), w_in, O_DK + h * 128 + ar(128))
            put(("dk", l, h, 1), w_in, O_DK + h * 128 + p128)
        for t, (a, n) in enumerate(VT_NA):
            put(("nav", l, t), w_in, O_NAV + a + ar(n))
        for t, (a, n) in enumerate(VT_D):
            put(("dv", l, t), w_in, O_DV + a + ar(n))
        vcols = np.concatenate([h * 256 + 128 + ar(128) for h in range(5)])
        for t, (a, n) in enumerate(VT_M):
            put(("mv", l, t), ukv, vcols[a:a + n])
        for h in range(6):
            put(("naq", l, h), w_in, O_NAQ + h * 128 + ar(128))
        for t in range(3):
            put(("cq", l, t), w_in, O_CQ + t * 256 + ar(256))
        for h in range(5):
            put(("uqn", l, h), uq, h * 192 + ar(128))
            put(("uqr", l, h), uq, np.concatenate([h * 192 + 128 + ar(64), h * 192 + 128 + p64]))
        for h in range(5):
            put(("dq", l, h, 0), w_in, O_DQ + h * 128 + ar(128))
            put(("dq", l, h, 1), w_in, O_DQ + h * 128 + p128)
        wb = inp["w_branch"][l]
        wo = inp["w_out"][l]
        for dj in range(16):
            for br, o in enumerate((O_GA, O_GB, O_GD)):
                put(("g", l, dj, br), w_in, o + dj * 128 + ar(128))
            put(("wb", l, dj), wb, dj * 128 + ar(128))
        for half in range(2):
            for t in range(8):
                put(("wo", l, half, t), wo[half * 1024:(half + 1) * 1024], t * 256 + ar(256))
    return np.ascontiguousarray(np.concatenate(tiles, axis=1)), dirn


def fm(v):
    return np.ascontiguousarray(v.reshape(-1, 128).T)


def build_vecs(inp):
    cols = []
    for l in range(2):
        for n in range(3):
            cols.append(fm(inp["norm_w"][l, n]))
    cols.append(fm(inp["final_norm"]))
    for l in range(2):
        cols.append(fm(inp["mla_q_norm"][l]))
    for l in range(2):
        cols.append(fm(inp["mla_kv_norm"][l]))
    for l in range(2):
        cols.append(fm(inp["diff_subln"][l]))
    return np.ascontiguousarray(np.concatenate(cols, axis=1).astype(np.float32))


V_NW, V_FN, V_QN, V_KVN, V_SUB = 0, 96, 112, 124, 132
NV = 134


def rope_table(rq):
    theta = 10000.0
    quarter = 16
    freqs = (np.float32(theta) ** (-np.arange(quarter, dtype=np.float32) / np.float32(quarter))).astype(np.float32)
    g = rq * TL + np.arange(TL)
    rows = (g // 64).astype(np.float32)
    colsp = (g % 64).astype(np.float32)
    tab = np.zeros((64, 2, T), np.float32)
    tab[:, 0, TL:] = 1.0
    for d in range(64):
        pos = rows if d < 32 else colsp
        ang = (pos * freqs[d % 16]).astype(np.float32)
        tab[d, 0, :TL] = np.cos(ang)
        sn = np.sin(ang)
        tab[d, 1, :TL] = -sn if (d % 32) < 16 else sn
    return np.ascontiguousarray(np.concatenate([tab, tab], axis=0))


def na_bias(rpb, rq):
    out = np.full((2, 6, 128, NA_OFF[-1], 64), NEG, np.float32)
    qc = np.arange(64)
    kc = np.arange(64)
    c0 = np.clip(qc - 8, 0, 48)
    col_ok = (kc[:, None] >= c0[None, :]) & (kc[:, None] < c0[None, :] + 16)
    col_off = np.clip(kc[:, None] - qc[None, :], -15, 15) + 15
    for r in range(16):
        gr = 16 * rq + r
        r0 = min(max(gr - 4, 0), 56)
        clo, chi = NA_CH[r]
        for ci, c in enumerate(range(clo, chi + 1)):
            for kk in range(2):
                kr = 16 * rq + (-4 + 2 * c + kk)
                if kr < 0 or kr >= 64 or kr < r0 or kr >= r0 + 8:
                    continue
                ro = kr - gr + 7
                vals = rpb[:, :, ro, :][:, :, col_off]
                blk = out[:, :, kk * 64:(kk + 1) * 64, NA_OFF[r] + ci, :]
                blk[...] = np.where(col_ok[None, None], vals, NEG)
    return out


class Op:
    __slots__ = ("eng", "fn", "deps", "pos", "awaited", "kind", "idx", "waits", "sem", "semval", "ccg")

    def __init__(self, eng, fn, kind):
        self.eng = eng
        self.fn = fn
        self.kind = kind
        self.deps = []
        self.awaited = False
        self.waits = []
        self.sem = None
        self.semval = 0


ENGS = ("pe", "act", "dve", "pool", "sp")
NDSEM = 16
SEM_CH = 4000


class Prog:
    def __init__(self, nc):
        self.nc = nc
        self.ops = []
        self.eng_ops = {e: [] for e in ENGS}
        self.recs = {}
        self.ndma = 0
        self.dma_ops = []
        self.skip = set()

    @staticmethod
    def _dsz(dt):
        return mybir.dt.size(dt)

    def region(self, ap):
        t = ap.tensor
        name = t.name
        if name in self.skip:
            return None
        pat = ap.ap
        rng = t.manual_sbuf_range
        if rng is not None:
            ps = 1
            for s in t.shape[1:]:
                ps *= int(s)
            lo = int(ap.offset) % ps
            ext = 1
            for (st, cnt) in pat[1:]:
                ext += (int(cnt) - 1) * abs(int(st))
            esz = self._dsz(t.dtype)
            return ("sb", rng[0] + lo * esz, rng[0] + (lo + ext) * esz)
        tn = type(t).__name__
        if tn.startswith("PSum") or tn.startswith("SB"):
            ps = 1
            for s in t.shape[1:]:
                ps *= int(s)
            lo = int(ap.offset) % ps
            ext = 1
            for (st, cnt) in pat[1:]:
                ext += (int(cnt) - 1) * abs(int(st))
            return (name, lo, lo + ext)
        lo = int(ap.offset)
        ext = 1
        for (st, cnt) in pat:
            ext += (int(cnt) - 1) * abs(int(st))
        return (name, lo, lo + ext)

    def _pages(self, sp, lo, hi):
        pg = 2048 if sp == "sb" else 65536
        return range(lo // pg, (hi - 1) // pg + 1)

    def add(self, eng, fn, reads=(), writes=(), kind="c", ccg=0):
        op = Op(eng, fn, kind)
        op.ccg = ccg
        op.idx = len(self.ops)
        deps = {}
        rkey = (eng + kind) if kind == "c" else ("d", op.idx)
        for ap in reads:
            rg = self.region(ap)
            if rg is None:
                continue
            sp, lo, hi = rg
            pages = self.recs.setdefault(sp, {})
            found = None
            for pgi in self._pages(sp, lo, hi):
                for rec in pages.get(pgi, ()):
                    if rec[0] < hi and lo < rec[1]:
                        w = rec[2]
                        if w is not None:
                            deps[w.idx] = (w, "raw")
                        if rec[0] == lo and rec[1] == hi:
                            found = rec
            if found is None:
                found = [lo, hi, None, {}]
                for pgi in self._pages(sp, lo, hi):
                    pages.setdefault(pgi, []).append(found)
            found[3][rkey] = op
        for ap in writes:
            rg = self.region(ap)
            if rg is None:
                continue
            sp, lo, hi = rg
            pages = self.recs.setdefault(sp, {})
            newrec = [lo, hi, op, {}]
            for pgi in self._pages(sp, lo, hi):
                lst = pages.get(pgi)
                if lst is None:
                    pages[pgi] = [newrec]
                    continue
                keep = []
                for rec in lst:
                    if rec[0] < hi and lo < rec[1]:
                        w = rec[2]
                        if w is not None and w.idx not in deps:
                            deps[w.idx] = (w, "waw")
                        for rd in rec[3].values():
                            if rd is not op and rd.idx not in deps:
                                deps[rd.idx] = (rd, "war")
                        if lo <= rec[0] and rec[1] <= hi:
                            continue
                    keep.append(rec)
                keep.append(newrec)
                pages[pgi] = keep
        if kind != "c":
            if kind == "d":
                op.sem = ("d", self.ndma % NDSEM)
                op.semval = 16 * (self.ndma // NDSEM + 1)
                if self.ndma >= NDSEM:
                    prev = self.dma_ops[self.ndma - NDSEM]
                    deps.setdefault(prev.idx, (prev, "raw"))
                self.dma_ops.append(op)
                self.ndma += 1
        for (d, typ) in deps.values():
            if d.kind == "c" and d.eng == eng:
                if eng == "pe":
                    continue
                if typ != "raw" and kind == "c":
                    continue
            op.deps.append(d)
        op.pos = len(self.eng_ops[eng])
        self.eng_ops[eng].append(op)
        self.ops.append(op)
        return op

    def finalize(self, stack):
        nc = self.nc
        known = {e: {f: -1 for f in ENGS} for e in ENGS}
        knownd = {e: set() for e in ENGS}
        for op in self.ops:
            e = op.eng
            for d in op.deps:
                if d.kind == "c":
                    if d.pos <= known[e][d.eng]:
                        continue
                    known[e][d.eng] = d.pos
                    d.awaited = True
                    op.waits.append(d)
                else:
                    if d.idx in knownd[e]:
                        continue
                    knownd[e].add(d.idx)
                    op.waits.append(d)
        esems = {}
        for e in ENGS:
            cnt = 0
            for op in self.eng_ops[e]:
                if op.kind == "c" and op.awaited:
                    op.sem = (e, cnt // SEM_CH)
                    op.semval = cnt % SEM_CH + 1
                    cnt += 1
            esems[e] = [stack.enter_context(nc.semaphore(f"s_{e}_{i}")) for i in range(cnt // SEM_CH + 1)]
        dsems = [stack.enter_context(nc.semaphore(f"s_dma_{i}")) for i in range(NDSEM)]
        ccsems = [stack.enter_context(nc.semaphore(f"s_cc_{i}")) for i in range(2)]
        ccops = [op for op in self.ops if op.kind == "cc"]
        for op in ccops:
            if op.ccg == 0:
                op.sem = ("cc", 0)
                op.semval = sum(1 for o in ccops if o.ccg == 0)
            else:
                op.sem = ("cc", 1)
                op.semval = sum(1 for o in ccops if 0 < o.ccg <= op.ccg)

        def semof(op):
            kind, i = op.sem
            if kind == "d":
                return dsems[i]
            if kind == "cc":
                return ccsems[i]
            return esems[kind][i]

        def emit(ename, eng):
            for op in self.eng_ops[ename]:
                for d in op.waits:
                    eng.wait_ge(semof(d), d.semval)
                if op.fn is None:
                    continue
                inst = op.fn(eng)
                if op.kind == "d":
                    inst.then_inc(semof(op), 16)
                elif op.kind == "cc":
                    inst.then_inc(semof(op))
                elif op.awaited:
                    inst.then_inc(semof(op), 1)

        block = stack.enter_context(nc.Block())

        @block.tensor
        def _(eng):
            emit("pe", eng)

        @block.scalar
        def _(eng):
            emit("act", eng)

        @block.vector
        def _(eng):
            emit("dve", eng)

        @block.gpsimd
        def _(eng):
            emit("pool", eng)

        @block.sync
        def _(eng):
            emit("sp", eng)


class StopBuild(Exception):
    pass


class Builder:
    def __init__(self, dirn, wtot):
        self.dirn = dirn
        self.wtot = wtot
        self.nc = nc = bass.Bass("TRN2", target_bir_lowering=False)
        self.P = Prog(nc)
        P = self.P
        dt = nc.dram_tensor
        self.xT = dt("xT", [D, T], F32, kind="ExternalInput").ap()
        self.cT = dt("cT", [128, KC * 2], F32, kind="ExternalInput").ap()
        if FAKE:
            wtot = self.wtot = 8192
        self.adaw = dt("adaw", [1 if FAKE else 72, 128, 2048], F32, kind="ExternalInput").ap()
        self.adab = dt("adab", [1, 72 * 128], F32, kind="ExternalInput").ap()
        self.wbig = dt("wbig", [128, wtot], F32, kind="ExternalInput").ap()
        self.vecs_d = dt("vecs", [128, NV], F32, kind="ExternalInput").ap()
        self.dlam_d = dt("dlam", [1, 512], F32, kind="ExternalInput").ap()
        self.rope_d = dt("rope", [128, 2 * T], F32, kind="ExternalInput").ap()
        self.nab_d = dt("nab", [1 if FAKE else 12, 128, NA_OFF[-1] * 64], F32, kind="ExternalInput").ap()
        self.sel_d = dt("sel", [128, 16], F32, kind="ExternalInput").ap()
        self.ident_d = dt("ident", [128, 128], F32, kind="ExternalInput").ap()
        for n in ("xT", "cT", "adaw", "adab", "wbig", "vecs", "dlam", "rope", "nab", "sel", "ident"):
            P.skip.add(n)
        self.out = dt("out", [TL, D], F32, kind="ExternalOutput").ap()
        self.dbg = None
        if DEBUG_STOP:
            self.dbg = dt("dbg", [128, KC * T], F32, kind="ExternalOutput").ap()
        self.cinK = [dt(f"cinK{i}", [n, T], BF16) for i, n in enumerate(KCH_ROWS)]
        self.coutK = [dt(f"coutK{i}", [4 * n, T], BF16) for i, n in enumerate(KCH_ROWS)]
        self.cinV = [dt(f"cinV{i}", [T, n], BF16) for i, n in enumerate(VCH_COLS)]
        self.coutV = [dt(f"coutV{i}", [4 * T, n], BF16) for i, n in enumerate(VCH_COLS)]
        self.cinA = dt("cinA", [128, 144], F32)
        self.coutA = dt("coutA", [4 * 128, 144], F32)
        arena = nc.alloc_sbuf_tensor("arena", [128, 212736], mybir.dt.uint8)
        self.abase = int(nc.lookup_mloc(arena).addr)
        self.ncnt = 0
        self.psum = [nc.alloc_psum_tensor(f"ps{i}", [128, 512], F32) for i in range(8)]
        self.OX = 0
        self.OB = 69632
        self.OC = self.OB + 34816
        self.OD = self.OC + 47872
        self.X = self.sb("X", [128, KC, T], F32, self.OX)
        self.H = self.sb("H", [128, KC, T], BF16, self.OB)
        self.OT = self.sb("OT", [128, KC, T], BF16, self.OB)
        self.U = self.sb("U", [128, 22, T], BF16, self.OC)
        self.H2 = self.sb("H2", [128, KC, T], BF16, self.OC)
        self.QNA = self.sb("QNA", [128, 6, T], BF16, self.OC)
        self.QMN = self.sb("QMN", [128, 5, T], BF16, self.OC + 13056)
        self.QMR = self.sb("QMR", [128, 5, T], BF16, self.OC + 13056 + 10880)
        self.QD = self.sb("QD", [128, 5, T], BF16, self.OC + 13056 + 21760)
        self.CQN = self.sb("CQN", [128, 6, T], BF16, self.OC + 13056 + 21760)
        o = self.OD
        self.MALL = self.sb("MALL", [128, 4, 144], F32, o); o += 3456
        self.MODS = self.sb("MODS", [128, 2, 144], F32, o); o += 1152
        self.VECS = self.sb("VECS", [128, NV], F32, o); o += 544
        self.AV = self.sb("AV", [128, 2, 16], F32, o); o += 128
        self.BV = self.sb("BV", [128, 2, 16], F32, o); o += 128
        self.GV = self.sb("GV", [128, 2, 16], F32, o); o += 128
        self.SEL = self.sb("SEL", [128, 16], F32, o); o += 64
        self.NEGLAM = self.sb("NEGLAM", [128, 4], F32, o); o += 32
        self.SUBW = self.sb("SUBW", [128, 2], F32, o); o += 32
        self.EPS_AP = self.sb("EPSC", [128, 1], F32, o); o += 32
        self.ONESB = self.sb("ONESB", [128, 128], BF16, o); o += 256
        self.ONESF = self.sb("ONESF", [128, 128], F32, o); o += 512
        self.o_scr = o
        self.RS = self.sb("RS", [128, T], F32, o); o += 4352
        self.TMPF = self.sb("TMPF", [128, T], F32, o); o += 4352
        self.SQ = self.sb("SQ", [128, 2, 512], BF16, o); o += 2048
        self.SA = self.sb("SA", [128, 2, 512], F32, o); o += 4096
        o_x = o
        o += 2048 + 4096
        self.ROPE = self.sb("ROPE", [128, 2, T], BF16, self.o_scr + 4352)
        self.KST = [self.sb(f"KST{i}", [128, T], BF16, self.o_scr + 10752 + i * 2176) for i in range(2)]
        self.VST = [self.sb(f"VST{i}", [128, 384], BF16, self.o_scr + 10752 + 4352 + i * 768) for i in range(2)]
        self.RT = [self.sb(f"RT{i}", [128, 512], F32, o_x + 2048 + i * 2048) for i in range(2)]
        self.CKVN = self.sb("CKVN", [128, 4, T], BF16, self.OC + 13056 + 21760)
        a = self.o_scr
        self.PT = [self.sb(f"PT{i}", [128, 512], BF16, a + i * 1024) for i in range(4)]; a += 4096
        self.RC = [self.sb(f"RC{i}", [128, 512], F32, a + i * 2048) for i in range(2)]; a += 4096
        self.AO = [self.sb(f"AO{i}", [128, 512], F32, a + i * 2048) for i in range(2)]; a += 4096
        self.TBI = self.sb("TBI", [128, 384], F32, a); a += 1536
        self.TBI2 = self.sb("TBI2", [128, 384], F32, self.o_scr + 17920)
        self.KCX = [self.sb(f"KCX{i}", [128, 2, 4, 64], BF16, a + i * 1024) for i in range(2)]; a += 2048
        self.VCX = [self.sb(f"VCX{i}", [64, 4, 128], BF16, a + i * 1024) for i in range(2)]; a += 2048
        assert a <= o, (a, o)
        a = self.o_scr + 4352
        self.SG = [self.sb(f"SG{i}", [128, 512], F32, a + i * 2048) for i in range(3)]; a += 6144
        self.MT = [self.sb(f"MT{i}", [128, 512], F32, a + i * 2048) for i in range(2)]; a += 4096
        assert a <= o
        self.YQ = self.sb("YQ", [128, 4, T], BF16, self.OC + 34816)
        self.o_ring = o
        self.NSLOT = 4
        self.ring = [self.sb(f"ring{i}", [128, 4096], BF16, o + i * 8192) for i in range(self.NSLOT)]
        self.ringf = [self.sb(f"ringf{i}", [128, 2048], F32, o + i * 8192) for i in range(self.NSLOT)]
        o += self.NSLOT * 8192
        self.o_end = o
        assert o <= 212736, o
        self.wi = 0
        self.psi = 0
        self.pending = []

    def sb(self, name, shape, dt, off):
        self.ncnt += 1
        return self.nc.alloc_sbuf_tensor_at(f"{name}_{self.ncnt}", shape, dt, offset=self.abase + off)

    def mm(self, out, lhsT, rhs, start=True, stop=True):
        self.P.add("pe", lambda e: e.matmul(out, lhsT, rhs, start=start, stop=stop), [lhsT, rhs], [out])

    def act(self, out, in_, func, bias=None, scale=None):
        kw = {}
        rd = [in_]
        if bias is not None:
            kw["bias"] = bias
            if not isinstance(bias, (int, float)):
                rd.append(bias)
        if scale is not None:
            kw["scale"] = scale
            if not isinstance(scale, (int, float)):
                rd.append(scale)
        self.P.add("act", lambda e: e.activation(out, in_, func, **kw), rd, [out])

    def tt(self, out, in0, in1, op):
        self.P.add("dve", lambda e: e.tensor_tensor(out, in0, in1, op), [in0, in1], [out])

    def ts(self, out, in0, s1, s2, op0, op1=None):
        rd = [in0] + [s for s in (s1, s2) if s is not None and not isinstance(s, (int, float))]
        if op1 is None:
            self.P.add("dve", lambda e: e.tensor_scalar(out, in0, s1, None, op0), rd, [out])
        else:
            self.P.add("dve", lambda e: e.tensor_scalar(out, in0, s1, s2, op0, op1), rd, [out])

    def stt(self, out, in0, scalar, in1, op0, op1):
        rd = [in0, in1] + ([] if isinstance(scalar, (int, float)) else [scalar])
        self.P.add("dve", lambda e: e.scalar_tensor_tensor(out, in0, scalar, in1, op0, op1), rd, [out])

    def recip(self, out, in_):
        self.P.add("dve", lambda e: e.reciprocal(out, in_), [in_], [out])

    def vcopy(self, out, in_):
        self.P.add("dve", lambda e: e.tensor_copy(out, in_), [in_], [out])

    def memset(self, ap, v):
        self.P.add("dve", lambda e: e.memset(ap, v), [], [ap])

    def dma(self, out, in_, eng="sp"):
        return self.P.add(eng, lambda e: e.dma_start(out=out, in_=in_), [in_], [out], kind="d")

    def bank(self):
        b = self.psum[self.psi % 6]
        self.psi += 1
        return b

    def wload(self, key):
        off, kc, n = self.dirn[key]
        if FAKE:
            off = 0
        slot = self.ring[self.wi % self.NSLOT]
        self.wi += 1
        v = slot[:, 0:kc * n]
        self.dma(v, self.wbig[:, off:off + kc * n], eng="pool")
        self.tick()
        return v.rearrange("p (k c) -> p k c", k=kc)

    def tick(self, flush=False):
        keep = []
        for it in self.pending:
            it[0] -= 1
            if it[0] <= 0 or flush:
                it[1]()
            else:
                keep.append(it)
        self.pending = keep

    def setup(self):
        self.memset(self.ONESB[:, :], 1.0)
        self.memset(self.ONESF[:, :], 1.0)
        self.dma(self.X[:, :, :], self.xT.rearrange("(j p) t -> p j t", p=128))
        self.dma(self.VECS[:, :], self.vecs_d[:, :])
        self.dma(self.SEL[:, :], self.sel_d[:, :])
        CS = self.sb("CS", [128, KC * 2], F32, self.o_scr)
        BAD = self.sb("BAD", [1, 72 * 128], F32, self.OC)
        ADAL = self.sb("ADAL", [128, 144], F32, self.o_scr + 192)
        self.dma(CS[:, :], self.cT[:, :])
        self.dma(BAD[:, :], self.adab[:, :])
        self.act(CS[:, :], CS[:, :], AF.Silu)
        CS3 = CS[:, :].rearrange("p (k v) -> p k v", v=2)
        for t in range(72):
            wt = self.ringf[self.wi % self.NSLOT]
            self.wi += 1
            self.dma(wt[:, :], self.adaw[0 if FAKE else t])
            w3 = wt[:, :].rearrange("p (k c) -> p k c", k=KC)
            ps = self.bank()
            for k in range(KC):
                self.mm(ps[:, 0:2], w3[:, k, :], CS3[:, k, :], start=(k == 0), stop=False)
            self.mm(ps[:, 0:2], BAD[0:1, t * 128:(t + 1) * 128], self.ONESF[0:1, 0:2], start=False, stop=True)
            self.vcopy(ADAL[:, t * 2:(t + 1) * 2], ps[:, 0:2])
        self.dma(self.cinA.ap(), ADAL[:, :])
        self.P.add("pool", lambda e: e.collective_compute(
            "AllGather", ALU.bypass, replica_groups=[[0, 1, 2, 3], [4, 5, 6, 7]],
            ins=[self.cinA.ap().opt()], outs=[self.coutA.ap().opt()]),
            [self.cinA.ap()], [self.coutA.ap()], kind="cc")
        self.dma(self.MALL[:, :, :], self.coutA.ap().rearrange("(i p) c -> p i c", p=128))

    def load_mods(self, l):
        for cls in range(2):
            src = self.MALL[:, :, l * 72 + cls:l * 72 + 72:2]
            dst = self.MODS[:, cls, :].rearrange("p (i j) -> p i j", i=4)
            self.vcopy(dst, src)

    def mod(self, cls, n, j=None):
        if j is None:
            return self.MODS[:, cls, n * 16:(n + 1) * 16]
        return self.MODS[:, cls, n * 16 + j:n * 16 + j + 1]

    def rms_stats(self, nfeat_scale=1.0 / D):
        for (t0, t1) in TBS:
            n = t1 - t0
            ps = self.psum[7]
            for j in range(KC):
                sq = self.SQ[:, j % 2, 0:n]
                self.act(sq, self.X[:, j, t0:t1], AF.Square)
                self.mm(ps[:, 0:n], self.ONESB[:, :], sq, start=(j == 0), stop=(j == KC - 1))
            self.act(self.RS[:, t0:t1], ps[:, 0:n], AF.Sqrt, bias=self.EPS_AP[:, 0:1], scale=nfeat_scale)
            self.recip(self.RS[:, t0:t1], self.RS[:, t0:t1])

    def modulate(self, l, nidx, n_shift, n_scale, dst):
        g = self.VECS[:, V_NW + (l * 3 + nidx) * 16:V_NW + (l * 3 + nidx + 1) * 16]
        for cls in range(2):
            self.ts(self.AV[:, cls, :], self.mod(cls, n_scale), 1.0, None, ALU.add)
            self.tt(self.AV[:, cls, :], self.AV[:, cls, :], g, ALU.mult)
        for (t0, t1) in TBS:
            n = t1 - t0
            ps = self.psum[7]
            for j in range(KC):
                sq = self.SQ[:, j % 2, 0:n]
                self.act(sq, self.X[:, j, t0:t1], AF.Square)
                self.mm(ps[:, 0:n], self.ONESB[:, :], sq, start=(j == 0), stop=(j == KC - 1))
            self.act(self.RS[:, t0:t1], ps[:, 0:n], AF.Sqrt, bias=self.EPS_AP[:, 0:1], scale=1.0 / D)
            self.recip(self.RS[:, t0:t1], self.RS[:, t0:t1])
            for j in range(KC):
                buf = self.TMPF[:, (j % 2) * 384:(j % 2) * 384 + n]
                for (a, b, cls) in ((t0, min(t1, TL), 0), (max(t0, TL), t1, 1)):
                    if b <= a:
                        continue
                    self.tt(buf[:, a - t0:b - t0], self.X[:, j, a:b], self.RS[:, a:b], ALU.mult)
                    self.act(dst[:, j, a:b], buf[:, a - t0:b - t0], AF.Identity,
                             bias=self.mod(cls, n_shift, j), scale=self.AV[:, cls, j:j + 1])

    def gate_vec(self, n_gate, factor):
        for cls in range(2):
            self.ts(self.GV[:, cls, :], self.mod(cls, n_gate), float(factor), None, ALU.mult)

    def resid_add(self, dj, t0, t1, ps):
        for (a, b, cls) in ((t0, min(t1, TL), 0), (max(t0, TL), t1, 1)):
            if b <= a:
                continue
            self.stt(self.X[:, dj, a:b], ps[:, a - t0:b - t0], self.GV[:, cls, dj:dj + 1], self.X[:, dj, a:b],
                     ALU.mult, ALU.add)

    def ffn(self, l, s, nidx, n0, tbs=None):
        tbs = tbs or TBS
        self.modulate(l, nidx, n0, n0 + 1, self.H)
        self.gate_vec(n0 + 2, 0.5)
        for hf in range(2):
            for jj in range(22):
                wt = self.wload(("fi", l, s, hf * 22 + jj))
                for (t0, t1) in tbs:
                    n = t1 - t0
                    pa = self.bank()
                    pb = self.bank()
                    for k in range(KC):
                        self.mm(pa[:, 0:n], wt[:, k, 0:128], self.H[:, k, t0:t1], start=(k == 0), stop=(k == KC - 1))
                    for k in range(KC):
                        self.mm(pb[:, 0:n], wt[:, k, 128:256], self.H[:, k, t0:t1], start=(k == 0), stop=(k == KC - 1))
                    sa = self.SA[:, (self.psi // 2) % 2, 0:n]
                    self.act(sa, pa[:, 0:n], AF.Silu)
                    self.tt(self.U[:, jj, t0:t1], sa, pb[:, 0:n], ALU.mult)
            for dj in range(KC):
                wt = self.wload(("fo", l, s, hf, dj))
                for (t0, t1) in tbs:
                    n = t1 - t0
                    ps = self.bank()
                    for k in range(22):
                        self.mm(ps[:, 0:n], wt[:, k, :], self.U[:, k, t0:t1], start=(k == 0), stop=(k == 21))
                    self.resid_add(dj, t0, t1, ps)


    def proj_fm(self, wt, c0, ncol, kc, src, ps, t0, t1):
        n = t1 - t0
        for k in range(kc):
            self.mm(ps[0:ncol, 0:n], wt[:, k, c0:c0 + ncol], src[:, k, t0:t1], start=(k == 0), stop=(k == kc - 1))

    def rope_evac(self, dst, ps1, ps2, np_, t0, t1):
        n = t1 - t0
        self.tt(self.RT[0][0:np_, 0:n], ps1[0:np_, 0:n], self.ROPE[0:np_, 0, t0:t1], ALU.mult)
        self.tt(self.RT[1][0:np_, 0:n], ps2[0:np_, 0:n], self.ROPE[0:np_, 1, t0:t1], ALU.mult)
        self.tt(dst, self.RT[0][0:np_, 0:n], self.RT[1][0:np_, 0:n], ALU.add)

    def mla_norm(self, l, tiles, nch, wcol0, dst, tbs=None):
        tbs = tbs or TBS
        for (t0, t1) in tbs:
            n = t1 - t0
            pss = []
            for c in range(nch):
                ps = self.psum[c]
                wt = tiles[c // 2]
                self.proj_fm(wt, (c % 2) * 128, 128, KC, self.H, ps, t0, t1)
                pss.append(ps)
            pn = self.psum[7]
            for c in range(nch):
                sq = self.SQ[:, c % 2, 0:n]
                self.act(sq, pss[c][:, 0:n], AF.Square)
                self.mm(pn[:, 0:n], self.ONESB[:, :], sq, start=(c == 0), stop=(c == nch - 1))
            self.act(self.RS[:, t0:t1], pn[:, 0:n], AF.Sqrt, bias=self.EPS_AP[:, 0:1], scale=1.0 / (nch * 128))
            self.recip(self.RS[:, t0:t1], self.RS[:, t0:t1])
            for c in range(nch):
                self.stt(dst[:, c, t0:t1], pss[c][:, 0:n], self.VECS[:, wcol0 + c:wcol0 + c + 1],
                         self.RS[:, t0:t1], ALU.mult, ALU.mult)

    def k_phase(self, l):
        def kdst(br, h, nrows=128):
            ci, r0 = kvloc(br, h)
            return self.cinK[ci].ap()[r0:r0 + nrows, :]

        self.dma(self.ROPE[:, :, :].rearrange("p a t -> p (a t)"), self.rope_d[:, :], eng="pool")
        ki = [0]
        vi = [0]
        grp = [[0, 1, 2, 3], [4, 5, 6, 7]]
        nocc = DEBUG_STOP.endswith("_nocc")

        def fire(kind, i):
            ci, co = (self.cinK[i], self.coutK[i]) if kind == "k" else (self.cinV[i], self.coutV[i])
            self.P.add("pool", lambda e, ci=ci, co=co: e.collective_compute(
                "AllGather", ALU.bypass, replica_groups=grp,
                ins=[ci.ap().opt()], outs=[co.ap().opt()]),
                [ci.ap()], [co.ap()], kind="cc", ccg=l + 1)

        def trigger(kind, i):
            if nocc:
                return
            self.pending.append([3, lambda: fire(kind, i)])

        def kst():
            ki[0] += 1
            return self.KST[ki[0] % 2]

        def vproj(key, src, kc, chunk, col0):
            wt = self.wload(key)
            ncol = self.dirn[key][2]
            dst = self.cinV[chunk].ap()
            for tc in range(9):
                t0 = tc * 128
                nt = min(128, T - t0)
                ps = self.bank()
                for k in range(kc):
                    self.mm(ps[0:nt, 0:ncol], src[:, k, t0:t0 + nt], wt[:, k, :], start=(k == 0), stop=(k == kc - 1))
                vi[0] += 1
                vs = self.VST[vi[0] % 2]
                self.act(vs[0:nt, 0:ncol], ps[0:nt, 0:ncol], AF.Copy)
                self.dma(dst[t0:t0 + nt, col0:col0 + ncol], vs[0:nt, 0:ncol])

        for h in range(6):
            wt = self.wload(("nak", l, h))
            st = kst()
            for (t0, t1) in TBS:
                ps = self.bank()
                self.proj_fm(wt, 0, 128, KC, self.H, ps, t0, t1)
                self.act(st[:, t0:t1], ps[:, 0:t1 - t0], AF.Copy)
            self.dma(kdst("na", h), st[:, :])
            if h % 3 == 2:
                trigger("k", h // 3)
        for t, (a, n) in enumerate(VT_NA):
            vproj(("nav", l, t), self.H, KC, a // 384, a % 384)
            if t % 2 == 1:
                trigger("v", t // 2)
        tiles = [self.wload(("ckv", l, t)) for t in range(2)]
        self.mla_norm(l, tiles, 4, V_KVN + l * 4, self.CKVN)
        for h in range(5):
            wt = self.wload(("ukvk", l, h))
            st = kst()
            for (t0, t1) in TBS:
                ps = self.bank()
                self.proj_fm(wt, 0, 128, 4, self.CKVN, ps, t0, t1)
                self.act(st[:, t0:t1], ps[:, 0:t1 - t0], AF.Copy)
            self.dma(kdst("mla", h), st[:, :])
            if h == 2:
                trigger("k", 2)
        wt = self.wload(("kr", l))
        st = kst()
        for (t0, t1) in TBS:
            p1 = self.bank()
            p2 = self.bank()
            self.proj_fm(wt, 0, 64, KC, self.H, p1, t0, t1)
            self.proj_fm(wt, 64, 64, KC, self.H, p2, t0, t1)
            self.rope_evac(st[0:64, t0:t1], p1, p2, 64, t0, t1)
        self.dma(kdst("kr", 0, 64), st[0:64, :])
        trigger("k", 3)
        for t, (a, n) in enumerate(VT_M):
            vproj(("mv", l, t), self.CKVN, 4, 2 + a // 384, a % 384)
            trigger("v", 2 + t)
        for h in range(5):
            w0 = self.wload(("dk", l, h, 0))
            w1 = self.wload(("dk", l, h, 1))
            st = kst()
            for (t0, t1) in TBS:
                p1 = self.bank()
                p2 = self.bank()
                self.proj_fm(w0, 0, 128, KC, self.H, p1, t0, t1)
                self.proj_fm(w1, 0, 128, KC, self.H, p2, t0, t1)
                self.rope_evac(st[:, t0:t1], p1, p2, 128, t0, t1)
            self.dma(kdst("diff", h), st[:, :])
            if h == 2 or h == 4:
                trigger("k", 4 + h // 3)
        for t, (a, n) in enumerate(VT_D):
            vproj(("dv", l, t), self.H, KC, 4 + a // 384, a % 384)
            if t >= 1:
                trigger("v", 4 + t - 1)

    def q_phase(self, l):
        for h in range(6):
            wt = self.wload(("naq", l, h))
            for (t0, t1) in self.cur_tbs:
                ps = self.bank()
                self.proj_fm(wt, 0, 128, KC, self.H, ps, t0, t1)
                self.act(self.QNA[:, h, t0:t1], ps[:, 0:t1 - t0], AF.Copy)
        tiles = [self.wload(("cq", l, t)) for t in range(3)]
        self.mla_norm(l, tiles, 6, V_QN + l * 6, self.CQN, self.cur_tbs)
        for h in range(5):
            wn = self.wload(("uqn", l, h))
            wr = self.wload(("uqr", l, h))
            for (t0, t1) in self.cur_tbs:
                ps = self.bank()
                self.proj_fm(wn, 0, 128, 6, self.CQN, ps, t0, t1)
                self.act(self.QMN[:, h, t0:t1], ps[:, 0:t1 - t0], AF.Copy)
                p1 = self.bank()
                p2 = self.bank()
                self.proj_fm(wr, 0, 64, 6, self.CQN, p1, t0, t1)
                self.proj_fm(wr, 64, 64, 6, self.CQN, p2, t0, t1)
                self.rope_evac(self.QMR[0:64, h, t0:t1], p1, p2, 64, t0, t1)
        for h in range(5):
            w0 = self.wload(("dq", l, h, 0))
            w1 = self.wload(("dq", l, h, 1))
            for (t0, t1) in self.cur_tbs:
                p1 = self.bank()
                p2 = self.bank()
                self.proj_fm(w0, 0, 128, KC, self.H, p1, t0, t1)
                self.proj_fm(w1, 0, 128, KC, self.H, p2, t0, t1)
                self.rope_evac(self.QD[:, h, t0:t1], p1, p2, 128, t0, t1)

    def attend(self, nq, chunks, maps, v_fn, scale, obanks, lbanks):
        nm = len(maps)
        sb_i = [0]

        def scores(ci):
            ch = chunks[ci]
            nk = ch[0]
            out = []
            for m in range(nm):
                ps = self.psum[sb_i[0] % 4]
                sb_i[0] += 1
                pieces = maps[m]
                for pi, (kf, q) in enumerate(pieces):
                    self.mm(ps[0:nk, 0:nq], kf(ch), q, start=(pi == 0), stop=(pi == len(pieces) - 1))
                out.append(ps)
            return out

        pend = scores(0)
        pti = 0
        for ci in range(len(chunks)):
            cur = pend
            if ci + 1 < len(chunks):
                pend = scores(ci + 1)
            nk = chunks[ci][0]
            for m in range(nm):
                pt = self.PT[pti % 4]
                pti += 1
                self.act(pt[0:nk, 0:nq], cur[m][0:nk, 0:nq], AF.Exp, scale=float(scale))
                first = ci == 0
                last = ci == len(chunks) - 1
                self.mm(obanks[m][:, 0:nq], v_fn(chunks[ci]), pt[0:nk, 0:nq], start=first, stop=last)
                self.mm(lbanks[m][:, 0:nq], self.ONESB[0:nk, :], pt[0:nk, 0:nq], start=first, stop=last)

    def kview(self, br, h, nrows=128):
        ci, r0 = kvloc(br, h)
        return self.coutK[ci].ap().rearrange("(r k) t -> r k t", r=4)[:, r0:r0 + nrows, :]

    def vview(self, br, h):
        ci, c0 = kvloc(br, h)
        return self.coutV[ci].ap().rearrange("(r t) c -> r t c", r=4)[:, :, c0:c0 + 128]

    def mla_attn(self, l):
        KR = self.ring[0][0:64, :].rearrange("p (r t) -> p r t", r=4)
        ckr = self.kview("kr", 0, 64)
        self.dma(KR, ckr[:, :, 0:TL].rearrange("r d t -> d r t"))
        KRC = self.sb("KRC", [64, 4, 64], BF16, self.o_scr + 20992 - 512)
        self.dma(KRC[:, :, :], ckr[:, :, TL:T].rearrange("r d t -> d r t"))
        for h in range(5):
            KN = self.ring[1 if h % 2 == 0 else 3][:, :].rearrange("p (r t) -> p r t", r=4)
            VV = self.ring[2][:, :].rearrange("p (r n c) -> p r n c", r=4, n=8)
            KC_ = self.KCX[h % 2]
            VC_ = self.VCX[h % 2]
            ck = self.kview("mla", h)
            cv = self.vview("mla", h)
            self.dma(KN, ck[:, :, 0:TL].rearrange("r d t -> d r t"))
            self.dma(KC_[:, 0, :, :], ck[:, :, TL:T].rearrange("r d t -> d r t"))
            for r_ in range(4):
                self.dma(VV[:, r_, :, :], cv[r_, 0:TL, :].rearrange("(n p) c -> p n c", p=128))
            self.dma(VC_[:, :, :], cv[:, TL:T, :].rearrange("r p c -> p r c"))
            lat = [(128, "l", r, n) for r in range(4) for n in range(8)]
            ctxc = [(64, "c", r, 0) for r in range(4)]

            def kn(ch):
                return KN[:, ch[2], ch[3] * 128:(ch[3] + 1) * 128] if ch[1] == "l" else KC_[:, 0, ch[2], :]

            def kr(ch):
                return KR[:, ch[2], ch[3] * 128:(ch[3] + 1) * 128] if ch[1] == "l" else KRC[:, ch[2], :]

            def vf(ch):
                return VV[:, ch[2], ch[3], :] if ch[1] == "l" else VC_[:, ch[2], :]

            for (t0, t1) in (QBS[:2] if l == 1 else QBS):
                nq = t1 - t0
                chunks = (lat + ctxc) if t0 < TL else ctxc
                maps = [[(kn, self.QMN[:, h, t0:t1]), (kr, self.QMR[0:64, h, t0:t1])]]
                self.attend(nq, chunks, maps, vf, MLA_SCALE, [self.psum[4]], [self.psum[5]])
                self.act(self.AO[0][:, 0:nq], self.psum[4][:, 0:nq], AF.Copy)
                self.act(self.RC[0][:, 0:nq], self.psum[5][:, 0:nq], AF.Copy)
                self.recip(self.RC[0][:, 0:nq], self.RC[0][:, 0:nq])
                self.tt(self.OT[:, 6 + h, t0:t1], self.AO[0][:, 0:nq], self.RC[0][:, 0:nq], ALU.mult)

    def diff_attn(self, l, lam_init):
        DL = self.sb("DL", [1, 512], F32, self.o_scr + 16384)
        LS = self.sb("LS", [1, 8], F32, self.o_scr + 16384 + 2048)
        self.dma(DL[:, :], self.dlam_d[:, :])
        b0 = l * 256
        self.tt(DL[0:1, b0:b0 + 64], DL[0:1, b0:b0 + 64], DL[0:1, b0 + 64:b0 + 128], ALU.mult)
        self.tt(DL[0:1, b0 + 128:b0 + 192], DL[0:1, b0 + 128:b0 + 192], DL[0:1, b0 + 192:b0 + 256], ALU.mult)
        self.P.add("dve", lambda e: e.reduce_sum(LS[0:1, 0:1], DL[0:1, b0:b0 + 64], mybir.AxisListType.X),
                   [DL[0:1, b0:b0 + 64]], [LS[0:1, 0:1]])
        self.P.add("dve", lambda e: e.reduce_sum(LS[0:1, 1:2], DL[0:1, b0 + 128:b0 + 192], mybir.AxisListType.X),
                   [DL[0:1, b0 + 128:b0 + 192]], [LS[0:1, 1:2]])
        self.act(LS[0:1, 2:4], LS[0:1, 0:2], AF.Exp)
        self.tt(LS[0:1, 4:5], LS[0:1, 3:4], LS[0:1, 2:3], ALU.subtract)
        self.ts(LS[0:1, 4:5], LS[0:1, 4:5], float(-lam_init), None, ALU.add)
        ps = self.psum[0]
        self.mm(ps[:, 0:1], self.ONESF[0:1, :], LS[0:1, 4:5], start=True, stop=True)
        self.vcopy(self.NEGLAM[:, 0:1], ps[:, 0:1])
        self.ts(self.SUBW[:, 0:1], self.VECS[:, V_SUB + l:V_SUB + l + 1], float(1.0 - lam_init), None, ALU.mult)
        for h in range(5):
            KD = self.ring[h % 2][:, :].rearrange("p (r t) -> p r t", r=4)
            VV = self.ring[2 + h % 2][:, :].rearrange("p (r n c) -> p r n c", r=4, n=8)
            KC_ = self.KCX[h % 2]
            VC_ = self.VCX[h % 2]
            ck = self.kview("diff", h)
            cv = self.vview("diff", h)
            self.dma(KD, ck[:, :, 0:TL].rearrange("r d t -> d r t"))
            self.dma(KC_[:, 0, :, :], ck[:, :, TL:T].rearrange("r d t -> d r t"))
            for r_ in range(4):
                self.dma(VV[:, r_, :, :], cv[r_, 0:TL, :].rearrange("(n p) c -> p n c", p=128))
            self.dma(VC_[:, :, :], cv[:, TL:T, :].rearrange("r p c -> p r c"))
            lat = [(128, "l", r, n) for r in range(4) for n in range(8)]
            ctxc = [(64, "c", r, 0) for r in range(4)]

            def k1(ch):
                return KD[0:64, ch[2], ch[3] * 128:(ch[3] + 1) * 128] if ch[1] == "l" else KC_[0:64, 0, ch[2], :]

            def k2(ch):
                return KD[64:128, ch[2], ch[3] * 128:(ch[3] + 1) * 128] if ch[1] == "l" else KC_[64:128, 0, ch[2], :]

            def vf(ch):
                return VV[:, ch[2], ch[3], :] if ch[1] == "l" else VC_[:, ch[2], :]

            for (t0, t1) in (QBS[:2] if l == 1 else QBS):
                nq = t1 - t0
                chunks = (lat + ctxc) if t0 < TL else ctxc
                maps = [[(k1, self.QD[0:64, h, t0:t1])], [(k2, self.QD[64:128, h, t0:t1])]]
                O1, L1, O2, L2 = self.psum[4], self.psum[5], self.psum[6], self.psum[7]
                self.attend(nq, chunks, maps, vf, DIFF_SCALE, [O1, O2], [L1, L2])
                self.act(self.AO[0][:, 0:nq], O1[:, 0:nq], AF.Copy)
                self.act(self.AO[1][:, 0:nq], O2[:, 0:nq], AF.Copy)
                self.act(self.RC[0][:, 0:nq], L1[:, 0:nq], AF.Copy)
                self.act(self.RC[1][:, 0:nq], L2[:, 0:nq], AF.Copy)
                self.recip(self.RC[0][:, 0:nq], self.RC[0][:, 0:nq])
                self.recip(self.RC[1][:, 0:nq], self.RC[1][:, 0:nq])
                self.tt(self.AO[0][:, 0:nq], self.AO[0][:, 0:nq], self.RC[0][:, 0:nq], ALU.mult)
                self.tt(self.AO[1][:, 0:nq], self.AO[1][:, 0:nq], self.RC[1][:, 0:nq], ALU.mult)
                self.stt(self.AO[0][:, 0:nq], self.AO[1][:, 0:nq], self.NEGLAM[:, 0:1], self.AO[0][:, 0:nq],
                         ALU.mult, ALU.add)
                sq = self.PT[0][:, 0:nq]
                self.act(sq, self.AO[0][:, 0:nq], AF.Square)
                pn = self.psum[0]
                self.mm(pn[:, 0:nq], self.ONESB[:, :], sq, start=True, stop=True)
                self.act(self.RC[0][:, 0:nq], pn[:, 0:nq], AF.Sqrt, bias=self.EPS_AP[:, 0:1], scale=1.0 / 128)
                self.recip(self.RC[0][:, 0:nq], self.RC[0][:, 0:nq])
                self.stt(self.OT[:, 11 + h, t0:t1], self.AO[0][:, 0:nq], self.SUBW[:, 0:1], self.RC[0][:, 0:nq],
                         ALU.mult, ALU.mult)

    def na_attn(self, l):
        base = self.o_ring
        KW = self.sb("KW", [128, 1536], BF16, base)
        KCN = self.sb("KCN", [128, 4, 64], BF16, base + 3072)
        VW = self.sb("VW", [128, 12, 128], BF16, base + 3584)
        VCN = self.sb("VCN", [64, 4, 128], BF16, base + 6656)
        CKP = self.sb("CKP", [128, 4, 256], BF16, base + 8192)
        CKN = self.sb("CKN", [128, 4, 256], BF16, base + 8192 + 2048)
        CVP = self.sb("CVP", [128, 4, 2, 128], BF16, base + 8192 + 4096)
        CVN = self.sb("CVN", [128, 4, 2, 128], BF16, base + 8192 + 6144)
        NB = [self.sb("NB0", [128, 40 * 64], BF16, base + 16384), self.sb("NB1", [128, 38 * 64], BF16, base + 24576)]
        assert NA_OFF[8] == 40 and NA_OFF[16] - NA_OFF[8] == 38
        for h in range(6):
            ci_, o_ = kvloc("na", h)
            ck = self.kview("na", h)
            cv = self.vview("na", h)
            self.dma(KW[:, 256:1280], self.cinK[ci_].ap()[o_:o_ + 128, 0:TL])
            self.dma(CKP[:, :, :], ck[:, :, 768:1024].rearrange("r d t -> d r t"))
            self.dma(CKN[:, :, :], ck[:, :, 0:256].rearrange("r d t -> d r t"))
            self.dma(KCN[:, :, :], ck[:, :, TL:T].rearrange("r d t -> d r t"))
            self.dma(VW[:, 2:10, :], self.cinV[ci_].ap()[0:TL, o_:o_ + 128].rearrange("(n p) c -> p n c", p=128))
            for r_ in range(4):
                self.dma(CVP[:, r_, :, :], cv[r_, 768:1024, :].rearrange("(n p) c -> p n c", p=128))
                self.dma(CVN[:, r_, :, :], cv[r_, 0:256, :].rearrange("(n p) c -> p n c", p=128))
            self.dma(VCN[:, :, :], cv[:, TL:T, :].rearrange("r p c -> p r c"))
            nbsrc = self.nab_d[0 if FAKE else l * 6 + h]
            self.dma(NB[0][:, :], nbsrc[:, 0:40 * 64], eng="pool")
            self.dma(NB[1][:, :], nbsrc[:, 40 * 64:78 * 64], eng="pool")
            for (dst, cand, so) in ((KW[:, 0:256], CKP, 0), (KW[:, 1280:1536], CKN, 4)):
                self.ts(dst, cand[:, 0, :], self.SEL[:, so:so + 1], None, ALU.mult)
                for r in range(1, 4):
                    self.stt(dst, cand[:, r, :], self.SEL[:, so + r:so + r + 1], dst, ALU.mult, ALU.add)
            for (dst, cand, so) in ((VW[:, 0:2, :], CVP, 0), (VW[:, 10:12, :], CVN, 4)):
                self.ts(dst, cand[:, 0, :, :], self.SEL[:, so:so + 1], None, ALU.mult)
                for r in range(1, 4):
                    self.stt(dst, cand[:, r, :, :], self.SEL[:, so + r:so + r + 1], dst, ALU.mult, ALU.add)
            TB2 = [self.TBI, self.TBI2]

            def na_scores(r):
                q = self.QNA[:, h, r * 64:(r + 1) * 64]
                clo, chi = NA_CH[r]
                nch = chi - clo + 1
                ps = self.psum[(2 * r) % 4]
                pc = self.psum[(2 * r + 1) % 4]
                for ci in range(nch):
                    c = clo + ci
                    self.mm(ps[:, ci * 64:(ci + 1) * 64], KW[:, c * 128:(c + 1) * 128], q, start=True, stop=True)
                for rk in range(4):
                    self.mm(pc[0:64, rk * 64:(rk + 1) * 64], KCN[:, rk, :], q, start=True, stop=True)
                return ps, pc

            def na_rest(r, ps, pc):
                r8, rr = r // 8, r % 8
                O, L = self.psum[4 + 2 * r8], self.psum[5 + 2 * r8]
                clo, chi = NA_CH[r]
                nch = chi - clo + 1
                boff = (NA_OFF[r] - NA_OFF[r8 * 8]) * 64
                tb = TB2[r % 2][:, 0:nch * 64]
                self.stt(tb, ps[:, 0:nch * 64], float(NA_SCALE), NB[r8][:, boff:boff + nch * 64], ALU.mult, ALU.add)
                pt = self.PT[r % 2]
                ptc = self.PT[2 + r % 2]
                self.act(pt[:, 0:nch * 64], tb, AF.Exp)
                self.act(ptc[0:64, 0:256], pc[0:64, 0:256], AF.Exp, scale=float(NA_SCALE))
                oc = slice(rr * 64, (rr + 1) * 64)
                for ci in range(nch):
                    c = clo + ci
                    self.mm(O[:, oc], VW[:, c, :], pt[:, ci * 64:(ci + 1) * 64], start=(ci == 0), stop=False)
                    self.mm(L[:, oc], self.ONESB[:, :], pt[:, ci * 64:(ci + 1) * 64], start=(ci == 0), stop=False)
                for rk in range(4):
                    self.mm(O[:, oc], VCN[:, rk, :], ptc[0:64, rk * 64:(rk + 1) * 64], start=False, stop=(rk == 3))
                    self.mm(L[:, oc], self.ONESB[0:64, :], ptc[0:64, rk * 64:(rk + 1) * 64], start=False, stop=(rk == 3))
                if rr == 7:
                    self.recip(self.RC[r8][:, :], L[:, :])
                    self.tt(self.OT[:, h, r8 * 512:(r8 + 1) * 512], O[:, :], self.RC[r8][:, :], ALU.mult)

            pend = na_scores(0)
            for r in range(16):
                cur = pend
                if r + 1 < 16:
                    pend = na_scores(r + 1)
                na_rest(r, *cur)
            O, L = self.psum[4], self.psum[5]
            if l == 1:
                continue
            q = self.QNA[:, h, TL:T]
            pc = self.psum[0]
            for rk in range(4):
                self.mm(pc[0:64, rk * 64:(rk + 1) * 64], KCN[:, rk, :], q, start=True, stop=True)
            ptc = self.PT[2]
            self.act(ptc[0:64, 0:256], pc[0:64, 0:256], AF.Exp, scale=float(NA_SCALE))
            for rk in range(4):
                self.mm(O[:, 0:64], VCN[:, rk, :], ptc[0:64, rk * 64:(rk + 1) * 64], start=(rk == 0), stop=(rk == 3))
                self.mm(L[:, 0:64], self.ONESB[0:64, :], ptc[0:64, rk * 64:(rk + 1) * 64], start=(rk == 0), stop=(rk == 3))
            self.recip(self.RC[0][:, 0:64], L[:, 0:64])
            self.tt(self.OT[:, h, TL:T], O[:, 0:64], self.RC[0][:, 0:64], ALU.mult)

    def merge(self, l):
        self.modulate(l, 1, 3, 4, self.H2)
        self.gate_vec(5, 1.0)
        for q4 in range(4):
            for dq in range(4):
                dj = q4 * 4 + dq
                wg = [self.wload(("g", l, dj, br)) for br in range(3)]
                wb = self.wload(("wb", l, dj))
                for (t0, t1) in self.cur_tbs:
                    n = t1 - t0
                    pg = [self.psum[i] for i in range(3)]
                    pb = [self.psum[3 + i] for i in range(3)]
                    for br, (ka, kb) in enumerate(((0, 6), (6, 11), (11, 16))):
                        self.proj_fm(wg[br], 0, 128, KC, self.H2, pg[br], t0, t1)
                        for k in range(ka, kb):
                            self.mm(pb[br][:, 0:n], wb[:, k, :], self.OT[:, k, t0:t1], start=(k == ka), stop=(k == kb - 1))
                        self.act(self.SG[br][:, 0:n], pg[br][:, 0:n], AF.Sigmoid)
                        if br == 0:
                            self.tt(self.MT[0][:, 0:n], self.SG[0][:, 0:n], pb[0][:, 0:n], ALU.mult)
                        elif br == 1:
                            self.tt(self.MT[1][:, 0:n], self.SG[1][:, 0:n], pb[1][:, 0:n], ALU.mult)
                            self.tt(self.MT[0][:, 0:n], self.MT[0][:, 0:n], self.MT[1][:, 0:n], ALU.add)
                        else:
                            self.tt(self.MT[1][:, 0:n], self.SG[2][:, 0:n], pb[2][:, 0:n], ALU.mult)
                            self.tt(self.YQ[:, dq, t0:t1], self.MT[0][:, 0:n], self.MT[1][:, 0:n], ALU.add)
            for t in range(8):
                wt = self.wload(("wo", l, q4 // 2, t))
                ko = (q4 % 2) * 4
                for dd in range(2):
                    dj2 = t * 2 + dd
                    for (t0, t1) in self.cur_tbs:
                        n = t1 - t0
                        ps = self.psum[6 + (dd % 2)]
                        for k in range(4):
                            self.mm(ps[:, 0:n], wt[:, ko + k, dd * 128:(dd + 1) * 128], self.YQ[:, k, t0:t1],
                                    start=(k == 0), stop=(k == 3))
                        self.resid_add(dj2, t0, t1, ps)

    def mixer(self, l):
        lam_init = 0.8 - 0.6 * math.exp(-0.3 * l)
        self.modulate(l, 1, 3, 4, self.H)
        self.k_phase(l)
        if DEBUG_STOP.startswith("raw_kphase"):
            self.tick(flush=True)
        self.chk("kphase")
        self.cur_tbs = TBS_LAT if l == 1 else TBS
        self.q_phase(l)
        self.tick(flush=True)
        self.chk("qphase")
        self.na_attn(l)
        self.chk("na")
        self.mla_attn(l)
        self.chk("mla")
        self.diff_attn(l, lam_init)
        self.chk("diff")
        self.merge(l)

    def chk(self, name):
        if DEBUG_STOP in ("raw_" + name, "raw_" + name + "_nocc"):
            raise StopBuild()

    def dump(self, src_f32_ap):
        self.dma(self.dbg, src_f32_ap)

    def finish(self, out_ops):
        op = self.P.add("sp", None, [], [], kind="c")
        op.deps = list(out_ops) + [o for o in self.P.ops if o.kind == "cc"]

    def final(self):
        self.rms_stats()
        g = V_FN
        IDF = self.sb("IDF", [128, 128], F32, self.o_ring)
        OST = [self.sb(f"OST{i}", [128, D], F32, self.o_ring + 8192 * (1 + i)) for i in range(2)]
        YF = [self.sb(f"YF{i}", [128, 128], F32, self.o_ring + 512 + 512 * i) for i in range(4)]
        self.dma(IDF[:, :], self.ident_d[:, :])
        outs = []
        for tc in range(8):
            ost = OST[tc % 2]
            for j4 in range(4):
                ps = self.bank()
                for jj in range(4):
                    j = j4 * 4 + jj
                    yf = YF[j % 4]
                    self.stt(yf[:, :], self.X[:, j, tc * 128:(tc + 1) * 128], self.VECS[:, g + j:g + j + 1],
                             self.RS[:, tc * 128:(tc + 1) * 128], ALU.mult, ALU.mult)
                    self.P.add("pe", lambda e, o=ps[:, jj * 128:(jj + 1) * 128], i=yf[:, :]: e.transpose(o, i, IDF[:, :]),
                               [yf[:, :], IDF[:, :]], [ps[:, jj * 128:(jj + 1) * 128]])
                self.act(ost[:, j4 * 512:(j4 + 1) * 512], ps[:, :], AF.Copy)
            outs.append(self.dma(self.out[tc * 128:(tc + 1) * 128, :], ost[:, :]))
        return outs

    def build(self):
        nc = self.nc
        self.memset(self.EPS_AP[:, :], EPS)
        self.setup()
        outs = []
        stop = DEBUG_STOP
        done = False
        for l in range(2):
            self.load_mods(l)
            if stop == "mods" and l == 0:
                outs.append(self.dump_small(self.MODS[:, :, :].rearrange("p a b -> p (a b)"), 288))
                done = True
                break
            self.ffn(l, 0, 0, 0)
            if stop == "ffn1" and l == 0:
                outs.append(self.dma(self.dbg, self.X[:, :, :].rearrange("p j t -> p (j t)")))
                done = True
                break
            try:
                self.mixer(l)
            except StopBuild:
                src = self.OT if stop in ("raw_na", "raw_mla", "raw_diff") else self.H
                for j in range(KC):
                    self.vcopy(self.X[:, j, :], src[:, j, :])
                outs.append(self.dma(self.dbg, self.X[:, :, :].rearrange("p j t -> p (j t)")))
                done = True
                break
            if stop == "mixer" and l == 0:
                outs.append(self.dma(self.dbg, self.X[:, :, :].rearrange("p j t -> p (j t)")))
                done = True
                break
            self.ffn(l, 1, 2, 6, TBS_LAT if l == 1 else TBS)
            if stop == "layer0" and l == 0:
                outs.append(self.dma(self.dbg, self.X[:, :, :].rearrange("p j t -> p (j t)")))
                done = True
                break
        if not done:
            outs += self.final()
        self.finish(outs)
        with ExitStack() as stack:
            self.P.finalize(stack)
        return nc

    def dump_small(self, ap, n):
        return self.dma(self.dbg[:, 0:n], ap)


_CACHE = {}


def prep_inputs(inputs):
    inp = {k: np.asarray(v) for k, v in inputs.items()}
    wbig, dirn = build_wbig(inp)
    vecs = build_vecs(inp)
    dlam = np.ascontiguousarray(inp["diff_lambda"].reshape(1, 512).astype(np.float32))
    ident = np.eye(128, dtype=np.float32)
    maps = []
    for core in range(8):
        b, rq = core // 4, core % 4
        xT = np.concatenate([inp["x"][b, rq * TL:(rq + 1) * TL, :].T, inp["ctx"][b, rq * TCX:(rq + 1) * TCX, :].T], axis=1)
        cv = np.stack([inp["c"][b], inp["c_ctx"]], axis=-1)
        cT = cv.reshape(KC, 128, 2).transpose(1, 0, 2).reshape(128, KC * 2)
        adaw = np.empty((72, 128, 2048), np.float32)
        adab = np.empty((1, 72 * 128), np.float32)
        for l in range(2):
            for jj in range(36):
                c0 = (36 * rq + jj) * 128
                W = inp["w_ada"][l][:, c0:c0 + 128]
                adaw[l * 36 + jj] = W.reshape(KC, 128, 128).transpose(1, 0, 2).reshape(128, 2048)
                adab[0, (l * 36 + jj) * 128:(l * 36 + jj + 1) * 128] = inp["b_ada"][l, c0:c0 + 128]
        sel = np.zeros((128, 16), np.float32)
        sel[:, 8 + b] = 1.0
        if rq > 0:
            sel[:, rq - 1] = 1.0
        if rq < 3:
            sel[:, 4 + rq + 1] = 1.0
        nab = na_bias(inp["na_rpb"], rq).reshape(12, 128, NA_OFF[-1] * 64)
        maps.append({
            "xT": np.ascontiguousarray(xT.astype(np.float32)),
            "cT": np.ascontiguousarray(cT.astype(np.float32)),
            "adaw": adaw, "adab": adab, "wbig": wbig, "vecs": vecs, "dlam": dlam,
            "rope": rope_table(rq).reshape(128, 2 * T), "nab": np.ascontiguousarray(nab),
            "sel": sel, "ident": ident,
        })
    return maps, dirn, wbig.shape[1]


def kernel(**inputs):
    maps, dirn, wtot = prep_inputs(inputs)
    b = Builder(dirn, wtot)
    nc = b.build()
    res = run_bass_kernel_spmd(nc, maps, core_ids=list(range(8)))
    kernel.last = res
    out = np.empty((2, 4096, D), np.float32)
    for core in range(8):
        bb, rq = core // 4, core % 4
        out[bb, rq * TL:(rq + 1) * TL, :] = res.results[core]["out"]
    return out
```
